# Optimizing a Trainium2 kernel written in Bass

```python
import jax, jax.numpy as jnp
from jax import lax
import numpy as np

D_MODEL = 1024
BATCH = 8
SEQ = 2048
DEPTH = 2
DEC_BATCH = 128
DEC_SEQ = 1
PAST_LEN = 16384
PAGE_SIZE = 128

GLA_HEADS = 4
GLA_DK = D_MODEL // 2 // GLA_HEADS
GLA_DV = D_MODEL // GLA_HEADS
GLA_KEY_WIDTH = GLA_HEADS * GLA_DK
GLA_VAL_WIDTH = GLA_HEADS * GLA_DV
GATE_RANK = 16
GATE_TAU = 16.0
CHUNK = 64
POOL_WINDOWS = (2, 4, 8, 16)
POOL_GROUPS = 4
POOL_WIDTH = D_MODEL
POOL_GROUP_DIM = POOL_WIDTH // POOL_GROUPS
POOL_HIST = max(POOL_WINDOWS) - 1
N_MEM = 256
XA_HEADS = 4
XA_HEAD_DIM = D_MODEL // XA_HEADS
XA_WIDTH = XA_HEADS * XA_HEAD_DIM
N_BRANCH = 3
BRANCH_WIDTH = D_MODEL
EPS = 1e-6
SPLITS = (GLA_KEY_WIDTH, GLA_KEY_WIDTH, GLA_VAL_WIDTH, GLA_VAL_WIDTH, GATE_RANK,
          POOL_WIDTH, POOL_WIDTH, XA_WIDTH, XA_WIDTH, N_BRANCH * D_MODEL)
N_IN = sum(SPLITS)

kernel_name = 'gla_pool_xattn_hybrid_step'


def rmsnorm(x, g):
    xf = x.astype(jnp.float32)
    y = xf * lax.rsqrt(jnp.mean(xf * xf, axis=-1, keepdims=True) + EPS)
    return (y * g.astype(jnp.float32)).astype(x.dtype)


def gla_chunked(q, k, v, log_a, S0):
    B, T, H, _ = q.shape
    C = CHUNK if T % CHUNK == 0 else T
    N = T // C

    def blk(z):
        return z.astype(jnp.float32).reshape(B, N, C, H, -1).transpose(0, 3, 1, 2, 4)

    qb, kb, vb, ab = blk(q), blk(k), blk(v), blk(log_a)
    b = jnp.cumsum(ab, axis=3)
    b_last = b[:, :, :, -1:]
    q_dec = qb * jnp.exp(b)
    k_inv = kb * jnp.exp(-b)
    k_end = kb * jnp.exp(b_last - b)
    mask = np.tril(np.ones((C, C), dtype=bool))
    att = jnp.where(mask, jnp.einsum('bhncd,bhnsd->bhncs', q_dec, k_inv), 0.0)
    o_intra = jnp.einsum('bhncs,bhnse->bhnce', att, vb)
    dS = jnp.einsum('bhncd,bhnce->bhnde', k_end, vb)
    decay = jnp.exp(b_last[:, :, :, 0])

    def step(S, inp):
        dS_n, dec_n = inp
        return dec_n[..., None] * S + dS_n, S

    S_final, S_before = lax.scan(step, S0.astype(jnp.float32),
                                 (dS.transpose(2, 0, 1, 3, 4), decay.transpose(2, 0, 1, 3)))
    S_before = S_before.transpose(1, 2, 0, 3, 4)
    o_inter = jnp.einsum('bhncd,bhnde->bhnce', q_dec, S_before)
    o = (o_inter + o_intra).transpose(0, 2, 3, 1, 4).reshape(B, T, H, -1)
    return o, S_final


def multiscale_pool(u, hist, pos0, pool_w, pool_scale):
    B, T, W = u.shape
    n_hist = hist.shape[1]
    ext = jnp.concatenate([hist.astype(u.dtype), u], axis=1).astype(jnp.float32)
    cs = jnp.concatenate([jnp.zeros((B, 1, W), jnp.float32), jnp.cumsum(ext, axis=1)], axis=1)
    ends = np.arange(n_hist + 1, n_hist + T + 1)
    pos = pos0 + np.arange(T)
    groups = []
    for g, w in enumerate(POOL_WINDOWS):
        sl = slice(g * POOL_GROUP_DIM, (g + 1) * POOL_GROUP_DIM)
        starts = np.maximum(ends - w, 0)
        count = np.minimum(w, pos + 1).astype(np.float32)
        s = cs[:, ends, sl] - cs[:, starts, sl]
        groups.append(s / count[None, :, None])
    mean = jnp.stack(groups, axis=2)
    diff = mean - u.astype(jnp.float32).reshape(B, T, POOL_GROUPS, POOL_GROUP_DIM)
    mixed = jnp.einsum('btgc,gce->btge', diff, pool_w.astype(jnp.float32)).reshape(B, T, W)
    mixed = mixed * pool_scale.astype(jnp.float32)
    new_hist = ext[:, -POOL_HIST:].astype(u.dtype)
    return mixed.astype(u.dtype), new_hist


def cross_attn(q, mk, mv):
    s = jnp.einsum('bthd,bmhd->bhtm', q.astype(jnp.float32), mk.astype(jnp.float32)) * (XA_HEAD_DIM ** -0.5)
    p = jax.nn.softmax(s, axis=-1)
    o = jnp.einsum('bhtm,bmhd->bthd', p, mv.astype(jnp.float32))
    return o.astype(q.dtype)


def mixer_layer(x, mem_k, mem_v, S0, pool_hist, pos0,
                w_in, w_a2, b_a, gla_gain, pool_w, pool_scale, w_branch, w_out, norm_gain):
    B, T, _ = x.shape
    h = rmsnorm(x, norm_gain)
    proj = h @ w_in
    split_pts = [int(p) for p in np.cumsum(SPLITS)[:-1]]
    q, k, v, gla_g, a_lr, u, pool_g, xq, xg, merge = jnp.split(proj, split_pts, axis=-1)
    q = q.reshape(B, T, GLA_HEADS, GLA_DK) * (GLA_DK ** -0.5)
    k = k.reshape(B, T, GLA_HEADS, GLA_DK)
    v = v.reshape(B, T, GLA_HEADS, GLA_DV)
    z = (a_lr @ w_a2 + b_a).astype(jnp.float32)
    log_a = (jax.nn.log_sigmoid(z) / GATE_TAU).reshape(B, T, GLA_HEADS, GLA_DK)
    o, S = gla_chunked(q, k, v, log_a, S0)
    o = rmsnorm(o, gla_gain.reshape(GLA_HEADS, GLA_DV)).reshape(B, T, GLA_VAL_WIDTH).astype(x.dtype)
    br_gla = o * jax.nn.silu(gla_g)
    p, new_hist = multiscale_pool(u, pool_hist, pos0, pool_w, pool_scale)
    br_pool = p * jax.nn.silu(pool_g)
    c = cross_attn(xq.reshape(B, T, XA_HEADS, XA_HEAD_DIM), mem_k, mem_v).reshape(B, T, XA_WIDTH)
    br_x = c * jax.nn.silu(xg)
    branches = jnp.stack([br_gla, br_pool, br_x], axis=2)
    bproj = jnp.einsum('btnw,nwd->btnd', branches, w_branch)
    gates = jax.nn.sigmoid(merge.astype(jnp.float32)).reshape(B, T, N_BRANCH, D_MODEL)
    merged = jnp.sum(gates * bproj.astype(jnp.float32), axis=2).astype(x.dtype)
    y = merged @ w_out
    return x + y, S.astype(x.dtype), new_hist


def setup_inputs(seed: int = 0) -> dict:
    key = jax.random.key(seed)
    ks = jax.random.split(key, 24)
    f32 = jnp.float32
    nrm = lambda kk, shape, scale: jax.random.normal(kk, shape, f32) * scale
    return {
        'x_prompt': nrm(ks[0], (BATCH, SEQ, D_MODEL), 1.0),
        'x_sample': nrm(ks[1], (DEC_BATCH, DEC_SEQ, D_MODEL), 1.0),
        'mem_prompt': nrm(ks[2], (BATCH, N_MEM, D_MODEL), 1.0),
        'cache_mem_k': nrm(ks[3], (DEPTH, DEC_BATCH, N_MEM, XA_HEADS, XA_HEAD_DIM), 1.0),
        'cache_mem_v': nrm(ks[4], (DEPTH, DEC_BATCH, N_MEM, XA_HEADS, XA_HEAD_DIM), 1.0),
        'state_gla': nrm(ks[5], (DEPTH, DEC_BATCH, GLA_HEADS, GLA_DK, GLA_DV), 1.0),
        'state_pool': nrm(ks[6], (DEPTH, DEC_BATCH, POOL_HIST, POOL_WIDTH), 1.0),
        'w_in': nrm(ks[7], (DEPTH, D_MODEL, N_IN), D_MODEL ** -0.5),
        'w_a2': nrm(ks[8], (DEPTH, GATE_RANK, GLA_KEY_WIDTH), GATE_RANK ** -0.5),
        'b_a': nrm(ks[9], (DEPTH, GLA_KEY_WIDTH), 0.1),
        'gla_gain': 1.0 + nrm(ks[10], (DEPTH, GLA_VAL_WIDTH), 0.1),
        'pool_w': nrm(ks[11], (DEPTH, POOL_GROUPS, POOL_GROUP_DIM, POOL_GROUP_DIM), POOL_GROUP_DIM ** -0.5),
        'pool_scale': 1.0 + nrm(ks[12], (DEPTH, POOL_WIDTH), 0.1),
        'w_mk': nrm(ks[13], (DEPTH, D_MODEL, XA_WIDTH), D_MODEL ** -0.5),
        'w_mv': nrm(ks[14], (DEPTH, D_MODEL, XA_WIDTH), D_MODEL ** -0.5),
        'w_branch': nrm(ks[15], (DEPTH, N_BRANCH, BRANCH_WIDTH, D_MODEL), BRANCH_WIDTH ** -0.5),
        'w_out': nrm(ks[16], (DEPTH, D_MODEL, D_MODEL), D_MODEL ** -0.5),
        'norm_gain': 1.0 + nrm(ks[17], (DEPTH, D_MODEL), 0.1),
        'final_gain': 1.0 + nrm(ks[18], (D_MODEL,), 0.1),
    }


def reference(x_prompt, x_sample, mem_prompt, cache_mem_k, cache_mem_v, state_gla, state_pool,
              w_in, w_a2, b_a, gla_gain, pool_w, pool_scale, w_mk, w_mv, w_branch, w_out,
              norm_gain, final_gain):
    xp, xs = x_prompt, x_sample
    Bp, Mp = mem_prompt.shape[0], mem_prompt.shape[1]
    mk_list, mv_list, glap_list, poolp_list, glas_list, pools_list = [], [], [], [], [], []
    for l in range(DEPTH):
        params = (w_in[l], w_a2[l], b_a[l], gla_gain[l], pool_w[l], pool_scale[l], w_branch[l], w_out[l], norm_gain[l])
        mk = (mem_prompt @ w_mk[l]).reshape(Bp, Mp, XA_HEADS, XA_HEAD_DIM)
        mv = (mem_prompt @ w_mv[l]).reshape(Bp, Mp, XA_HEADS, XA_HEAD_DIM)
        S0p = jnp.zeros((xp.shape[0], GLA_HEADS, GLA_DK, GLA_DV), jnp.float32)
        hist0 = jnp.zeros((xp.shape[0], 0, POOL_WIDTH), xp.dtype)
        xp, Sp, hp = mixer_layer(xp, mk, mv, S0p, hist0, 0, *params)
        xs, Ss, hs = mixer_layer(xs, cache_mem_k[l], cache_mem_v[l], state_gla[l].astype(jnp.float32),
                                 state_pool[l], PAST_LEN, *params)
        mk_list.append(mk)
        mv_list.append(mv)
        glap_list.append(Sp)
        poolp_list.append(hp)
        glas_list.append(Ss)
        pools_list.append(hs)
    y_prompt = rmsnorm(xp, final_gain)
    y_sample = rmsnorm(xs, final_gain)
    return (y_prompt, y_sample, jnp.stack(mk_list), jnp.stack(mv_list), jnp.stack(glap_list),
            jnp.stack(poolp_list), jnp.stack(glas_list), jnp.stack(pools_list))
```

```python
import numpy as np
from contextlib import ExitStack
import concourse.bass as bass
import concourse.mybir as mybir
from concourse.bass_utils import run_bass_kernel_spmd

F32 = mybir.dt.float32
BF16 = mybir.dt.bfloat16
AF = mybir.ActivationFunctionType
ALU = mybir.AluOpType
AX = mybir.AxisListType

D = 1024
NIN = 10256
TB = 512
NBLK = 4
NSMP = 16
NT = TB + NSMP
Q0, K0, V0, GG0, A0, U0, PG0, XQ0, XG0, MG0 = 0, 512, 1024, 2048, 3072, 3088, 4112, 5136, 6160, 7184
EPS = 1e-6
NSLOT = 4
PREF = 3


class Buf:
    __slots__ = ("name", "w", "r", "excl")

    def __init__(self, name, excl=False):
        self.name = name
        self.w = None
        self.r = {}
        self.excl = excl


class Eng:
    def __init__(self, T, e, name):
        self.T, self.e, self.name = T, e, name
        self.sem = T.newsem("e_" + name)
        self.n = 0
        self.seen = {}

    def wait(self, ev):
        sem, val = ev
        k = id(sem)
        if self.seen.get(k, 0) >= val:
            return
        self.seen[k] = val
        if not self.T.dry:
            self.e.wait_ge(sem, val)


class Tracker:
    def __init__(self, nc, es, dry):
        self.nc, self.es, self.dry = nc, es, dry
        self.pe = Eng(self, nc.tensor, "pe")
        self.act = Eng(self, nc.scalar, "act")
        self.dve = Eng(self, nc.vector, "dve")
        self.pool = Eng(self, nc.gpsimd, "pool")
        self.sp = Eng(self, nc.sync, "sp")
        self.dsems = {}
        self.psl = []
        self.psi = 0
        self.ps_range = (0, 8)
        self.stopped = False
        self.fence = {}
        self.store_sems = {}

    def newsem(self, name):
        if self.dry:
            return object()
        return self.es.enter_context(self.nc.semaphore(name))

    def sb(self, name, shape, dt, stack=None):
        self.uid = getattr(self, "uid", 0) + 1
        nm = "%s_%d" % (name, self.uid)
        t = (stack or self.es).enter_context(self.nc.sbuf_tensor(nm, shape, dt))
        b = Buf(nm)
        if stack is not None:
            b.r = dict(self.fence)
        return t, b

    def init_psum(self):
        for i in range(8):
            t = self.es.enter_context(self.nc.psum_tensor("ps%d" % i, [128, 512], F32))
            self.psl.append((t, Buf("ps%d" % i, excl=True)))

    def ps(self):
        lo, hi = self.ps_range
        r = self.psl[lo + self.psi % (hi - lo)]
        self.psi += 1
        return r

    def _deps(self, eng, reads, writes):
        for b in reads:
            if b.w is not None:
                eng.wait(b.w)
            if b.excl:
                me = id(eng.sem)
                for k, ev in b.r.items():
                    if k != me:
                        eng.wait(ev)
        for b in writes:
            if b.w is not None:
                eng.wait(b.w)
            for ev in b.r.values():
                eng.wait(ev)

    def _mark(self, ev, reads, writes):
        k = id(ev[0])
        for b in reads:
            b.r[k] = ev
        for b in writes:
            b.w = ev
            b.r = {}

    def op(self, eng, fn, reads=(), writes=()):
        if self.stopped:
            return
        self._deps(eng, reads, writes)
        eng.n += 1
        if not self.dry:
            fn(eng.e).then_inc(eng.sem, 1)
        self._mark((eng.sem, eng.n), reads, writes)

    def mm(self, fns, reads=(), writes=()):
        if self.stopped:
            return
        eng = self.pe
        self._deps(eng, reads, writes)
        eng.n += 1
        if not self.dry:
            ins = None
            for fn in fns:
                ins = fn(eng.e)
            ins.then_inc(eng.sem, 1)
        self._mark((eng.sem, eng.n), reads, writes)

    def dma(self, eng, out, in_, key, reads=(), writes=()):
        if self.stopped:
            return
        self._deps(eng, reads, writes)
        if id(key) not in self.dsems:
            self.dsems[id(key)] = [self.newsem("d_" + key.name), 0]
        st = self.dsems[id(key)]
        st[1] += 16
        if len(reads) > 0:
            self.store_sems[id(st[0])] = st
        if not self.dry:
            eng.e.dma_start(out=out, in_=in_).then_inc(st[0], 16)
        self._mark((st[0], st[1]), reads, writes)

    def barrier(self, with_pool=False):
        if self.stopped:
            return
        engs = [self.pe, self.act, self.dve, self.sp]
        allv = engs + [self.pool]
        for e in (allv if with_pool else engs):
            for o in allv:
                if o.n > 0:
                    e.wait((o.sem, o.n))
            for st in self.dsems.values():
                e.wait((st[0], st[1]))

    def mark_fence(self):
        if self.stopped:
            return
        f = {}
        for o in [self.pe, self.act, self.dve, self.sp, self.pool]:
            if o.n > 0:
                f[id(o.sem)] = (o.sem, o.n)
        for st in self.store_sems.values():
            f[id(st[0])] = (st[0], st[1])
        self.fence = f

    def finish(self):
        self.barrier()


class Rot:
    def __init__(self, T, name, n, shape, dt, stack=None):
        self.items = [T.sb("%s%d" % (name, i), shape, dt, stack) for i in range(n)]
        self.i = 0

    def next(self):
        r = self.items[self.i % len(self.items)]
        self.i += 1
        return r


class _Stop(Exception):
    pass


STOP = None


def program(nc, wseq):
    dry = wseq is None
    rec = []
    es = ExitStack()
    with es:
      T = None
      try:
        _program(nc, wseq, dry, rec, es)
      except _Stop:
        pass
    return rec


def _program(nc, wseq, dry, rec, es):
    if True:
        es.enter_context(nc.allow_non_contiguous_dma(reason="small strided parameter loads"))
        es.enter_context(nc.allow_low_precision(reason="bf16 matmul operands, fp32 accumulate"))
        T = Tracker(nc, es, dry)
        pe, act, dve, pool, sp = T.pe, T.act, T.dve, T.pool, T.sp

        def chk(name):
            if STOP == name and not T.stopped:
                T.finish()
                T.stopped = True

        def din(name, shape):
            return nc.dram_tensor(name, shape, F32, kind="ExternalInput").ap()

        def dout(name, shape):
            return nc.dram_tensor(name, shape, F32, kind="ExternalOutput").ap()

        xp = din("xp", [2048, D]); xs = din("xs", [NSMP, D]); mem = din("mem", [256, D])
        ck = din("ck", [2, NSMP, 256, D]); cv = din("cv", [2, NSMP, 256, D])
        sg = din("sg", [2, NSMP, 4, 128, 256]); spool = din("spool", [2, NSMP, 15, D])
        w_in = din("w_in", [2, D, NIN]); w_a2 = din("w_a2", [2, 16, 512]); b_a = din("b_a", [2, 512])
        gla_gain = din("gla_gain", [2, D]); pool_w = din("pool_w", [2, 4, 256, 256])
        pool_scale = din("pool_scale", [2, D]); w_mk = din("w_mk", [2, D, D]); w_mv = din("w_mv", [2, D, D])
        w_branch = din("w_branch", [2, 3, D, D]); w_out = din("w_out", [2, D, D])
        norm_gain = din("norm_gain", [2, D]); final_gain = din("final_gain", [D])
        yp = dout("yp", [2048, D]); ys = dout("ys", [NSMP, D]); omk = dout("omk", [2, 256, D]); omv = dout("omv", [2, 256, D])
        oglap = dout("oglap", [2, 4, 128, 256]); opoolp = dout("opoolp", [2, 15, D])
        oglas = dout("oglas", [2, NSMP, 4, 128, 256]); opools = dout("opools", [2, NSMP, 15, D])

        T.init_psum()

        xP, _ = T.sb("xP", [128, 4, D], F32)
        xb = [Buf("xb%d" % i) for i in range(5)]
        xS, _ = T.sb("xS", [NSMP, D], F32)
        hT, _ = T.sb("hT", [128, 8, NT], BF16); hb = [Buf("hb%d" % i) for i in range(5)]
        brT, _ = T.sb("brT", [128, 8, NT], BF16); brb = [Buf("brb%d" % i) for i in range(5)]
        mgT, _ = T.sb("mgT", [128, 8, NT], BF16); mgb = [Buf("mgb%d" % i) for i in range(5)]
        gblk, b_gblk = T.sb("gblk", [128, 4, NT], BF16)
        slots = [T.sb("wslot%d" % i, [128, 8, 512], BF16) for i in range(NSLOT)]
        S, b_S = T.sb("S", [128, 2, D], F32)
        Sbf, b_Sbf = T.sb("Sbf", [128, D], BF16)
        mkT, b_mkT = T.sb("mkT", [128, 2, 8, 256], BF16)
        mvb, b_mvb = T.sb("mvb", [128, 2, 2, D], BF16)
        memT, b_memT = T.sb("memT", [128, 8, 256], BF16)
        ones_f, b_c = T.sb("ones_f", [128, 128], F32)
        triU_f, _ = T.sb("triU_f", [128, 128], F32)
        triLs_f, _ = T.sb("triLs_f", [128, 128], F32)
        ident_f, _ = T.sb("ident_f", [128, 128], F32)
        ident_b, _ = T.sb("ident_b", [128, 128], BF16)
        ones_b, _ = T.sb("ones_b", [128, 128], BF16)
        triU4_b, _ = T.sb("triU4_b", [128, 512], BF16)
        rc, _ = T.sb("rc", [128, 4, 16], F32)
        a_aug, b_aaug = T.sb("a_aug", [17, NT], F32)
        w_a2aug, b_wa2 = T.sb("w_a2aug", [17, 2, 512], F32)
        wa, b_wa = T.sb("wa", [128, 2, 8, 16], BF16)
        pw, b_pw = T.sb("pw", [128, 2, 4, 2, 256], BF16)
        gainT, b_gainT = T.sb("gainT", [128, 2, 8], F32)
        pscT, b_pscT = T.sb("pscT", [128, 2, 8], F32)
        uhist, b_uhist = T.sb("uhist", [128, 2, 8, 15], F32)
        junk, b_junk = T.sb("junk", [128, D], BF16)
        ssr = Rot(T, "ss", 6, [128, 16], F32)
        b_dd = Buf("dram2dram")

        class WS:
            pos = 0
            issued = 0

        def wsrc(key):
            kind = key[0]
            if kind == "in":
                src = w_in[key[1], :, key[2]:key[2] + 512]
            elif kind == "br":
                src = w_branch[key[1], key[2], :, key[3] * 512:(key[3] + 1) * 512]
            elif kind == "out":
                src = w_out[key[1], :, key[2] * 512:(key[2] + 1) * 512]
            elif kind == "mk":
                src = w_mk[key[1], :, key[2] * 512:(key[2] + 1) * 512]
            else:
                src = w_mv[key[1], :, key[2] * 512:(key[2] + 1) * 512]
            return src.rearrange("(k p) n -> p k n", p=128)

        def wget(key):
            if T.stopped:
                return slots[0]
            i = WS.pos
            WS.pos += 1
            if dry:
                rec.append(key)
                return slots[0]
            assert wseq[i] == key, (i, wseq[i], key)
            while WS.issued < len(wseq) and WS.issued <= i + PREF:
                j = WS.issued
                st, sbuf_ = slots[j % NSLOT]
                T.dma(pool, st[:], wsrc(wseq[j]), sbuf_, writes=[sbuf_])
                WS.issued += 1
            return slots[i % NSLOT]

        T.op(dve, lambda e: e.memset(ones_f[:], 1.0), writes=[b_c])
        T.op(dve, lambda e: e.memset(ones_b[:], 1.0), writes=[b_c])
        T.op(pool, lambda e: e.affine_select(out=triU_f[:], in_=ones_f[:], pattern=[[1, 128]], compare_op=ALU.is_ge,
                                             fill=0.0, base=0, channel_multiplier=-1), reads=[b_c], writes=[b_c])
        T.op(pool, lambda e: e.affine_select(out=triLs_f[:], in_=ones_f[:], pattern=[[-1, 128]], compare_op=ALU.is_gt,
                                             fill=0.0, base=0, channel_multiplier=1), reads=[b_c], writes=[b_c])
        T.op(pool, lambda e: e.affine_select(out=ident_f[:], in_=ones_f[:], pattern=[[1, 128]], compare_op=ALU.is_equal,
                                             fill=0.0, base=0, channel_multiplier=-1), reads=[b_c], writes=[b_c])
        T.op(pool, lambda e: e.affine_select(out=ident_b[:], in_=ones_f[:], pattern=[[1, 128]], compare_op=ALU.is_equal,
                                             fill=0.0, base=0, channel_multiplier=-1), reads=[b_c], writes=[b_c])
        for h in range(4):
            T.op(pool, lambda e, h=h: e.affine_select(out=triU4_b[:, h * 128:(h + 1) * 128], in_=ones_f[:], pattern=[[1, 128]],
                                                      compare_op=ALU.is_ge, fill=0.0, base=0, channel_multiplier=-1),
                 reads=[b_c], writes=[b_c])
        for g in range(4):
            w = 2 << g
            T.op(dve, lambda e, g=g, w=w: e.memset(rc[:, g, :], 1.0 / w), writes=[b_c])
            for t in range(w - 1):
                T.op(dve, lambda e, g=g, t=t: e.memset(rc[:, g, t:t + 1], 1.0 / (t + 1)), writes=[b_c])
        T.op(dve, lambda e: e.memset(a_aug[:], 1.0), writes=[b_aaug])
        T.op(dve, lambda e: e.memset(S[:], 0.0), writes=[b_S])
        T.op(dve, lambda e: e.memset(uhist[:], 0.0), writes=[b_uhist])
        for l in range(2):
            T.dma(sp, w_a2aug[0:16, l, :], w_a2[l], b_wa2, writes=[b_wa2])
            T.dma(sp, w_a2aug[16:17, l, :], b_a[l:l + 1, :], b_wa2, writes=[b_wa2])
            T.dma(pool, wa[:, l], w_in[l, :, A0:A0 + 16].rearrange("(k p) n -> p k n", p=128), b_wa, writes=[b_wa])
            for g in range(4):
                T.dma(pool, pw[:, l, g], pool_w[l, g].rearrange("(k p) e -> p k e", p=128), b_pw, writes=[b_pw])
            T.dma(sp, gainT[:, l, :], norm_gain[l].rearrange("(k p) -> p k", p=128), b_gainT, writes=[b_gainT])
            T.dma(sp, pscT[:, l, :], pool_scale[l].rearrange("(k p) -> p k", p=128), b_pscT, writes=[b_pscT])

        def tb(bufs, t0, tn):
            if t0 >= TB:
                return [bufs[4]]
            return bufs[t0 // 128:(t0 + tn + 127) // 128]

        def transpose8(src, b_src, p, dst3, b_dst, evac=None):
            pst, pb = T.ps()
            psb = pst[:].bitcast(BF16)
            T.mm([lambda e, j=j: e.transpose(psb[:, j * 128:j * 128 + p], src[0:p, j * 128:(j + 1) * 128], ident_b[0:p, 0:p])
                  for j in range(8)], reads=[b_src, b_c], writes=[pb])
            v = psb.rearrange("q (j t) -> q j t", t=128)[:, :, 0:p]
            if evac is None:
                T.op(act, lambda e: e.copy(dst3, v), reads=[pb], writes=[b_dst])
            else:
                evac(v, pb)

        def rms_stats(xap, b_x, p, n, ssl):
            ss, b_ss = ssl
            T.op(act, lambda e: e.activation(out=junk[0:p, 0:n], in_=xap, func=AF.Square, accum_out=ss[0:p, 0:1]),
                 reads=[b_x], writes=[b_junk, b_ss])
            T.op(act, lambda e: e.activation(out=ss[0:p, 1:2], in_=ss[0:p, 0:1], func=AF.Sqrt, scale=1.0 / n, bias=EPS),
                 reads=[b_ss], writes=[b_ss])
            T.op(dve, lambda e: e.reciprocal(out=ss[0:p, 2:3], in_=ss[0:p, 1:2]), reads=[b_ss], writes=[b_ss])
            return ss[0:p, 2:3], b_ss

        def bmode(wt, wbuf, ms, subs, rhs3, rbufs, evac, nk=8, lhs_fn=None):
            for (mi, c0, mc) in ms:
                for (t0, tn) in subs:
                    pst, pb = T.ps()
                    T.mm([lambda e, kc=kc: e.matmul(pst[0:mc, 0:tn],
                                                    (lhs_fn(kc, c0, mc) if lhs_fn else wt[:, kc, c0:c0 + mc]),
                                                    rhs3[:, kc, t0:t0 + tn], start=(kc == 0), stop=(kc == nk - 1))
                          for kc in range(nk)], reads=[wbuf] + tb(rbufs, t0, tn), writes=[pb])
                    evac(mi, t0, tn, pst[0:mc, 0:tn], pb)

        def amode(lhs3, lbufs, c0, p, wt, wbuf, ncols=512):
            pst, pb = T.ps()
            T.mm([lambda e, kc=kc: e.matmul(pst[0:p, 0:ncols], lhs3[:, kc, c0:c0 + p], wt[:, kc, 0:ncols],
                                            start=(kc == 0), stop=(kc == 7)) for kc in range(8)],
                 reads=[wbuf] + lbufs, writes=[pb])
            return pst, pb

        M4 = [(m, m * 128, 128) for m in range(4)]

        with ExitStack() as ph:
            memf, b_memf = T.sb("memf", [128, 2, D], F32, ph)
            memb, b_memb = T.sb("memb", [128, 2, D], BF16, ph)
            T.dma(sp, memf[:], mem.rearrange("(t p) d -> p t d", p=128), b_memf, writes=[b_memf])
            T.op(dve, lambda e: e.tensor_copy(memb[:], memf[:]), reads=[b_memf], writes=[b_memb])
            for mt in range(2):
                transpose8(memb[:, mt, :], b_memb, 128, memT[:, :, mt * 128:(mt + 1) * 128], b_memT)
            T.mark_fence()

        chk("setup")
        for blk in range(NBLK):
            has_s = blk == 0
            tiles = [(i, 128, i * 128) for i in range(4)] + ([(4, NSMP, TB)] if has_s else [])
            subs = [(0, TB)] + ([(TB, NSMP)] if has_s else [])
            psubs = [(0, TB)]

            def xtile(i):
                return (xP[:, i, :] if i < 4 else xS[:, :])

            if blk == 0:
                for i in range(4):
                    T.dma(sp, xP[:, i, :], xp[blk * TB + i * 128: blk * TB + (i + 1) * 128, :], xb[i], writes=[xb[i]])
            if has_s:
                T.dma(sp, xS[:, :], xs[:, :], xb[4], writes=[xb[4]])

            for l in range(2):
                ph0 = ExitStack()
                xnr = Rot(T, "xn", len(tiles), [128, D], BF16, ph0)
                gbc, b_gbc = T.sb("gbc", [128, D], F32, ph0)
                T.dma(sp, gbc[:], norm_gain[l].partition_broadcast(128), b_gbc, writes=[b_gbc])
                sts = [rms_stats(xtile(i), xb[i], p, D, ssr.next()) for (i, p, c0) in tiles]
                xns = []
                for k_, (i, p, c0) in enumerate(tiles):
                    rstd, b_ss = sts[k_]
                    xn, b_xn = xnr.next()
                    T.op(dve, lambda e: e.scalar_tensor_tensor(out=xn[0:p, :], in0=xtile(i), scalar=rstd, in1=gbc[0:p, :],
                                                               op0=ALU.mult, op1=ALU.mult),
                         reads=[xb[i], b_ss, b_gbc], writes=[b_xn])
                    xns.append((xn, b_xn))
                for k_, (i, p, c0) in enumerate(tiles):
                    xn, b_xn = xns[k_]
                    transpose8(xn, b_xn, p, hT[:, :, c0:c0 + p], hb[i])
                T.mark_fence()
                ph0.close()
                chk("prenorm_%d_%d" % (blk, l))

                with ExitStack() as ph:
                    ks_f, b_ksf = T.sb("ks_f", [NSMP, 512], BF16, ph)
                    vs_f, b_vsf = T.sb("vs_f", [NSMP, D], BF16, ph)
                    Xs, b_Xs = T.sb("Xs", [NSMP, D], BF16, ph)
                    qTs_f, b_qTs = T.sb("qTs_f", [128, 4, NSMP], F32, ph)
                    brr = Rot(T, "br", 2, [128, D], BF16, ph)
                    ph2 = ExitStack()
                    qT, b_qT = T.sb("qT", [128, 4, NT], BF16, ph2)
                    kT, b_kT = T.sb("kT", [128, 4, NT], BF16, ph2)
                    ktok, b_ktok = T.sb("ktok", [128, 4, 512], BF16, ph2)
                    vtok, b_vtok = T.sb("vtok", [128, 4, D], BF16, ph2)
                    Xtok, b_Xtok = T.sb("Xtok", [128, 4, D], BF16, ph2)
                    tmpF = Rot(T, "tmpF", 4, [128, 512], F32, ph2)
                    splr = Rot(T, "spl", 4, [128, 512], F32, ph2)
                    tmpB = Rot(T, "tmpB", 12, [128, 512], BF16, ph2)
                    dec, b_dec = T.sb("dec", [128, 4, 4], F32, ph2)
                    osbr = Rot(T, "osb", 2, [128, D], BF16, ph2)
                    ggl, b_ggl = T.sb("ggl", [128, D], F32, ph2)
                    T.dma(sp, ggl[:], gla_gain[l].partition_broadcast(128), b_ggl, writes=[b_ggl])

                    def ev_a(mi, t0, tn, ps_, pb):
                        T.op(act, lambda e: e.copy(a_aug[0:16, t0:t0 + tn], ps_), reads=[pb], writes=[b_aaug])
                    bmode(wa[:, l], b_wa, [(0, 0, 16)], subs, hT, hb, ev_a)
                    wt, wbuf = wget(("in", l, Q0))

                    def ev_q(mi, t0, tn, ps_, pb):
                        T.op(act, lambda e: e.activation(out=qT[:, mi, t0:t0 + tn], in_=ps_, func=AF.Copy, scale=128.0 ** -0.5),
                             reads=[pb], writes=[b_qT])
                        if t0 >= TB:
                            T.op(dve, lambda e: e.tensor_scalar(out=qTs_f[:, mi, :], in0=ps_, scalar1=128.0 ** -0.5, scalar2=None,
                                                                op0=ALU.mult), reads=[pb], writes=[b_qTs])
                    bmode(wt, wbuf, M4, subs, hT, hb, ev_q)
                    wt, wbuf = wget(("in", l, K0))

                    def ev_k(mi, t0, tn, ps_, pb):
                        T.op(dve, lambda e: e.tensor_copy(kT[:, mi, t0:t0 + tn], ps_), reads=[pb], writes=[b_kT])
                    bmode(wt, wbuf, M4, psubs, hT, hb, ev_k)
                    for (i, p, c0) in tiles:
                        pst, pb = amode(hT, [hb[i]], c0, p, wt, wbuf)
                        if i < 4:
                            T.op(act, lambda e: e.copy(ktok[:, i, :], pst[:, 0:512]), reads=[pb], writes=[b_ktok])
                        else:
                            T.op(act, lambda e: e.copy(ks_f[:, :], pst[0:p, 0:512]), reads=[pb], writes=[b_ksf])
                    for j in range(2):
                        wt, wbuf = wget(("in", l, V0 + j * 512))
                        for (i, p, c0) in tiles:
                            pst, pb = amode(hT, [hb[i]], c0, p, wt, wbuf)
                            if i < 4:
                                T.op(act, lambda e: e.copy(vtok[:, i, j * 512:(j + 1) * 512], pst[:, 0:512]), reads=[pb], writes=[b_vtok])
                            else:
                                T.op(act, lambda e: e.copy(vs_f[:, j * 512:(j + 1) * 512], pst[0:p, 0:512]), reads=[pb], writes=[b_vsf])
                    for j in range(2):
                        wt, wbuf = wget(("in", l, GG0 + j * 512))
                        for (i, p, c0) in tiles:
                            pst, pb = amode(hT, [hb[i]], c0, p, wt, wbuf)
                            tf, b_tf = tmpF.next()
                            T.op(act, lambda e: e.activation(out=tf[0:p, :], in_=pst[0:p, 0:512], func=AF.Silu), reads=[pb], writes=[b_tf])
                            dst, b_dst = (Xtok[:, i, j * 512:(j + 1) * 512], b_Xtok) if i < 4 else (Xs[:, j * 512:(j + 1) * 512], b_Xs)
                            T.op(dve, lambda e: e.tensor_tensor(out=dst, in0=tf[0:p, :], in1=ggl[0:p, j * 512:(j + 1) * 512], op=ALU.mult),
                                 reads=[b_tf, b_ggl], writes=[b_dst])

                    T.op(act, lambda e: e.copy(Sbf[:], S[:, l, :]), reads=[b_S], writes=[b_Sbf])
                    chk("glaproj_%d_%d" % (blk, l))

                    def post(osrc, b_os, p, Xap, b_X, c0, i, split=False):
                        ss, b_ss = ssr.next()
                        for h in range(4):
                            T.op(act, lambda e, h=h: e.activation(out=junk[0:p, 0:256], in_=osrc(h), func=AF.Square,
                                                                  accum_out=ss[0:p, h:h + 1]), reads=b_os, writes=[b_junk, b_ss])
                        T.op(act, lambda e: e.activation(out=ss[0:p, 4:8], in_=ss[0:p, 0:4], func=AF.Sqrt, scale=1.0 / 256, bias=EPS),
                             reads=[b_ss], writes=[b_ss])
                        T.op(dve, lambda e: e.reciprocal(out=ss[0:p, 8:12], in_=ss[0:p, 4:8]), reads=[b_ss], writes=[b_ss])
                        br, b_br = brr.next()
                        for h in range(4):
                            T.op(dve, lambda e, h=h: e.scalar_tensor_tensor(out=br[0:p, h * 256:(h + 1) * 256], in0=osrc(h),
                                                                            scalar=ss[0:p, 8 + h:9 + h], in1=Xap[:, h * 256:(h + 1) * 256],
                                                                            op0=ALU.mult, op1=ALU.mult),
                                 reads=b_os + [b_ss, b_X], writes=[b_br])
                        if split:
                            return (lambda: transpose8(br, b_br, p, brT[:, :, c0:c0 + p], brb[i]))
                        transpose8(br, b_br, p, brT[:, :, c0:c0 + p], brb[i])

                    zps = []
                    for i in range(4):
                        tok = slice(i * 128, (i + 1) * 128)
                        zp, zb = T.ps()
                        T.mm([lambda e: e.matmul(zp[:, 0:512], a_aug[0:17, tok], w_a2aug[0:17, l, :], start=True, stop=True)],
                             reads=[b_aaug, b_wa2], writes=[zb])
                        zps.append((zp, zb))
                    spls = []
                    for i in range(4):
                        zp, zb = zps[i]
                        spl, b_spl = splr.next()
                        T.op(act, lambda e: e.activation(out=spl[:], in_=zp[:, 0:512], func=AF.Exp, scale=-1.0), reads=[zb], writes=[b_spl])
                        T.op(act, lambda e: e.activation(out=spl[:], in_=spl[:], func=AF.Ln, bias=1.0, scale=1.0), reads=[b_spl], writes=[b_spl])
                        spls.append((spl, b_spl))
                    cums = []
                    for i in range(4):
                        spl, b_spl = spls[i]
                        bp, bb = T.ps()
                        T.mm([lambda e, h=h: e.matmul(bp[:, h * 128:(h + 1) * 128], spl[:, h * 128:(h + 1) * 128], triU_f[:], start=True, stop=True)
                              for h in range(4)], reads=[b_spl, b_c], writes=[bb])
                        rp, rb = T.ps()
                        T.mm([lambda e: e.matmul(rp[:, 0:512], triLs_f[:], spl[:], start=True, stop=True)], reads=[b_spl, b_c], writes=[rb])
                        cums.append((bp, bb, rp, rb))
                    qk = []
                    for i in range(4):
                        tok = slice(i * 128, (i + 1) * 128)
                        bp, bb, rp, rb = cums[i]
                        expb, b_expb = tmpF.next()
                        T.op(act, lambda e: e.activation(out=expb[:], in_=bp[:, 0:512], func=AF.Exp, scale=-1.0 / 16), reads=[bb], writes=[b_expb])
                        expnb, b_expnb = tmpF.next()
                        T.op(act, lambda e: e.activation(out=expnb[:], in_=bp[:, 0:512], func=AF.Exp, scale=1.0 / 16), reads=[bb], writes=[b_expnb])
                        expr, b_expr = tmpF.next()
                        T.op(act, lambda e: e.activation(out=expr[:], in_=rp[:, 0:512], func=AF.Exp, scale=-1.0 / 16), reads=[rb], writes=[b_expr])
                        qdec, b_qdec = tmpB.next()
                        T.op(dve, lambda e: e.tensor_tensor(out=qdec[:].rearrange("q (h c) -> q h c", h=4), in0=qT[:, :, tok],
                                                            in1=expb[:].rearrange("q (h c) -> q h c", h=4), op=ALU.mult),
                             reads=[b_qT, b_expb], writes=[b_qdec])
                        T.op(dve, lambda e: e.tensor_copy(dec[:, i, :], expb[:].rearrange("q (h c) -> q h c", h=4)[:, :, 127]),
                             reads=[b_expb], writes=[b_dec])
                        kinv, b_kinv = tmpB.next()
                        T.op(dve, lambda e: e.tensor_tensor(out=kinv[:].rearrange("q (h c) -> q h c", h=4), in0=kT[:, :, tok],
                                                            in1=expnb[:].rearrange("q (h c) -> q h c", h=4), op=ALU.mult),
                             reads=[b_kT, b_expnb], writes=[b_kinv])
                        kend, b_kend = tmpB.next()
                        T.op(dve, lambda e: e.tensor_tensor(out=kend[:], in0=ktok[:, i, :], in1=expr[:], op=ALU.mult),
                             reads=[b_ktok, b_expr], writes=[b_kend])
                        qk.append((qdec, b_qdec, kinv, b_kinv, kend, b_kend))
                    atts = []
                    for i in range(4):
                        qdec, b_qdec, kinv, b_kinv, kend, b_kend = qk[i]
                        ap_, ab = T.ps()
                        T.mm([lambda e, h=h: e.matmul(ap_[:, h * 128:(h + 1) * 128], kinv[:, h * 128:(h + 1) * 128],
                                                      qdec[:, h * 128:(h + 1) * 128], start=True, stop=True) for h in range(4)],
                             reads=[b_kinv, b_qdec], writes=[ab])
                        atts.append((ap_, ab))
                    for i in range(4):
                        ap_, ab = atts[i]
                        attT, b_attT = qk[i][2], qk[i][3]
                        T.op(dve, lambda e: e.tensor_tensor(out=attT[:], in0=ap_[:, 0:512], in1=triU4_b[:], op=ALU.mult),
                             reads=[ab, b_c], writes=[b_attT])
                        atts[i] = (attT, b_attT)
                    pend = None
                    pend_b = None
                    T.ps_range = (4, 8)
                    for i in range(4):
                        qdec, b_qdec, kinv, b_kinv, kend, b_kend = qk[i]
                        attT, b_attT = atts[i]
                        ob = [T.psl[(i % 2) * 2], T.psl[(i % 2) * 2 + 1]]
                        for bk in range(2):
                            fns = []
                            for hh in range(2):
                                h = bk * 2 + hh
                                o_ap = ob[bk][0][:, hh * 256:(hh + 1) * 256]
                                fns.append(lambda e, h=h, o_ap=o_ap: e.matmul(o_ap, qdec[:, h * 128:(h + 1) * 128], Sbf[:, h * 256:(h + 1) * 256],
                                                                              start=True, stop=False))
                                fns.append(lambda e, h=h, o_ap=o_ap: e.matmul(o_ap, attT[:, h * 128:(h + 1) * 128], vtok[:, i, h * 256:(h + 1) * 256],
                                                                              start=False, stop=True))
                            T.mm(fns, reads=[b_qdec, b_Sbf, b_attT, b_vtok], writes=[ob[bk][1]])
                        db = [T.ps(), T.ps()]
                        for bk in range(2):
                            T.mm([lambda e, h=bk * 2 + hh, hh=hh: e.matmul(db[bk][0][:, hh * 256:(hh + 1) * 256], kend[:, h * 128:(h + 1) * 128],
                                                                           vtok[:, i, h * 256:(h + 1) * 256], start=True, stop=True)
                                  for hh in range(2)], reads=[b_kend, b_vtok], writes=[db[bk][1]])
                        osb_t, b_osb = osbr.next()
                        for bk in range(2):
                            T.op(act, lambda e, bk=bk: e.copy(osb_t[:, bk * 512:(bk + 1) * 512], ob[bk][0][:, 0:512]), reads=[ob[bk][1]], writes=[b_osb])
                        for h in range(4):
                            T.op(dve, lambda e, h=h: e.scalar_tensor_tensor(out=S[:, l, h * 256:(h + 1) * 256], in0=S[:, l, h * 256:(h + 1) * 256],
                                                                            scalar=dec[:, i, h:h + 1],
                                                                            in1=db[h // 2][0][:, (h % 2) * 256:(h % 2 + 1) * 256],
                                                                            op0=ALU.mult, op1=ALU.add),
                                 reads=[b_S, b_dec, db[h // 2][1]], writes=[b_S])
                        T.op(dve, lambda e: e.tensor_copy(Sbf[:], S[:, l, :]), reads=[b_S], writes=[b_Sbf])
                        if pend_b is not None:
                            pend_b()
                            pend_b = None
                        if pend is not None:
                            pend_b = pend()
                        pend = (lambda osb_t=osb_t, b_osb=b_osb, i=i: post(lambda h: osb_t[:, h * 256:(h + 1) * 256], [b_osb], 128,
                                                                           Xtok[:, i, :], b_Xtok, i * 128, i, split=True))
                    if pend_b is not None:
                        pend_b()
                    pend()()
                    T.ps_range = (0, 8)
                    if blk == NBLK - 1:
                        T.dma(sp, oglap[l].rearrange("h k v -> k h v"), S[:, l, :].rearrange("q (h v) -> q h v", h=4), b_S, reads=[b_S])

                    T.mark_fence()
                    ph2.close()
                    chk("glachunk_%d_%d" % (blk, l))
                    if has_s:
                        aTs, b_aTs = T.sb("aTs", [128, 64], F32, ph)
                        qmask, b_qmask = T.sb("qmask", [128, 4, NSMP, NSMP], BF16, ph)
                        s0r = Rot(T, "s0", 2, [128, D], F32, ph)
                        snr = Rot(T, "sn", 2, [128, D], F32, ph)
                        kmr = Rot(T, "km", 2, [NSMP, 512], BF16, ph)
                        snbr = Rot(T, "snb", 2, [128, D], BF16, ph)
                        zs, zsb = T.ps()
                        T.mm([lambda e, h=h: e.matmul(zs[:, h * 16:(h + 1) * 16], w_a2aug[0:17, l, h * 128:(h + 1) * 128], a_aug[0:17, TB:NT],
                                                      start=True, stop=True) for h in range(4)], reads=[b_aaug, b_wa2], writes=[zsb])
                        T.op(act, lambda e: e.activation(out=aTs[:], in_=zs[:, 0:64], func=AF.Exp, scale=-1.0), reads=[zsb], writes=[b_aTs])
                        T.op(act, lambda e: e.activation(out=aTs[:], in_=aTs[:], func=AF.Ln, bias=1.0, scale=1.0), reads=[b_aTs], writes=[b_aTs])
                        T.op(act, lambda e: e.activation(out=aTs[:], in_=aTs[:], func=AF.Exp, scale=-1.0 / 16), reads=[b_aTs], writes=[b_aTs])
                        T.op(dve, lambda e: e.memset(qmask[:], 0.0), writes=[b_qmask])
                        for b in range(NSMP):
                            T.op(dve, lambda e, b=b: e.tensor_copy(qmask[:, :, b, b], qTs_f[:, :, b]), reads=[b_qTs], writes=[b_qmask])
                        sst = {}

                        def stA(b):
                            s0, b_s0 = s0r.next()
                            T.dma(sp, s0[:].rearrange("q (h v) -> q h v", h=4), sg[l, b].rearrange("h k v -> k h v"), b_s0, writes=[b_s0])
                            km, b_km = kmr.next()
                            T.op(dve, lambda e: e.tensor_scalar(out=km[:], in0=ks_f[:], scalar1=ident_f[0:NSMP, b:b + 1], scalar2=None,
                                                                op0=ALU.mult), reads=[b_ksf, b_c], writes=[b_km])
                            kvb = [T.ps(), T.ps()]
                            for bk in range(2):
                                T.mm([lambda e, h=bk * 2 + hh, hh=hh: e.matmul(kvb[bk][0][:, hh * 256:(hh + 1) * 256], km[0:NSMP, h * 128:(h + 1) * 128],
                                                                               vs_f[0:NSMP, h * 256:(h + 1) * 256], start=True, stop=True)
                                      for hh in range(2)], reads=[b_km, b_vsf], writes=[kvb[bk][1]])
                            sst[b] = [s0, b_s0, kvb]

                        def stB(b):
                            s0, b_s0, kvb = sst[b]
                            sn, b_sn = snr.next()
                            for h in range(4):
                                T.op(dve, lambda e, h=h: e.scalar_tensor_tensor(out=sn[:, h * 256:(h + 1) * 256], in0=s0[:, h * 256:(h + 1) * 256],
                                                                                scalar=aTs[:, h * 16 + b:h * 16 + b + 1],
                                                                                in1=kvb[h // 2][0][:, (h % 2) * 256:(h % 2 + 1) * 256],
                                                                                op0=ALU.mult, op1=ALU.add),
                                     reads=[b_s0, b_aTs, kvb[h // 2][1]], writes=[b_sn])
                            T.dma(sp, oglas[l, b].rearrange("h k v -> k h v"), sn[:].rearrange("q (h v) -> q h v", h=4), b_sn, reads=[b_sn])
                            snb, b_snb = snbr.next()
                            T.op(act, lambda e: e.copy(snb[:], sn[:]), reads=[b_sn], writes=[b_snb])
                            sst[b] = [snb, b_snb]

                        def stC(b):
                            snb, b_snb = sst.pop(b)
                            T.mm([lambda e, h=h: e.matmul(T.psl[h][0][0:NSMP, 0:256], qmask[:, h, b, :], snb[:, h * 256:(h + 1) * 256],
                                                          start=(b == 0), stop=(b == NSMP - 1)) for h in range(4)],
                                 reads=[b_qmask, b_snb], writes=[T.psl[h][1] for h in range(4)])

                        T.ps_range = (4, 8)
                        stA(0)
                        for b in range(NSMP):
                            if b + 1 < NSMP:
                                stA(b + 1)
                            stB(b)
                            if b >= 1:
                                stC(b - 1)
                        stC(NSMP - 1)
                        post(lambda h: T.psl[h][0][0:NSMP, 0:256], [T.psl[h][1] for h in range(4)], NSMP, Xs[:, :], b_Xs, TB, 4)
                        T.ps_range = (0, 8)
                    T.mark_fence()

                def merge(bi):
                    for j in range(2):
                        wt, wbuf = wget(("in", l, MG0 + bi * D + j * 512))

                        def ev_g(mi, t0, tn, ps_, pb):
                            T.op(act, lambda e: e.activation(out=gblk[:, mi, t0:t0 + tn], in_=ps_, func=AF.Sigmoid), reads=[pb], writes=[b_gblk])
                        bmode(wt, wbuf, M4, subs, hT, hb, ev_g)
                        wt, wbuf = wget(("br", l, bi, j))

                        def ev_m(mi, t0, tn, ps_, pb):
                            dst = mgT[:, 4 * j + mi, t0:t0 + tn]
                            mb = tb(mgb, t0, tn)
                            if bi == 0:
                                T.op(dve, lambda e: e.tensor_tensor(out=dst, in0=ps_, in1=gblk[:, mi, t0:t0 + tn], op=ALU.mult),
                                     reads=[pb, b_gblk], writes=mb)
                            else:
                                tm, b_tm = mtmp.next()
                                T.op(dve, lambda e: e.tensor_tensor(out=tm[:, 0:tn], in0=ps_, in1=gblk[:, mi, t0:t0 + tn], op=ALU.mult),
                                     reads=[pb, b_gblk], writes=[b_tm])
                                T.op(dve, lambda e: e.tensor_tensor(out=dst, in0=dst, in1=tm[:, 0:tn], op=ALU.add),
                                     reads=[b_tm] + mb, writes=mb)
                        bmode(wt, wbuf, M4, subs, brT, brb, ev_m)

                chk("glasample_%d_%d" % (blk, l))
                merge(0)
                chk("merge0_%d_%d" % (blk, l))

                with ExitStack() as ph:
                    L = 15 + TB
                    uT, b_uT = T.sb("uT", [128, 4, 15 + NT], F32, ph)
                    wa_, b_wa_ = T.sb("wa_", [128, L], F32, ph)
                    wb_, b_wb_ = T.sb("wb_", [128, L], F32, ph)
                    diffT, b_diffT = T.sb("diffT", [128, 8, NT], BF16, ph)
                    spg, b_spg = T.sb("spg", [128, 8, NT], BF16, ph)
                    us_f, b_usf = T.sb("us_f", [NSMP, D], F32, ph)
                    ulast, b_ulast = T.sb("ulast", [128, D], F32, ph)
                    t16, b_t16 = T.sb("t16", [128, 16], F32, ph)
                    mtmp = Rot(T, "mtmp", 2, [128, 512], BF16, ph)
                    for jb in range(2):
                        wt, wbuf = wget(("in", l, U0 + jb * 512))

                        def ev_u(mi, t0, tn, ps_, pb):
                            T.op(act, lambda e: e.copy(uT[:, mi, 15 + t0:15 + t0 + tn], ps_), reads=[pb], writes=[b_uT])
                        bmode(wt, wbuf, M4, psubs, hT, hb, ev_u)
                        if has_s:
                            pst, pb = amode(hT, [hb[4]], TB, NSMP, wt, wbuf)
                            T.op(act, lambda e: e.copy(us_f[:, jb * 512:(jb + 1) * 512], pst[0:NSMP, 0:512]), reads=[pb], writes=[b_usf])
                        if blk == NBLK - 1:
                            pst, pb = amode(hT, [hb[3]], 384, 128, wt, wbuf)
                            T.op(act, lambda e: e.copy(ulast[:, jb * 512:(jb + 1) * 512], pst[:, 0:512]), reads=[pb], writes=[b_ulast])
                        for mi in range(4):
                            ch = 4 * jb + mi
                            g = ch // 2
                            w = 2 << g
                            u = uT[:, mi, :]
                            T.op(dve, lambda e: e.tensor_copy(uT[:, mi, 0:15], uhist[:, l, ch, :]), reads=[b_uhist], writes=[b_uT])
                            T.op(dve, lambda e: e.tensor_tensor(out=wa_[:, 1:L], in0=u[:, 1:L], in1=u[:, 0:L - 1], op=ALU.add),
                                 reads=[b_uT], writes=[b_wa_])
                            s_ap, b_s = wa_, b_wa_
                            if g >= 1:
                                T.op(dve, lambda e: e.tensor_tensor(out=wb_[:, 3:L], in0=wa_[:, 3:L], in1=wa_[:, 1:L - 2], op=ALU.add),
                                     reads=[b_wa_], writes=[b_wb_])
                                s_ap, b_s = wb_, b_wb_
                            if g >= 2:
                                T.op(dve, lambda e: e.tensor_tensor(out=wa_[:, 7:L], in0=wb_[:, 7:L], in1=wb_[:, 3:L - 4], op=ALU.add),
                                     reads=[b_wb_], writes=[b_wa_])
                                s_ap, b_s = wa_, b_wa_
                            if g >= 3:
                                T.op(dve, lambda e: e.tensor_tensor(out=wb_[:, 15:L], in0=wa_[:, 15:L], in1=wa_[:, 7:L - 8], op=ALU.add),
                                     reads=[b_wa_], writes=[b_wb_])
                                s_ap, b_s = wb_, b_wb_
                            T.op(dve, lambda e: e.scalar_tensor_tensor(out=diffT[:, ch, 0:TB], in0=s_ap[:, 15:L], scalar=1.0 / w,
                                                                       in1=u[:, 15:L], op0=ALU.mult, op1=ALU.subtract),
                                 reads=[b_s, b_uT], writes=[b_diffT])
                            if blk == 0:
                                T.op(dve, lambda e: e.tensor_tensor(out=t16[:], in0=s_ap[:, 15:31], in1=rc[:, g, :], op=ALU.mult),
                                     reads=[b_s, b_c], writes=[b_t16])
                                T.op(dve, lambda e: e.tensor_tensor(out=diffT[:, ch, 0:16], in0=t16[:], in1=u[:, 15:31], op=ALU.subtract),
                                     reads=[b_t16, b_uT], writes=[b_diffT])
                            if blk < NBLK - 1:
                                T.op(dve, lambda e: e.tensor_copy(uhist[:, l, ch, :], uT[:, mi, TB:TB + 15]), reads=[b_uT], writes=[b_uhist])
                    if blk == NBLK - 1:
                        T.dma(sp, opoolp[l], ulast[113:128, :], b_ulast, reads=[b_ulast])
                    if has_s:
                        hbuf, b_hbuf = T.sb("hbuf", [NSMP, 15, 256], F32, ph)
                        hs, b_hs = T.sb("hs", [NSMP, 256], F32, ph)
                        dsb, b_dsb = T.sb("dsb", [NSMP, D], BF16, ph)
                        for g in range(4):
                            w = 2 << g
                            cs = slice(g * 256, (g + 1) * 256)
                            T.dma(sp, hbuf[:, 0:w - 1, :], spool[l, :, 15 - (w - 1):15, cs], b_hbuf, writes=[b_hbuf])
                            T.op(dve, lambda e: e.tensor_reduce(out=hs[:], in_=hbuf[:, 0:w - 1, :].rearrange("p r c -> p c r"), axis=AX.X, op=ALU.add),
                                 reads=[b_hbuf], writes=[b_hs])
                            T.op(dve, lambda e: e.tensor_tensor(out=hs[:], in0=hs[:], in1=us_f[:, cs], op=ALU.add), reads=[b_hs, b_usf], writes=[b_hs])
                            T.op(dve, lambda e: e.scalar_tensor_tensor(out=dsb[:, cs], in0=hs[:], scalar=1.0 / w, in1=us_f[:, cs],
                                                                       op0=ALU.mult, op1=ALU.subtract), reads=[b_hs, b_usf], writes=[b_dsb])
                        transpose8(dsb, b_dsb, NSMP, diffT[:, :, TB:NT], b_diffT)
                        T.dma(sp, opools[l, :, 14, :], us_f[:, :], b_usf, reads=[b_usf])
                        T.dma(sp, opools[l, :, 0:14, :], spool[l, :, 1:15, :], b_dd)
                    for jb in range(2):
                        wt, wbuf = wget(("in", l, PG0 + jb * 512))

                        def ev_pg(mi, t0, tn, ps_, pb):
                            T.op(act, lambda e: e.activation(out=spg[:, 4 * jb + mi, t0:t0 + tn], in_=ps_, func=AF.Silu), reads=[pb], writes=[b_spg])
                        bmode(wt, wbuf, M4, subs, hT, hb, ev_pg)
                    for jb in range(2):
                        for gg in range(2):
                            g = 2 * jb + gg
                            for ee in range(2):
                                ch = 2 * g + ee
                                for (t0, tn) in subs:
                                    pst, pb = T.ps()
                                    T.mm([lambda e, k2=k2: e.matmul(pst[:, 0:tn], pw[:, l, g, k2, ee * 128:(ee + 1) * 128],
                                                                    diffT[:, 2 * g + k2, t0:t0 + tn], start=(k2 == 0), stop=(k2 == 1))
                                          for k2 in range(2)], reads=[b_pw, b_diffT], writes=[pb])
                                    T.op(dve, lambda e: e.scalar_tensor_tensor(out=brT[:, ch, t0:t0 + tn], in0=pst[:, 0:tn],
                                                                               scalar=pscT[:, l, ch:ch + 1], in1=spg[:, ch, t0:t0 + tn],
                                                                               op0=ALU.mult, op1=ALU.mult),
                                         reads=[pb, b_pscT, b_spg], writes=tb(brb, t0, tn))
                    merge(1)
                    T.mark_fence()
                chk("pool_%d_%d" % (blk, l))

                with ExitStack() as ph:
                    xqb, b_xqb = T.sb("xqb", [128, 4, NT], BF16, ph)
                    sxg, b_sxg = T.sb("sxg", [128, 4, TB], BF16, ph)
                    sxgs, b_sxgs = T.sb("sxgs", [128, 8, NSMP], BF16, ph)
                    pTr = Rot(T, "pT", 2, [128, 2, TB], BF16, ph)
                    rinv, b_rinv = T.sb("rinv", [128, TB], F32, ph)
                    tmpx = Rot(T, "tmpx", 2, [128, TB], F32, ph)
                    stg = Rot(T, "stg", 2, [128, 512], F32, ph)
                    xqs_f, b_xqs = T.sb("xqs_f", [NSMP, D], BF16, ph)
                    mtmp = Rot(T, "mtmp", 2, [128, 512], BF16, ph)
                    if blk == 0:
                        for j in range(2):
                            wt, wbuf = wget(("mk", l, j))
                            for mt in range(2):
                                pst, pb = amode(memT, [b_memT], mt * 128, 128, wt, wbuf)
                                st_, b_st = stg.next()
                                T.op(act, lambda e: e.copy(st_[:], pst[:, 0:512]), reads=[pb], writes=[b_st])
                                T.dma(sp, omk[l, mt * 128:(mt + 1) * 128, j * 512:(j + 1) * 512], st_[:], b_st, reads=[b_st])

                            def ev_mk(mi, t0, tn, ps_, pb):
                                T.op(dve, lambda e: e.tensor_copy(mkT[:, l, 4 * j + mi, :], ps_), reads=[pb], writes=[b_mkT])
                            bmode(wt, wbuf, M4, [(0, 256)], memT, [b_memT] * 5, ev_mk)
                        for j in range(2):
                            wt, wbuf = wget(("mv", l, j))
                            for mt in range(2):
                                pst, pb = amode(memT, [b_memT], mt * 128, 128, wt, wbuf)
                                st_, b_st = stg.next()
                                T.op(act, lambda e: e.copy(st_[:], pst[:, 0:512]), reads=[pb], writes=[b_st])
                                T.dma(sp, omv[l, mt * 128:(mt + 1) * 128, j * 512:(j + 1) * 512], st_[:], b_st, reads=[b_st])
                                T.op(dve, lambda e: e.tensor_copy(mvb[:, l, mt, j * 512:(j + 1) * 512], pst[:, 0:512]), reads=[pb], writes=[b_mvb])
                    chk("xa1_%d_%d" % (blk, l))
                    for jb in range(2):
                        wt, wbuf = wget(("in", l, XQ0 + jb * 512))

                        def ev_xq(mi, t0, tn, ps_, pb):
                            T.op(act, lambda e: e.activation(out=xqb[:, mi, t0:t0 + tn], in_=ps_, func=AF.Copy, scale=1.0 / 16),
                                 reads=[pb], writes=[b_xqb])
                        bmode(wt, wbuf, M4, psubs, hT, hb, ev_xq)
                        if has_s:
                            pst, pb = amode(hT, [hb[4]], TB, NSMP, wt, wbuf)
                            T.op(act, lambda e: e.activation(out=xqs_f[:, jb * 512:(jb + 1) * 512], in_=pst[0:NSMP, 0:512], func=AF.Copy, scale=1.0 / 16),
                                 reads=[pb], writes=[b_xqs])
                        wt, wbuf = wget(("in", l, XG0 + jb * 512))

                        def ev_xg(mi, t0, tn, ps_, pb):
                            if t0 >= TB:
                                T.op(act, lambda e: e.activation(out=sxgs[:, 4 * jb + mi, :], in_=ps_, func=AF.Silu), reads=[pb], writes=[b_sxgs])
                            else:
                                T.op(act, lambda e: e.activation(out=sxg[:, mi, t0:t0 + tn], in_=ps_, func=AF.Silu), reads=[pb], writes=[b_sxg])
                        bmode(wt, wbuf, M4, subs, hT, hb, ev_xg)
                        hst = {}

                        def head_a(hh):
                            h = 2 * jb + hh
                            pT, b_pT = pTr.next()
                            for mc in range(2):
                                sp_, sb_ = T.ps()
                                T.mm([lambda e, dc=dc: e.matmul(sp_[:, 0:TB], mkT[:, l, 2 * h + dc, mc * 128:(mc + 1) * 128], xqb[:, 2 * hh + dc, 0:TB],
                                                                start=(dc == 0), stop=(dc == 1)) for dc in range(2)],
                                     reads=[b_mkT, b_xqb], writes=[sb_])
                                T.op(act, lambda e: e.activation(out=pT[:, mc, :], in_=sp_[:, 0:TB], func=AF.Exp), reads=[sb_], writes=[b_pT])
                            hst[hh] = (pT, b_pT)

                        def head_b(hh):
                            h = 2 * jb + hh
                            pT, b_pT = hst[hh]
                            sm_, smb = T.ps()
                            T.mm([lambda e, mc=mc: e.matmul(sm_[:, 0:TB], ones_b[:], pT[:, mc, :], start=(mc == 0), stop=(mc == 1)) for mc in range(2)],
                                 reads=[b_pT, b_c], writes=[smb])
                            T.op(act, lambda e: e.activation(out=rinv[:], in_=sm_[:, 0:TB], func=AF.Ln), reads=[smb], writes=[b_rinv])
                            T.op(act, lambda e: e.activation(out=rinv[:], in_=rinv[:], func=AF.Exp, scale=-1.0), reads=[b_rinv], writes=[b_rinv])
                            for dc in range(2):
                                op_, opb = T.ps()
                                T.mm([lambda e, mc=mc: e.matmul(op_[:, 0:TB], mvb[:, l, mc, (2 * h + dc) * 128:(2 * h + dc + 1) * 128], pT[:, mc, :],
                                                                start=(mc == 0), stop=(mc == 1)) for mc in range(2)],
                                     reads=[b_mvb, b_pT], writes=[opb])
                                tx, b_tx = tmpx.next()
                                T.op(dve, lambda e: e.tensor_tensor(out=tx[:], in0=op_[:, 0:TB], in1=rinv[:], op=ALU.mult),
                                     reads=[opb, b_rinv], writes=[b_tx])
                                T.op(dve, lambda e: e.tensor_tensor(out=brT[:, 2 * h + dc, 0:TB], in0=tx[:], in1=sxg[:, 2 * hh + dc, :], op=ALU.mult),
                                     reads=[b_tx, b_sxg], writes=brb[0:4])

                        head_a(0)
                        head_a(1)
                        head_b(0)
                        head_b(1)
                    chk("xa2_%d_%d" % (blk, l))
                    if has_s:
                        kbr = Rot(T, "kbuf", 2, [128, D], F32, ph)
                        vbuf = Rot(T, "vbuf", 2, [128, 2, D], BF16, ph)
                        vfr = Rot(T, "vf", 2, [128, D], F32, ph)
                        prodr = Rot(T, "prod", 3, [128, 512], F32, ph)
                        sTall, b_sTall = T.sb("sTall", [128, 2, NSMP, 4], F32, ph)
                        stok, b_stok = T.sb("stok", [64, 256], F32, ph)
                        pnf, b_pnf = T.sb("pnf", [64, 256], F32, ph)
                        pn, b_pn = T.sb("pn", [64, 256], BF16, ph)
                        smx, b_smx = T.sb("smx", [64, 8], F32, ph)
                        pTs, b_pTs = T.sb("pTs", [128, 2, NSMP, 4], BF16, ph)
                        pmask, b_pmask = T.sb("pmask", [128, 2, 4, NSMP, NSMP], BF16, ph)
                        cb, b_cb = T.sb("cb", [NSMP, D], BF16, ph)
                        sel, b_sel = T.sb("sel", [NSMP, NSMP, 128], BF16, ph)
                        for b in range(NSMP):
                            T.op(dve, lambda e, b=b: e.tensor_scalar(out=sel[:, b, :], in0=ones_f[0:NSMP, :], scalar1=ident_f[0:NSMP, b:b + 1],
                                                                     scalar2=None, op0=ALU.mult), reads=[b_c], writes=[b_sel])
                        for b in range(NSMP):
                            qb = [T.ps(), T.ps()]
                            for bk in range(2):
                                T.mm([lambda e, bk=bk, b=b: e.matmul(qb[bk][0][:, 0:512], sel[0:NSMP, b, :], xqs_f[0:NSMP, bk * 512:(bk + 1) * 512],
                                                                     start=True, stop=True)], reads=[b_sel, b_xqs], writes=[qb[bk][1]])
                            for mt in range(2):
                                kb, b_kb = kbr.next()
                                T.dma(sp, kb[:], ck[l, b, mt * 128:(mt + 1) * 128, :], b_kb, writes=[b_kb])
                                pa, b_pa = prodr.next()
                                pb_, b_pb = prodr.next()
                                T.op(dve, lambda e: e.tensor_tensor(out=pa[:], in0=qb[0][0][:, 0:512], in1=kb[:, 0:512], op=ALU.mult),
                                     reads=[qb[0][1], b_kb], writes=[b_pa])
                                T.op(dve, lambda e: e.tensor_tensor(out=pb_[:], in0=qb[1][0][:, 0:512], in1=kb[:, 512:1024], op=ALU.mult),
                                     reads=[qb[1][1], b_kb], writes=[b_pb])
                                T.op(dve, lambda e, mt=mt, b=b: e.tensor_reduce(out=sTall[:, mt, b, 0:2], in_=pa[:].rearrange("q (h d) -> q h d", h=2),
                                                                                axis=AX.X, op=ALU.add), reads=[b_pa], writes=[b_sTall])
                                for hh in range(2):
                                    T.op(act, lambda e, mt=mt, b=b, hh=hh: e.activation(out=junk[:, 0:256], in_=pb_[:, hh * 256:(hh + 1) * 256], func=AF.Copy,
                                                                                         accum_out=sTall[:, mt, b, 2 + hh:3 + hh]),
                                         reads=[b_pb], writes=[b_junk, b_sTall])
                        chk("xa3_%d_%d" % (blk, l))
                        for mt in range(2):
                            tp_, tpb = T.ps()
                            T.mm([lambda e, mt=mt: e.transpose(tp_[0:64, 0:128], sTall[:, mt].rearrange("q b h -> q (b h)"), ident_f[:])],
                                 reads=[b_sTall, b_c], writes=[tpb])
                            T.op(act, lambda e, mt=mt: e.copy(stok[:, mt * 128:(mt + 1) * 128], tp_[0:64, 0:128]), reads=[tpb], writes=[b_stok])
                        T.op(dve, lambda e: e.reduce_max(out=smx[:, 0:1], in_=stok[:], axis=AX.X), reads=[b_stok], writes=[b_smx])
                        T.op(dve, lambda e: e.tensor_scalar(out=smx[:, 1:2], in0=smx[:, 0:1], scalar1=-1.0, scalar2=None, op0=ALU.mult),
                             reads=[b_smx], writes=[b_smx])
                        T.op(act, lambda e: e.activation(out=pnf[:], in_=stok[:], func=AF.Exp, bias=smx[:, 1:2], scale=1.0, accum_out=smx[:, 2:3]),
                             reads=[b_stok, b_smx], writes=[b_pnf, b_smx])
                        T.op(dve, lambda e: e.reciprocal(out=smx[:, 3:4], in_=smx[:, 2:3]), reads=[b_smx], writes=[b_smx])
                        T.op(dve, lambda e: e.tensor_scalar(out=pn[:], in0=pnf[:], scalar1=smx[:, 3:4], scalar2=None, op0=ALU.mult),
                             reads=[b_pnf, b_smx], writes=[b_pn])
                        for mt in range(2):
                            tp_, tpb = T.ps()
                            tpb16 = tp_[:].bitcast(BF16)
                            T.mm([lambda e, mt=mt: e.transpose(tpb16[:, 0:64], pn[0:64, mt * 128:(mt + 1) * 128], ident_b[0:64, 0:64])],
                                 reads=[b_pn, b_c], writes=[tpb])
                            T.op(dve, lambda e, mt=mt: e.tensor_copy(pTs[:, mt].rearrange("q b h -> q (b h)"), tpb16[:, 0:64]),
                                 reads=[tpb], writes=[b_pTs])
                        T.op(dve, lambda e: e.memset(pmask[:], 0.0), writes=[b_pmask])
                        for b in range(NSMP):
                            T.op(dve, lambda e, b=b: e.tensor_copy(pmask[:, :, :, b, b], pTs[:, :, b, :]), reads=[b_pTs], writes=[b_pmask])
                        chk("xa4_%d_%d" % (blk, l))
                        T.ps_range = (4, 8)
                        for b in range(NSMP):
                            vb, b_vb = vbuf.next()
                            for mt in range(2):
                                vf, b_vf = vfr.next()
                                T.dma(sp, vf[:], cv[l, b, mt * 128:(mt + 1) * 128, :], b_vf, writes=[b_vf])
                                T.op(act, lambda e, mt=mt: e.copy(vb[:, mt, :], vf[:]), reads=[b_vf], writes=[b_vb])
                            for h in range(4):
                                T.mm([lambda e, h=h, mt=mt, b=b: e.matmul(T.psl[h][0][0:NSMP, 0:256], pmask[:, mt, h, b, :], vb[:, mt, h * 256:(h + 1) * 256],
                                                                          start=(b == 0 and mt == 0), stop=(b == NSMP - 1 and mt == 1))
                                      for mt in range(2)], reads=[b_pmask, b_vb], writes=[T.psl[h][1]])
                        for h in range(4):
                            T.op(dve, lambda e, h=h: e.tensor_copy(cb[:, h * 256:(h + 1) * 256], T.psl[h][0][0:NSMP, 0:256]),
                                 reads=[T.psl[h][1]], writes=[b_cb])
                        T.ps_range = (0, 8)

                        def ev_c(v, pb):
                            T.op(dve, lambda e: e.tensor_tensor(out=brT[:, :, TB:NT], in0=v, in1=sxgs[:], op=ALU.mult),
                                 reads=[pb, b_sxgs], writes=[brb[4]])
                        transpose8(cb, b_cb, NSMP, None, None, evac=ev_c)
                    chk("xattn_%d_%d" % (blk, l))
                    merge(2)

                    for j in range(2):
                        wt, wbuf = wget(("out", l, j))
                        for (i, p, c0) in tiles:
                            pst, pb = amode(mgT, [mgb[i]], c0, p, wt, wbuf)
                            xa = xtile(i)[:, j * 512:(j + 1) * 512] if i < 4 else xS[:, j * 512:(j + 1) * 512]
                            T.op(dve, lambda e: e.tensor_tensor(out=xa, in0=xa, in1=pst[0:p, 0:512], op=ALU.add), reads=[pb, xb[i]], writes=[xb[i]])
                    T.mark_fence()
                chk("layer_%d_%d" % (blk, l))

            phf = ExitStack()
            ytr = Rot(T, "yt", 2, [128, D], F32, phf)
            fgbc, b_fgbc = T.sb("fgbc", [128, D], F32, phf)
            T.dma(sp, fgbc[:], final_gain.partition_broadcast(128), b_fgbc, writes=[b_fgbc])
            sts = [rms_stats(xtile(i), xb[i], p, D, ssr.next()) for (i, p, c0) in tiles]
            for k_, (i, p, c0) in enumerate(tiles):
                rstd, b_ss = sts[k_]
                yt, b_yt = ytr.next()
                T.op(dve, lambda e: e.scalar_tensor_tensor(out=yt[0:p, :], in0=xtile(i), scalar=rstd, in1=fgbc[0:p, :], op0=ALU.mult, op1=ALU.mult),
                     reads=[xb[i], b_ss, b_fgbc], writes=[b_yt])
                if i < 4:
                    T.dma(sp, yp[blk * TB + i * 128: blk * TB + (i + 1) * 128, :], yt[:], b_yt, reads=[b_yt])
                    if blk + 1 < NBLK:
                        T.dma(sp, xP[:, i, :], xp[(blk + 1) * TB + i * 128: (blk + 1) * TB + (i + 1) * 128, :], xb[i], writes=[xb[i]])
                else:
                    T.dma(sp, ys[:, :], yt[0:NSMP, :], b_yt, reads=[b_yt])
            T.mark_fence()
            phf.close()
        T.finish()


_CACHE = {}


def _build():
    if "nc" not in _CACHE:
        nc0 = bass.Bass("TRN2", target_bir_lowering=False)
        seq = program(nc0, None)
        nc = bass.Bass("TRN2", target_bir_lowering=False)
        program(nc, seq)
        _CACHE["nc"] = nc
    return _CACHE["nc"]


def kernel(x_prompt, x_sample, mem_prompt, cache_mem_k, cache_mem_v, state_gla, state_pool,
           w_in, w_a2, b_a, gla_gain, pool_w, pool_scale, w_mk, w_mv, w_branch, w_out,
           norm_gain, final_gain):
    f = lambda a: np.ascontiguousarray(np.asarray(a, dtype=np.float32))
    x_prompt, x_sample, mem_prompt = f(x_prompt), f(x_sample), f(mem_prompt)
    cache_mem_k, cache_mem_v, state_gla, state_pool = f(cache_mem_k), f(cache_mem_v), f(state_gla), f(state_pool)
    shared = dict(w_in=f(w_in), w_a2=f(w_a2), b_a=f(b_a), gla_gain=f(gla_gain), pool_w=f(pool_w), pool_scale=f(pool_scale),
                  w_mk=f(w_mk), w_mv=f(w_mv), w_branch=f(w_branch), w_out=f(w_out), norm_gain=f(norm_gain), final_gain=f(final_gain))
    nc = _build()
    in_maps = []
    for c in range(8):
        s = slice(c * NSMP, (c + 1) * NSMP)
        m = dict(shared)
        m["xp"] = x_prompt[c]
        m["xs"] = np.ascontiguousarray(x_sample[s, 0, :])
        m["mem"] = mem_prompt[c]
        m["ck"] = np.ascontiguousarray(cache_mem_k[:, s].reshape(2, NSMP, 256, D))
        m["cv"] = np.ascontiguousarray(cache_mem_v[:, s].reshape(2, NSMP, 256, D))
        m["sg"] = np.ascontiguousarray(state_gla[:, s])
        m["spool"] = np.ascontiguousarray(state_pool[:, s])
        in_maps.append(m)
    res = run_bass_kernel_spmd(nc, in_maps, core_ids=list(range(8)))
    R = res.results
    y_prompt = np.stack([R[c]["yp"] for c in range(8)], axis=0)
    y_sample = np.concatenate([R[c]["ys"] for c in range(8)], axis=0).reshape(128, 1, D)
    new_mk = np.stack([R[c]["omk"] for c in range(8)], axis=1).reshape(2, 8, 256, 4, 256)
    new_mv = np.stack([R[c]["omv"] for c in range(8)], axis=1).reshape(2, 8, 256, 4, 256)
    new_glap = np.stack([R[c]["oglap"] for c in range(8)], axis=1)
    new_poolp = np.stack([R[c]["opoolp"] for c in range(8)], axis=1)
    new_glas = np.concatenate([R[c]["oglas"] for c in range(8)], axis=1)
    new_pools = np.concatenate([R[c]["opools"] for c in range(8)], axis=1)
    out = (y_prompt, y_sample, new_mk, new_mv, new_glap, new_poolp, new_glas, new_pools)
    return tuple(np.ascontiguousarray(o, dtype=np.float32) for o in out)
```

```python
import numpy as np
from contextlib import ExitStack
import concourse.bass as bass
import concourse.mybir as mybir
from concourse.bass_utils import run_bass_kernel_spmd

F32 = mybir.dt.float32
BF16 = mybir.dt.bfloat16
AF = mybir.ActivationFunctionType
ALU = mybir.AluOpType
AX = mybir.AxisListType

D = 1024
NIN = 10256
TB = 512
NBLK = 4
NSMP = 16
NT = TB + NSMP
Q0, K0, V0, GG0, A0, U0, PG0, XQ0, XG0, MG0 = 0, 512, 1024, 2048, 3072, 3088, 4112, 5136, 6160, 7184
EPS = 1e-6
NSLOT = 4
PREF = 3


class Buf:
    __slots__ = ("name", "w", "r", "excl")

    def __init__(self, name, excl=False):
        self.name = name
        self.w = None
        self.r = {}
        self.excl = excl


class Eng:
    def __init__(self, T, e, name):
        self.T, self.e, self.name = T, e, name
        self.sem = T.newsem("e_" + name)
        self.n = 0
        self.seen = {}

    def wait(self, ev):
        sem, val = ev
        k = id(sem)
        if self.seen.get(k, 0) >= val:
            return
        self.seen[k] = val
        if not self.T.dry:
            self.e.wait_ge(sem, val)


class Tracker:
    def __init__(self, nc, es, dry):
        self.nc, self.es, self.dry = nc, es, dry
        self.pe = Eng(self, nc.tensor, "pe")
        self.act = Eng(self, nc.scalar, "act")
        self.dve = Eng(self, nc.vector, "dve")
        self.pool = Eng(self, nc.gpsimd, "pool")
        self.sp = Eng(self, nc.sync, "sp")
        self.dsems = {}
        self.psl = []
        self.psi = 0
        self.ps_range = (0, 8)
        self.stopped = False
        self.fence = {}
        self.store_sems = {}

    def newsem(self, name):
        if self.dry:
            return object()
        return self.es.enter_context(self.nc.semaphore(name))

    def sb(self, name, shape, dt, stack=None):
        self.uid = getattr(self, "uid", 0) + 1
        nm = "%s_%d" % (name, self.uid)
        t = (stack or self.es).enter_context(self.nc.sbuf_tensor(nm, shape, dt))
        b = Buf(nm)
        if stack is not None:
            b.r = dict(self.fence)
        return t, b

    def init_psum(self):
        for i in range(8):
            t = self.es.enter_context(self.nc.psum_tensor("ps%d" % i, [128, 512], F32))
            self.psl.append((t, Buf("ps%d" % i, excl=True)))

    def ps(self):
        lo, hi = self.ps_range
        r = self.psl[lo + self.psi % (hi - lo)]
        self.psi += 1
        return r

    def _deps(self, eng, reads, writes):
        for b in reads:
            if b.w is not None:
                eng.wait(b.w)
            if b.excl:
                me = id(eng.sem)
                for k, ev in b.r.items():
                    if k != me:
                        eng.wait(ev)
        for b in writes:
            if b.w is not None:
                eng.wait(b.w)
            for ev in b.r.values():
                eng.wait(ev)

    def _mark(self, ev, reads, writes):
        k = id(ev[0])
        for b in reads:
            b.r[k] = ev
        for b in writes:
            b.w = ev
            b.r = {}

    def op(self, eng, fn, reads=(), writes=()):
        if self.stopped:
            return
        self._deps(eng, reads, writes)
        eng.n += 1
        if not self.dry:
            fn(eng.e).then_inc(eng.sem, 1)
        self._mark((eng.sem, eng.n), reads, writes)

    def mm(self, fns, reads=(), writes=()):
        if self.stopped:
            return
        eng = self.pe
        self._deps(eng, reads, writes)
        eng.n += 1
        if not self.dry:
            ins = None
            for fn in fns:
                ins = fn(eng.e)
            ins.then_inc(eng.sem, 1)
        self._mark((eng.sem, eng.n), reads, writes)

    def dma(self, eng, out, in_, key, reads=(), writes=()):
        if self.stopped:
            return
        self._deps(eng, reads, writes)
        if id(key) not in self.dsems:
            self.dsems[id(key)] = [self.newsem("d_" + key.name), 0]
        st = self.dsems[id(key)]
        st[1] += 16
        if len(reads) > 0:
            self.store_sems[id(st[0])] = st
        if not self.dry:
            eng.e.dma_start(out=out, in_=in_).then_inc(st[0], 16)
        self._mark((st[0], st[1]), reads, writes)

    def barrier(self, with_pool=False):
        if self.stopped:
            return
        engs = [self.pe, self.act, self.dve, self.sp]
        allv = engs + [self.pool]
        for e in (allv if with_pool else engs):
            for o in allv:
                if o.n > 0:
                    e.wait((o.sem, o.n))
            for st in self.dsems.values():
                e.wait((st[0], st[1]))

    def mark_fence(self):
        if self.stopped:
            return
        f = {}
        for o in [self.pe, self.act, self.dve, self.sp, self.pool]:
            if o.n > 0:
                f[id(o.sem)] = (o.sem, o.n)
        for st in self.store_sems.values():
            f[id(st[0])] = (st[0], st[1])
        self.fence = f

    def finish(self):
        self.barrier()


class Rot:
    def __init__(self, T, name, n, shape, dt, stack=None):
        self.items = [T.sb("%s%d" % (name, i), shape, dt, stack) for i in range(n)]
        self.i = 0

    def next(self):
        r = self.items[self.i % len(self.items)]
        self.i += 1
        return r


class _Stop(Exception):
    pass


STOP = None


def program(nc, wseq):
    dry = wseq is None
    rec = []
    es = ExitStack()
    with es:
      T = None
      try:
        _program(nc, wseq, dry, rec, es)
      except _Stop:
        pass
    return rec


def _program(nc, wseq, dry, rec, es):
    if True:
        es.enter_context(nc.allow_non_contiguous_dma(reason="small strided parameter loads"))
        es.enter_context(nc.allow_low_precision(reason="bf16 matmul operands, fp32 accumulate"))
        T = Tracker(nc, es, dry)
        pe, act, dve, pool, sp = T.pe, T.act, T.dve, T.pool, T.sp

        def chk(name):
            if STOP == name and not T.stopped:
                T.finish()
                T.stopped = True

        def din(name, shape):
            return nc.dram_tensor(name, shape, F32, kind="ExternalInput").ap()

        def dout(name, shape):
            return nc.dram_tensor(name, shape, F32, kind="ExternalOutput").ap()

        xp = din("xp", [2048, D]); xs = din("xs", [NSMP, D]); mem = din("mem", [256, D])
        ck = din("ck", [2, NSMP, 256, D]); cv = din("cv", [2, NSMP, 256, D])
        sg = din("sg", [2, NSMP, 4, 128, 256]); spool = din("spool", [2, NSMP, 15, D])
        w_in = din("w_in", [2, D, NIN]); w_a2 = din("w_a2", [2, 16, 512]); b_a = din("b_a", [2, 512])
        gla_gain = din("gla_gain", [2, D]); pool_w = din("pool_w", [2, 4, 256, 256])
        pool_scale = din("pool_scale", [2, D]); w_mk = din("w_mk", [2, D, D]); w_mv = din("w_mv", [2, D, D])
        w_branch = din("w_branch", [2, 3, D, D]); w_out = din("w_out", [2, D, D])
        norm_gain = din("norm_gain", [2, D]); final_gain = din("final_gain", [D])
        yp = dout("yp", [2048, D]); ys = dout("ys", [NSMP, D]); omk = dout("omk", [2, 256, D]); omv = dout("omv", [2, 256, D])
        oglap = dout("oglap", [2, 4, 128, 256]); opoolp = dout("opoolp", [2, 15, D])
        oglas = dout("oglas", [2, NSMP, 4, 128, 256]); opools = dout("opools", [2, NSMP, 15, D])

        T.init_psum()

        xP, _ = T.sb("xP", [128, 4, D], F32)
        xb = [Buf("xb%d" % i) for i in range(5)]
        xS, _ = T.sb("xS", [NSMP, D], F32)
        hT, _ = T.sb("hT", [128, 8, NT], BF16); hb = [Buf("hb%d" % i) for i in range(5)]
        brT, _ = T.sb("brT", [128, 8, NT], BF16); brb = [Buf("brb%d" % i) for i in range(5)]
        mgT, _ = T.sb("mgT", [128, 8, NT], BF16); mgb = [Buf("mgb%d" % i) for i in range(5)]
        gblk, b_gblk = T.sb("gblk", [128, 4, NT], BF16)
        slots = [T.sb("wslot%d" % i, [128, 8, 512], BF16) for i in range(NSLOT)]
        S, b_S = T.sb("S", [128, 2, D], F32)
        Sbf, b_Sbf = T.sb("Sbf", [128, D], BF16)
        mkT, b_mkT = T.sb("mkT", [128, 2, 8, 256], BF16)
        mvb, b_mvb = T.sb("mvb", [128, 2, 2, D], BF16)
        memT, b_memT = T.sb("memT", [128, 8, 256], BF16)
        ones_f, b_c = T.sb("ones_f", [128, 128], F32)
        triU_f, _ = T.sb("triU_f", [128, 128], F32)
        triLs_f, _ = T.sb("triLs_f", [128, 128], F32)
        ident_f, _ = T.sb("ident_f", [128, 128], F32)
        ident_b, _ = T.sb("ident_b", [128, 128], BF16)
        ones_b, _ = T.sb("ones_b", [128, 128], BF16)
        triU4_b, _ = T.sb("triU4_b", [128, 512], BF16)
        rc, _ = T.sb("rc", [128, 4, 16], F32)
        a_aug, b_aaug = T.sb("a_aug", [17, NT], F32)
        w_a2aug, b_wa2 = T.sb("w_a2aug", [17, 2, 512], F32)
        wa, b_wa = T.sb("wa", [128, 2, 8, 16], BF16)
        pw, b_pw = T.sb("pw", [128, 2, 4, 2, 256], BF16)
        gainT, b_gainT = T.sb("gainT", [128, 2, 8], F32)
        pscT, b_pscT = T.sb("pscT", [128, 2, 8], F32)
        uhist, b_uhist = T.sb("uhist", [128, 2, 8, 15], F32)
        junk, b_junk = T.sb("junk", [128, D], BF16)
        ssr = Rot(T, "ss", 6, [128, 16], F32)
        b_dd = Buf("dram2dram")

        class WS:
            pos = 0
            issued = 0

        def wsrc(key):
            kind = key[0]
            if kind == "in":
                src = w_in[key[1], :, key[2]:key[2] + 512]
            elif kind == "br":
                src = w_branch[key[1], key[2], :, key[3] * 512:(key[3] + 1) * 512]
            elif kind == "out":
                src = w_out[key[1], :, key[2] * 512:(key[2] + 1) * 512]
            elif kind == "mk":
                src = w_mk[key[1], :, key[2] * 512:(key[2] + 1) * 512]
            else:
                src = w_mv[key[1], :, key[2] * 512:(key[2] + 1) * 512]
            return src.rearrange("(k p) n -> p k n", p=128)

        def wget(key):
            if T.stopped:
                return slots[0]
            i = WS.pos
            WS.pos += 1
            if dry:
                rec.append(key)
                return slots[0]
            assert wseq[i] == key, (i, wseq[i], key)
            while WS.issued < len(wseq) and WS.issued <= i + PREF:
                j = WS.issued
                st, sbuf_ = slots[j % NSLOT]
                T.dma(pool, st[:], wsrc(wseq[j]), sbuf_, writes=[sbuf_])
                WS.issued += 1
            return slots[i % NSLOT]

        T.op(dve, lambda e: e.memset(ones_f[:], 1.0), writes=[b_c])
        T.op(dve, lambda e: e.memset(ones_b[:], 1.0), writes=[b_c])
        T.op(pool, lambda e: e.affine_select(out=triU_f[:], in_=ones_f[:], pattern=[[1, 128]], compare_op=ALU.is_ge,
                                             fill=0.0, base=0, channel_multiplier=-1), reads=[b_c], writes=[b_c])
        T.op(pool, lambda e: e.affine_select(out=triLs_f[:], in_=ones_f[:], pattern=[[-1, 128]], compare_op=ALU.is_gt,
                                             fill=0.0, base=0, channel_multiplier=1), reads=[b_c], writes=[b_c])
        T.op(pool, lambda e: e.affine_select(out=ident_f[:], in_=ones_f[:], pattern=[[1, 128]], compare_op=ALU.is_equal,
                                             fill=0.0, base=0, channel_multiplier=-1), reads=[b_c], writes=[b_c])
        T.op(pool, lambda e: e.affine_select(out=ident_b[:], in_=ones_f[:], pattern=[[1, 128]], compare_op=ALU.is_equal,
                                             fill=0.0, base=0, channel_multiplier=-1), reads=[b_c], writes=[b_c])
        for h in range(4):
            T.op(pool, lambda e, h=h: e.affine_select(out=triU4_b[:, h * 128:(h + 1) * 128], in_=ones_f[:], pattern=[[1, 128]],
                                                      compare_op=ALU.is_ge, fill=0.0, base=0, channel_multiplier=-1),
                 reads=[b_c], writes=[b_c])
        for g in range(4):
            w = 2 << g
            T.op(dve, lambda e, g=g, w=w: e.memset(rc[:, g, :], 1.0 / w), writes=[b_c])
            for t in range(w - 1):
                T.op(dve, lambda e, g=g, t=t: e.memset(rc[:, g, t:t + 1], 1.0 / (t + 1)), writes=[b_c])
        T.op(dve, lambda e: e.memset(a_aug[:], 1.0), writes=[b_aaug])
        T.op(dve, lambda e: e.memset(S[:], 0.0), writes=[b_S])
        T.op(dve, lambda e: e.memset(uhist[:], 0.0), writes=[b_uhist])
        for l in range(2):
            T.dma(sp, w_a2aug[0:16, l, :], w_a2[l], b_wa2, writes=[b_wa2])
            T.dma(sp, w_a2aug[16:17, l, :], b_a[l:l + 1, :], b_wa2, writes=[b_wa2])
            T.dma(pool, wa[:, l], w_in[l, :, A0:A0 + 16].rearrange("(k p) n -> p k n", p=128), b_wa, writes=[b_wa])
            for g in range(4):
                T.dma(pool, pw[:, l, g], pool_w[l, g].rearrange("(k p) e -> p k e", p=128), b_pw, writes=[b_pw])
            T.dma(sp, gainT[:, l, :], norm_gain[l].rearrange("(k p) -> p k", p=128), b_gainT, writes=[b_gainT])
            T.dma(sp, pscT[:, l, :], pool_scale[l].rearrange("(k p) -> p k", p=128), b_pscT, writes=[b_pscT])

        def tb(bufs, t0, tn):
            if t0 >= TB:
                return [bufs[4]]
            return bufs[t0 // 128:(t0 + tn + 127) // 128]

        def transpose8(src, b_src, p, dst3, b_dst, evac=None):
            pst, pb = T.ps()
            psb = pst[:].bitcast(BF16)
            T.mm([lambda e, j=j: e.transpose(psb[:, j * 128:j * 128 + p], src[0:p, j * 128:(j + 1) * 128], ident_b[0:p, 0:p])
                  for j in range(8)], reads=[b_src, b_c], writes=[pb])
            v = psb.rearrange("q (j t) -> q j t", t=128)[:, :, 0:p]
            if evac is None:
                T.op(act, lambda e: e.copy(dst3, v), reads=[pb], writes=[b_dst])
            else:
                evac(v, pb)

        def rms_stats(xap, b_x, p, n, ssl):
            ss, b_ss = ssl
            T.op(act, lambda e: e.activation(out=junk[0:p, 0:n], in_=xap, func=AF.Square, accum_out=ss[0:p, 0:1]),
                 reads=[b_x], writes=[b_junk, b_ss])
            T.op(act, lambda e: e.activation(out=ss[0:p, 1:2], in_=ss[0:p, 0:1], func=AF.Sqrt, scale=1.0 / n, bias=EPS),
                 reads=[b_ss], writes=[b_ss])
            T.op(dve, lambda e: e.reciprocal(out=ss[0:p, 2:3], in_=ss[0:p, 1:2]), reads=[b_ss], writes=[b_ss])
            return ss[0:p, 2:3], b_ss

        def bmode(wt, wbuf, ms, subs, rhs3, rbufs, evac, nk=8, lhs_fn=None):
            for (mi, c0, mc) in ms:
                for (t0, tn) in subs:
                    pst, pb = T.ps()
                    T.mm([lambda e, kc=kc: e.matmul(pst[0:mc, 0:tn],
                                                    (lhs_fn(kc, c0, mc) if lhs_fn else wt[:, kc, c0:c0 + mc]),
                                                    rhs3[:, kc, t0:t0 + tn], start=(kc == 0), stop=(kc == nk - 1))
                          for kc in range(nk)], reads=[wbuf] + tb(rbufs, t0, tn), writes=[pb])
                    evac(mi, t0, tn, pst[0:mc, 0:tn], pb)

        def amode(lhs3, lbufs, c0, p, wt, wbuf, ncols=512):
            pst, pb = T.ps()
            T.mm([lambda e, kc=kc: e.matmul(pst[0:p, 0:ncols], lhs3[:, kc, c0:c0 + p], wt[:, kc, 0:ncols],
                                            start=(kc == 0), stop=(kc == 7)) for kc in range(8)],
                 reads=[wbuf] + lbufs, writes=[pb])
            return pst, pb

        M4 = [(m, m * 128, 128) for m in range(4)]

        with ExitStack() as ph:
            memf, b_memf = T.sb("memf", [128, 2, D], F32, ph)
            memb, b_memb = T.sb("memb", [128, 2, D], BF16, ph)
            T.dma(sp, memf[:], mem.rearrange("(t p) d -> p t d", p=128), b_memf, writes=[b_memf])
            T.op(dve, lambda e: e.tensor_copy(memb[:], memf[:]), reads=[b_memf], writes=[b_memb])
            for mt in range(2):
                transpose8(memb[:, mt, :], b_memb, 128, memT[:, :, mt * 128:(mt + 1) * 128], b_memT)
            T.mark_fence()

        chk("setup")
        for blk in range(NBLK):
            has_s = blk == 0
            tiles = [(i, 128, i * 128) for i in range(4)] + ([(4, NSMP, TB)] if has_s else [])
            subs = [(0, TB)] + ([(TB, NSMP)] if has_s else [])
            psubs = [(0, TB)]

            def xtile(i):
                return (xP[:, i, :] if i < 4 else xS[:, :])

            if blk == 0:
                for i in range(4):
                    T.dma(sp, xP[:, i, :], xp[blk * TB + i * 128: blk * TB + (i + 1) * 128, :], xb[i], writes=[xb[i]])
            if has_s:
                T.dma(sp, xS[:, :], xs[:, :], xb[4], writes=[xb[4]])

            for l in range(2):
                ph0 = ExitStack()
                xnr = Rot(T, "xn", len(tiles), [128, D], BF16, ph0)
                gbc, b_gbc = T.sb("gbc", [128, D], F32, ph0)
                T.dma(sp, gbc[:], norm_gain[l].partition_broadcast(128), b_gbc, writes=[b_gbc])
                sts = [rms_stats(xtile(i), xb[i], p, D, ssr.next()) for (i, p, c0) in tiles]
                xns = []
                for k_, (i, p, c0) in enumerate(tiles):
                    rstd, b_ss = sts[k_]
                    xn, b_xn = xnr.next()
                    T.op(dve, lambda e: e.scalar_tensor_tensor(out=xn[0:p, :], in0=xtile(i), scalar=rstd, in1=gbc[0:p, :],
                                                               op0=ALU.mult, op1=ALU.mult),
                         reads=[xb[i], b_ss, b_gbc], writes=[b_xn])
                    xns.append((xn, b_xn))
                for k_, (i, p, c0) in enumerate(tiles):
                    xn, b_xn = xns[k_]
                    transpose8(xn, b_xn, p, hT[:, :, c0:c0 + p], hb[i])
                T.mark_fence()
                ph0.close()
                chk("prenorm_%d_%d" % (blk, l))

                def merge_steps(bi, subs_):
                    steps = []
                    for j in range(2):
                        st_ = {}

                        def ev_g(mi, t0, tn, ps_, pb):
                            T.op(act, lambda e: e.activation(out=gblk[:, mi, t0:t0 + tn], in_=ps_, func=AF.Sigmoid), reads=[pb], writes=[b_gblk])

                        def mk_ev_m(j):
                            def ev_m(mi, t0, tn, ps_, pb):
                                dst = mgT[:, 4 * j + mi, t0:t0 + tn]
                                mb = tb(mgb, t0, tn)
                                if bi == 0:
                                    T.op(dve, lambda e: e.tensor_tensor(out=dst, in0=ps_, in1=gblk[:, mi, t0:t0 + tn], op=ALU.mult),
                                         reads=[pb, b_gblk], writes=mb)
                                else:
                                    tm, b_tm = mtmp.next()
                                    T.op(dve, lambda e: e.tensor_tensor(out=tm[:, 0:tn], in0=ps_, in1=gblk[:, mi, t0:t0 + tn], op=ALU.mult),
                                         reads=[pb, b_gblk], writes=[b_tm])
                                    T.op(dve, lambda e: e.tensor_tensor(out=dst, in0=dst, in1=tm[:, 0:tn], op=ALU.add),
                                         reads=[b_tm] + mb, writes=mb)
                            return ev_m

                        def gstep(mi, j=j, st_=st_, ev_g=ev_g):
                            if "g" not in st_:
                                st_["g"] = wget(("in", l, MG0 + bi * D + j * 512))
                            bmode(st_["g"][0], st_["g"][1], [M4[mi]], subs_, hT, hb, ev_g)

                        def bstep(mi, j=j, st_=st_, mk_ev_m=mk_ev_m):
                            if "b" not in st_:
                                st_["b"] = wget(("br", l, bi, j))
                            bmode(st_["b"][0], st_["b"][1], [M4[mi]], subs_, brT, brb, mk_ev_m(j))
                        for mi in range(4):
                            steps.append(lambda mi=mi, f=gstep: f(mi))
                        for mi in range(4):
                            steps.append(lambda mi=mi, f=bstep: f(mi))
                    return steps

                def merge(bi, subs_=None):
                    for stp in merge_steps(bi, subs if subs_ is None else subs_):
                        stp()

                with ExitStack() as ph:
                    ks_f, b_ksf = T.sb("ks_f", [NSMP, 512], BF16, ph)
                    vs_f, b_vsf = T.sb("vs_f", [NSMP, D], BF16, ph)
                    Xs, b_Xs = T.sb("Xs", [NSMP, D], BF16, ph)
                    qTs_f, b_qTs = T.sb("qTs_f", [128, 4, NSMP], F32, ph)
                    brr = Rot(T, "br", 2, [128, D], BF16, ph)
                    ph2 = ExitStack()
                    qT, b_qT = T.sb("qT", [128, 4, NT], BF16, ph2)
                    kT, b_kT = T.sb("kT", [128, 4, NT], BF16, ph2)
                    ktok, b_ktok = T.sb("ktok", [128, 4, 512], BF16, ph2)
                    vtok, b_vtok = T.sb("vtok", [128, 4, D], BF16, ph2)
                    Xtok, b_Xtok = T.sb("Xtok", [128, 4, D], BF16, ph2)
                    tmpF = Rot(T, "tmpF", 4, [128, 512], F32, ph2)
                    splr = Rot(T, "spl", 4, [128, 512], F32, ph2)
                    tmpB = Rot(T, "tmpB", 12, [128, 512], BF16, ph2)
                    dec, b_dec = T.sb("dec", [128, 4, 4], F32, ph2)
                    osbr = Rot(T, "osb", 2, [128, D], BF16, ph2)
                    ggl, b_ggl = T.sb("ggl", [128, D], F32, ph2)
                    T.dma(sp, ggl[:], gla_gain[l].partition_broadcast(128), b_ggl, writes=[b_ggl])

                    def ev_a(mi, t0, tn, ps_, pb):
                        T.op(act, lambda e: e.copy(a_aug[0:16, t0:t0 + tn], ps_), reads=[pb], writes=[b_aaug])
                    bmode(wa[:, l], b_wa, [(0, 0, 16)], subs, hT, hb, ev_a)
                    wt, wbuf = wget(("in", l, Q0))

                    def ev_q(mi, t0, tn, ps_, pb):
                        T.op(act, lambda e: e.activation(out=qT[:, mi, t0:t0 + tn], in_=ps_, func=AF.Copy, scale=128.0 ** -0.5),
                             reads=[pb], writes=[b_qT])
                        if t0 >= TB:
                            T.op(dve, lambda e: e.tensor_scalar(out=qTs_f[:, mi, :], in0=ps_, scalar1=128.0 ** -0.5, scalar2=None,
                                                                op0=ALU.mult), reads=[pb], writes=[b_qTs])
                    bmode(wt, wbuf, M4, subs, hT, hb, ev_q)
                    wt, wbuf = wget(("in", l, K0))

                    def ev_k(mi, t0, tn, ps_, pb):
                        T.op(dve, lambda e: e.tensor_copy(kT[:, mi, t0:t0 + tn], ps_), reads=[pb], writes=[b_kT])
                    bmode(wt, wbuf, M4, psubs, hT, hb, ev_k)
                    for (i, p, c0) in tiles:
                        pst, pb = amode(hT, [hb[i]], c0, p, wt, wbuf)
                        if i < 4:
                            T.op(act, lambda e: e.copy(ktok[:, i, :], pst[:, 0:512]), reads=[pb], writes=[b_ktok])
                        else:
                            T.op(act, lambda e: e.copy(ks_f[:, :], pst[0:p, 0:512]), reads=[pb], writes=[b_ksf])
                    for j in range(2):
                        wt, wbuf = wget(("in", l, V0 + j * 512))
                        for (i, p, c0) in tiles:
                            pst, pb = amode(hT, [hb[i]], c0, p, wt, wbuf)
                            if i < 4:
                                T.op(act, lambda e: e.copy(vtok[:, i, j * 512:(j + 1) * 512], pst[:, 0:512]), reads=[pb], writes=[b_vtok])
                            else:
                                T.op(act, lambda e: e.copy(vs_f[:, j * 512:(j + 1) * 512], pst[0:p, 0:512]), reads=[pb], writes=[b_vsf])
                    for j in range(2):
                        wt, wbuf = wget(("in", l, GG0 + j * 512))
                        for (i, p, c0) in tiles:
                            pst, pb = amode(hT, [hb[i]], c0, p, wt, wbuf)
                            tf, b_tf = tmpF.next()
                            T.op(act, lambda e: e.activation(out=tf[0:p, :], in_=pst[0:p, 0:512], func=AF.Silu), reads=[pb], writes=[b_tf])
                            dst, b_dst = (Xtok[:, i, j * 512:(j + 1) * 512], b_Xtok) if i < 4 else (Xs[:, j * 512:(j + 1) * 512], b_Xs)
                            T.op(dve, lambda e: e.tensor_tensor(out=dst, in0=tf[0:p, :], in1=ggl[0:p, j * 512:(j + 1) * 512], op=ALU.mult),
                                 reads=[b_tf, b_ggl], writes=[b_dst])

                    T.op(act, lambda e: e.copy(Sbf[:], S[:, l, :]), reads=[b_S], writes=[b_Sbf])
                    chk("glaproj_%d_%d" % (blk, l))

                    def post(osrc, b_os, p, Xap, b_X, c0, i, split=False):
                        ss, b_ss = ssr.next()
                        for h in range(4):
                            T.op(act, lambda e, h=h: e.activation(out=junk[0:p, 0:256], in_=osrc(h), func=AF.Square,
                                                                  accum_out=ss[0:p, h:h + 1]), reads=b_os, writes=[b_junk, b_ss])
                        T.op(act, lambda e: e.activation(out=ss[0:p, 4:8], in_=ss[0:p, 0:4], func=AF.Sqrt, scale=1.0 / 256, bias=EPS),
                             reads=[b_ss], writes=[b_ss])
                        T.op(dve, lambda e: e.reciprocal(out=ss[0:p, 8:12], in_=ss[0:p, 4:8]), reads=[b_ss], writes=[b_ss])
                        br, b_br = brr.next()
                        for h in range(4):
                            T.op(dve, lambda e, h=h: e.scalar_tensor_tensor(out=br[0:p, h * 256:(h + 1) * 256], in0=osrc(h),
                                                                            scalar=ss[0:p, 8 + h:9 + h], in1=Xap[:, h * 256:(h + 1) * 256],
                                                                            op0=ALU.mult, op1=ALU.mult),
                                 reads=b_os + [b_ss, b_X], writes=[b_br])
                        if split:
                            return (lambda: transpose8(br, b_br, p, brT[:, :, c0:c0 + p], brb[i]))
                        transpose8(br, b_br, p, brT[:, :, c0:c0 + p], brb[i])

                    zps = []
                    for i in range(4):
                        tok = slice(i * 128, (i + 1) * 128)
                        zp, zb = T.ps()
                        T.mm([lambda e: e.matmul(zp[:, 0:512], a_aug[0:17, tok], w_a2aug[0:17, l, :], start=True, stop=True)],
                             reads=[b_aaug, b_wa2], writes=[zb])
                        zps.append((zp, zb))
                    spls = []
                    for i in range(4):
                        zp, zb = zps[i]
                        spl, b_spl = splr.next()
                        T.op(act, lambda e: e.activation(out=spl[:], in_=zp[:, 0:512], func=AF.Exp, scale=-1.0), reads=[zb], writes=[b_spl])
                        T.op(act, lambda e: e.activation(out=spl[:], in_=spl[:], func=AF.Ln, bias=1.0, scale=1.0), reads=[b_spl], writes=[b_spl])
                        spls.append((spl, b_spl))
                    cums = []
                    for i in range(4):
                        spl, b_spl = spls[i]
                        bp, bb = T.ps()
                        T.mm([lambda e, h=h: e.matmul(bp[:, h * 128:(h + 1) * 128], spl[:, h * 128:(h + 1) * 128], triU_f[:], start=True, stop=True)
                              for h in range(4)], reads=[b_spl, b_c], writes=[bb])
                        rp, rb = T.ps()
                        T.mm([lambda e: e.matmul(rp[:, 0:512], triLs_f[:], spl[:], start=True, stop=True)], reads=[b_spl, b_c], writes=[rb])
                        cums.append((bp, bb, rp, rb))
                    qk = []
                    for i in range(4):
                        tok = slice(i * 128, (i + 1) * 128)
                        bp, bb, rp, rb = cums[i]
                        expb, b_expb = tmpF.next()
                        T.op(act, lambda e: e.activation(out=expb[:], in_=bp[:, 0:512], func=AF.Exp, scale=-1.0 / 16), reads=[bb], writes=[b_expb])
                        expnb, b_expnb = tmpF.next()
                        T.op(act, lambda e: e.activation(out=expnb[:], in_=bp[:, 0:512], func=AF.Exp, scale=1.0 / 16), reads=[bb], writes=[b_expnb])
                        expr, b_expr = tmpF.next()
                        T.op(act, lambda e: e.activation(out=expr[:], in_=rp[:, 0:512], func=AF.Exp, scale=-1.0 / 16), reads=[rb], writes=[b_expr])
                        qdec, b_qdec = tmpB.next()
                        T.op(dve, lambda e: e.tensor_tensor(out=qdec[:].rearrange("q (h c) -> q h c", h=4), in0=qT[:, :, tok],
                                                            in1=expb[:].rearrange("q (h c) -> q h c", h=4), op=ALU.mult),
                             reads=[b_qT, b_expb], writes=[b_qdec])
                        T.op(dve, lambda e: e.tensor_copy(dec[:, i, :], expb[:].rearrange("q (h c) -> q h c", h=4)[:, :, 127]),
                             reads=[b_expb], writes=[b_dec])
                        kinv, b_kinv = tmpB.next()
                        T.op(dve, lambda e: e.tensor_tensor(out=kinv[:].rearrange("q (h c) -> q h c", h=4), in0=kT[:, :, tok],
                                                            in1=expnb[:].rearrange("q (h c) -> q h c", h=4), op=ALU.mult),
                             reads=[b_kT, b_expnb], writes=[b_kinv])
                        kend, b_kend = tmpB.next()
                        T.op(dve, lambda e: e.tensor_tensor(out=kend[:], in0=ktok[:, i, :], in1=expr[:], op=ALU.mult),
                             reads=[b_ktok, b_expr], writes=[b_kend])
                        qk.append((qdec, b_qdec, kinv, b_kinv, kend, b_kend))
                    atts = []
                    for i in range(4):
                        qdec, b_qdec, kinv, b_kinv, kend, b_kend = qk[i]
                        ap_, ab = T.ps()
                        T.mm([lambda e, h=h: e.matmul(ap_[:, h * 128:(h + 1) * 128], kinv[:, h * 128:(h + 1) * 128],
                                                      qdec[:, h * 128:(h + 1) * 128], start=True, stop=True) for h in range(4)],
                             reads=[b_kinv, b_qdec], writes=[ab])
                        atts.append((ap_, ab))
                    for i in range(4):
                        ap_, ab = atts[i]
                        attT, b_attT = qk[i][2], qk[i][3]
                        T.op(dve, lambda e: e.tensor_tensor(out=attT[:], in0=ap_[:, 0:512], in1=triU4_b[:], op=ALU.mult),
                             reads=[ab, b_c], writes=[b_attT])
                        atts[i] = (attT, b_attT)
                    pend = None
                    pend_b = None
                    T.ps_range = (4, 8)
                    for i in range(4):
                        qdec, b_qdec, kinv, b_kinv, kend, b_kend = qk[i]
                        attT, b_attT = atts[i]
                        ob = [T.psl[(i % 2) * 2], T.psl[(i % 2) * 2 + 1]]
                        for bk in range(2):
                            fns = []
                            for hh in range(2):
                                h = bk * 2 + hh
                                o_ap = ob[bk][0][:, hh * 256:(hh + 1) * 256]
                                fns.append(lambda e, h=h, o_ap=o_ap: e.matmul(o_ap, qdec[:, h * 128:(h + 1) * 128], Sbf[:, h * 256:(h + 1) * 256],
                                                                              start=True, stop=False))
                                fns.append(lambda e, h=h, o_ap=o_ap: e.matmul(o_ap, attT[:, h * 128:(h + 1) * 128], vtok[:, i, h * 256:(h + 1) * 256],
                                                                              start=False, stop=True))
                            T.mm(fns, reads=[b_qdec, b_Sbf, b_attT, b_vtok], writes=[ob[bk][1]])
                        db = [T.ps(), T.ps()]
                        for bk in range(2):
                            T.mm([lambda e, h=bk * 2 + hh, hh=hh: e.matmul(db[bk][0][:, hh * 256:(hh + 1) * 256], kend[:, h * 128:(h + 1) * 128],
                                                                           vtok[:, i, h * 256:(h + 1) * 256], start=True, stop=True)
                                  for hh in range(2)], reads=[b_kend, b_vtok], writes=[db[bk][1]])
                        osb_t, b_osb = osbr.next()
                        for bk in range(2):
                            T.op(act, lambda e, bk=bk: e.copy(osb_t[:, bk * 512:(bk + 1) * 512], ob[bk][0][:, 0:512]), reads=[ob[bk][1]], writes=[b_osb])
                        for h in range(4):
                            T.op(dve, lambda e, h=h: e.scalar_tensor_tensor(out=S[:, l, h * 256:(h + 1) * 256], in0=S[:, l, h * 256:(h + 1) * 256],
                                                                            scalar=dec[:, i, h:h + 1],
                                                                            in1=db[h // 2][0][:, (h % 2) * 256:(h % 2 + 1) * 256],
                                                                            op0=ALU.mult, op1=ALU.add),
                                 reads=[b_S, b_dec, db[h // 2][1]], writes=[b_S])
                        T.op(dve, lambda e: e.tensor_copy(Sbf[:], S[:, l, :]), reads=[b_S], writes=[b_Sbf])
                        if pend_b is not None:
                            pend_b()
                            pend_b = None
                        if pend is not None:
                            pend_b = pend()
                        pend = (lambda osb_t=osb_t, b_osb=b_osb, i=i: post(lambda h: osb_t[:, h * 256:(h + 1) * 256], [b_osb], 128,
                                                                           Xtok[:, i, :], b_Xtok, i * 128, i, split=True))
                    if pend_b is not None:
                        pend_b()
                    pend()()
                    T.ps_range = (0, 8)
                    if blk == NBLK - 1:
                        T.dma(sp, oglap[l].rearrange("h k v -> k h v"), S[:, l, :].rearrange("q (h v) -> q h v", h=4), b_S, reads=[b_S])

                    T.mark_fence()
                    ph2.close()
                    chk("glachunk_%d_%d" % (blk, l))
                    if has_s:
                        aTs, b_aTs = T.sb("aTs", [128, 64], F32, ph)
                        qmask, b_qmask = T.sb("qmask", [128, 4, NSMP, NSMP], BF16, ph)
                        s0r = Rot(T, "s0", 2, [128, D], F32, ph)
                        snr = Rot(T, "sn", 2, [128, D], F32, ph)
                        kmr = Rot(T, "km", 2, [NSMP, 512], BF16, ph)
                        snbr = Rot(T, "snb", 2, [128, D], BF16, ph)
                        zs, zsb = T.ps()
                        T.mm([lambda e, h=h: e.matmul(zs[:, h * 16:(h + 1) * 16], w_a2aug[0:17, l, h * 128:(h + 1) * 128], a_aug[0:17, TB:NT],
                                                      start=True, stop=True) for h in range(4)], reads=[b_aaug, b_wa2], writes=[zsb])
                        T.op(act, lambda e: e.activation(out=aTs[:], in_=zs[:, 0:64], func=AF.Exp, scale=-1.0), reads=[zsb], writes=[b_aTs])
                        T.op(act, lambda e: e.activation(out=aTs[:], in_=aTs[:], func=AF.Ln, bias=1.0, scale=1.0), reads=[b_aTs], writes=[b_aTs])
                        T.op(act, lambda e: e.activation(out=aTs[:], in_=aTs[:], func=AF.Exp, scale=-1.0 / 16), reads=[b_aTs], writes=[b_aTs])
                        T.op(dve, lambda e: e.memset(qmask[:], 0.0), writes=[b_qmask])
                        for b in range(NSMP):
                            T.op(dve, lambda e, b=b: e.tensor_copy(qmask[:, :, b, b], qTs_f[:, :, b]), reads=[b_qTs], writes=[b_qmask])
                        sst = {}

                        def stA(b):
                            s0, b_s0 = s0r.next()
                            T.dma(sp, s0[:].rearrange("q (h v) -> q h v", h=4), sg[l, b].rearrange("h k v -> k h v"), b_s0, writes=[b_s0])
                            sst[b] = [s0, b_s0]

                        def stB(b):
                            s0, b_s0 = sst[b]
                            km, b_km = kmr.next()
                            T.op(dve, lambda e: e.tensor_scalar(out=km[:], in0=ks_f[:], scalar1=ident_f[0:NSMP, b:b + 1], scalar2=None,
                                                                op0=ALU.mult), reads=[b_ksf, b_c], writes=[b_km])
                            kvb = [T.ps(), T.ps()]
                            for bk in range(2):
                                T.mm([lambda e, h=bk * 2 + hh, hh=hh: e.matmul(kvb[bk][0][:, hh * 256:(hh + 1) * 256], km[0:NSMP, h * 128:(h + 1) * 128],
                                                                               vs_f[0:NSMP, h * 256:(h + 1) * 256], start=True, stop=True)
                                      for hh in range(2)], reads=[b_km, b_vsf], writes=[kvb[bk][1]])
                            sn, b_sn = snr.next()
                            for h in range(4):
                                T.op(dve, lambda e, h=h: e.scalar_tensor_tensor(out=sn[:, h * 256:(h + 1) * 256], in0=s0[:, h * 256:(h + 1) * 256],
                                                                                scalar=aTs[:, h * 16 + b:h * 16 + b + 1],
                                                                                in1=kvb[h // 2][0][:, (h % 2) * 256:(h % 2 + 1) * 256],
                                                                                op0=ALU.mult, op1=ALU.add),
                                     reads=[b_s0, b_aTs, kvb[h // 2][1]], writes=[b_sn])
                            T.dma(sp, oglas[l, b].rearrange("h k v -> k h v"), sn[:].rearrange("q (h v) -> q h v", h=4), b_sn, reads=[b_sn])
                            snb, b_snb = snbr.next()
                            T.op(act, lambda e: e.copy(snb[:], sn[:]), reads=[b_sn], writes=[b_snb])
                            sst[b] = [snb, b_snb]

                        def stC(b):
                            snb, b_snb = sst.pop(b)
                            T.mm([lambda e, h=h: e.matmul(T.psl[h][0][0:NSMP, 0:256], qmask[:, h, b, :], snb[:, h * 256:(h + 1) * 256],
                                                          start=(b == 0), stop=(b == NSMP - 1)) for h in range(4)],
                                 reads=[b_qmask, b_snb], writes=[T.psl[h][1] for h in range(4)])

                        T.ps_range = (4, 8)
                        msteps = merge_steps(0, psubs)
                        stA(0)
                        for b in range(NSMP):
                            if b + 1 < NSMP:
                                stA(b + 1)
                            stB(b)
                            if b >= 1:
                                stC(b - 1)
                            if msteps:
                                msteps.pop(0)()
                        stC(NSMP - 1)
                        while msteps:
                            msteps.pop(0)()
                        post(lambda h: T.psl[h][0][0:NSMP, 0:256], [T.psl[h][1] for h in range(4)], NSMP, Xs[:, :], b_Xs, TB, 4)
                        T.ps_range = (0, 8)
                        merge(0, [(TB, NSMP)])
                    T.mark_fence()

                chk("glasample_%d_%d" % (blk, l))
                if not has_s:
                    merge(0)
                chk("merge0_%d_%d" % (blk, l))

                with ExitStack() as ph:
                    L = 15 + TB
                    uT, b_uT = T.sb("uT", [128, 4, 15 + NT], F32, ph)
                    wa_, b_wa_ = T.sb("wa_", [128, L], F32, ph)
                    wb_, b_wb_ = T.sb("wb_", [128, L], F32, ph)
                    diffT, b_diffT = T.sb("diffT", [128, 8, NT], BF16, ph)
                    spg, b_spg = T.sb("spg", [128, 8, NT], BF16, ph)
                    us_f, b_usf = T.sb("us_f", [NSMP, D], F32, ph)
                    ulast, b_ulast = T.sb("ulast", [128, D], F32, ph)
                    t16, b_t16 = T.sb("t16", [128, 16], F32, ph)
                    mtmp = Rot(T, "mtmp", 2, [128, 512], BF16, ph)
                    for jb in range(2):
                        wt, wbuf = wget(("in", l, U0 + jb * 512))

                        def ev_u(mi, t0, tn, ps_, pb):
                            T.op(act, lambda e: e.copy(uT[:, mi, 15 + t0:15 + t0 + tn], ps_), reads=[pb], writes=[b_uT])
                        bmode(wt, wbuf, M4, psubs, hT, hb, ev_u)
                        if has_s:
                            pst, pb = amode(hT, [hb[4]], TB, NSMP, wt, wbuf)
                            T.op(act, lambda e: e.copy(us_f[:, jb * 512:(jb + 1) * 512], pst[0:NSMP, 0:512]), reads=[pb], writes=[b_usf])
                        if blk == NBLK - 1:
                            pst, pb = amode(hT, [hb[3]], 384, 128, wt, wbuf)
                            T.op(act, lambda e: e.copy(ulast[:, jb * 512:(jb + 1) * 512], pst[:, 0:512]), reads=[pb], writes=[b_ulast])
                        for mi in range(4):
                            ch = 4 * jb + mi
                            g = ch // 2
                            w = 2 << g
                            u = uT[:, mi, :]
                            T.op(dve, lambda e: e.tensor_copy(uT[:, mi, 0:15], uhist[:, l, ch, :]), reads=[b_uhist], writes=[b_uT])
                            T.op(dve, lambda e: e.tensor_tensor(out=wa_[:, 1:L], in0=u[:, 1:L], in1=u[:, 0:L - 1], op=ALU.add),
                                 reads=[b_uT], writes=[b_wa_])
                            s_ap, b_s = wa_, b_wa_
                            if g >= 1:
                                T.op(dve, lambda e: e.tensor_tensor(out=wb_[:, 3:L], in0=wa_[:, 3:L], in1=wa_[:, 1:L - 2], op=ALU.add),
                                     reads=[b_wa_], writes=[b_wb_])
                                s_ap, b_s = wb_, b_wb_
                            if g >= 2:
                                T.op(dve, lambda e: e.tensor_tensor(out=wa_[:, 7:L], in0=wb_[:, 7:L], in1=wb_[:, 3:L - 4], op=ALU.add),
                                     reads=[b_wb_], writes=[b_wa_])
                                s_ap, b_s = wa_, b_wa_
                            if g >= 3:
                                T.op(dve, lambda e: e.tensor_tensor(out=wb_[:, 15:L], in0=wa_[:, 15:L], in1=wa_[:, 7:L - 8], op=ALU.add),
                                     reads=[b_wa_], writes=[b_wb_])
                                s_ap, b_s = wb_, b_wb_
                            T.op(dve, lambda e: e.scalar_tensor_tensor(out=diffT[:, ch, 0:TB], in0=s_ap[:, 15:L], scalar=1.0 / w,
                                                                       in1=u[:, 15:L], op0=ALU.mult, op1=ALU.subtract),
                                 reads=[b_s, b_uT], writes=[b_diffT])
                            if blk == 0:
                                T.op(dve, lambda e: e.tensor_tensor(out=t16[:], in0=s_ap[:, 15:31], in1=rc[:, g, :], op=ALU.mult),
                                     reads=[b_s, b_c], writes=[b_t16])
                                T.op(dve, lambda e: e.tensor_tensor(out=diffT[:, ch, 0:16], in0=t16[:], in1=u[:, 15:31], op=ALU.subtract),
                                     reads=[b_t16, b_uT], writes=[b_diffT])
                            if blk < NBLK - 1:
                                T.op(dve, lambda e: e.tensor_copy(uhist[:, l, ch, :], uT[:, mi, TB:TB + 15]), reads=[b_uT], writes=[b_uhist])
                    if blk == NBLK - 1:
                        T.dma(sp, opoolp[l], ulast[113:128, :], b_ulast, reads=[b_ulast])
                    if has_s:
                        hbuf, b_hbuf = T.sb("hbuf", [NSMP, 15, 256], F32, ph)
                        hs, b_hs = T.sb("hs", [NSMP, 256], F32, ph)
                        dsb, b_dsb = T.sb("dsb", [NSMP, D], BF16, ph)
                        for g in range(4):
                            w = 2 << g
                            cs = slice(g * 256, (g + 1) * 256)
                            T.dma(sp, hbuf[:, 0:w - 1, :], spool[l, :, 15 - (w - 1):15, cs], b_hbuf, writes=[b_hbuf])
                            T.op(dve, lambda e: e.tensor_reduce(out=hs[:], in_=hbuf[:, 0:w - 1, :].rearrange("p r c -> p c r"), axis=AX.X, op=ALU.add),
                                 reads=[b_hbuf], writes=[b_hs])
                            T.op(dve, lambda e: e.tensor_tensor(out=hs[:], in0=hs[:], in1=us_f[:, cs], op=ALU.add), reads=[b_hs, b_usf], writes=[b_hs])
                            T.op(dve, lambda e: e.scalar_tensor_tensor(out=dsb[:, cs], in0=hs[:], scalar=1.0 / w, in1=us_f[:, cs],
                                                                       op0=ALU.mult, op1=ALU.subtract), reads=[b_hs, b_usf], writes=[b_dsb])
                        transpose8(dsb, b_dsb, NSMP, diffT[:, :, TB:NT], b_diffT)
                        T.dma(sp, opools[l, :, 14, :], us_f[:, :], b_usf, reads=[b_usf])
                        T.dma(sp, opools[l, :, 0:14, :], spool[l, :, 1:15, :], b_dd)
                    for jb in range(2):
                        wt, wbuf = wget(("in", l, PG0 + jb * 512))

                        def ev_pg(mi, t0, tn, ps_, pb):
                            T.op(act, lambda e: e.activation(out=spg[:, 4 * jb + mi, t0:t0 + tn], in_=ps_, func=AF.Silu), reads=[pb], writes=[b_spg])
                        bmode(wt, wbuf, M4, subs, hT, hb, ev_pg)
                    for jb in range(2):
                        for gg in range(2):
                            g = 2 * jb + gg
                            for ee in range(2):
                                ch = 2 * g + ee
                                for (t0, tn) in subs:
                                    pst, pb = T.ps()
                                    T.mm([lambda e, k2=k2: e.matmul(pst[:, 0:tn], pw[:, l, g, k2, ee * 128:(ee + 1) * 128],
                                                                    diffT[:, 2 * g + k2, t0:t0 + tn], start=(k2 == 0), stop=(k2 == 1))
                                          for k2 in range(2)], reads=[b_pw, b_diffT], writes=[pb])
                                    T.op(dve, lambda e: e.scalar_tensor_tensor(out=brT[:, ch, t0:t0 + tn], in0=pst[:, 0:tn],
                                                                               scalar=pscT[:, l, ch:ch + 1], in1=spg[:, ch, t0:t0 + tn],
                                                                               op0=ALU.mult, op1=ALU.mult),
                                         reads=[pb, b_pscT, b_spg], writes=tb(brb, t0, tn))
                    merge(1)
                    T.mark_fence()
                chk("pool_%d_%d" % (blk, l))

                with ExitStack() as ph:
                    xqb, b_xqb = T.sb("xqb", [128, 4, NT], BF16, ph)
                    sxg, b_sxg = T.sb("sxg", [128, 4, TB], BF16, ph)
                    sxgs, b_sxgs = T.sb("sxgs", [128, 8, NSMP], BF16, ph)
                    pTr = Rot(T, "pT", 2, [128, 2, TB], BF16, ph)
                    rinv, b_rinv = T.sb("rinv", [128, TB], F32, ph)
                    tmpx = Rot(T, "tmpx", 2, [128, TB], F32, ph)
                    stg = Rot(T, "stg", 2, [128, 512], F32, ph)
                    xqs_f, b_xqs = T.sb("xqs_f", [NSMP, D], BF16, ph)
                    mtmp = Rot(T, "mtmp", 2, [128, 512], BF16, ph)
                    if blk == 0:
                        for j in range(2):
                            wt, wbuf = wget(("mk", l, j))
                            for mt in range(2):
                                pst, pb = amode(memT, [b_memT], mt * 128, 128, wt, wbuf)
                                st_, b_st = stg.next()
                                T.op(act, lambda e: e.copy(st_[:], pst[:, 0:512]), reads=[pb], writes=[b_st])
                                T.dma(sp, omk[l, mt * 128:(mt + 1) * 128, j * 512:(j + 1) * 512], st_[:], b_st, reads=[b_st])

                            def ev_mk(mi, t0, tn, ps_, pb):
                                T.op(dve, lambda e: e.tensor_copy(mkT[:, l, 4 * j + mi, :], ps_), reads=[pb], writes=[b_mkT])
                            bmode(wt, wbuf, M4, [(0, 256)], memT, [b_memT] * 5, ev_mk)
                        for j in range(2):
                            wt, wbuf = wget(("mv", l, j))
                            for mt in range(2):
                                pst, pb = amode(memT, [b_memT], mt * 128, 128, wt, wbuf)
                                st_, b_st = stg.next()
                                T.op(act, lambda e: e.copy(st_[:], pst[:, 0:512]), reads=[pb], writes=[b_st])
                                T.dma(sp, omv[l, mt * 128:(mt + 1) * 128, j * 512:(j + 1) * 512], st_[:], b_st, reads=[b_st])
                                T.op(dve, lambda e: e.tensor_copy(mvb[:, l, mt, j * 512:(j + 1) * 512], pst[:, 0:512]), reads=[pb], writes=[b_mvb])
                    chk("xa1_%d_%d" % (blk, l))
                    for jb in range(2):
                        wt, wbuf = wget(("in", l, XQ0 + jb * 512))

                        def ev_xq(mi, t0, tn, ps_, pb):
                            T.op(act, lambda e: e.activation(out=xqb[:, mi, t0:t0 + tn], in_=ps_, func=AF.Copy, scale=1.0 / 16),
                                 reads=[pb], writes=[b_xqb])
                        bmode(wt, wbuf, M4, psubs, hT, hb, ev_xq)
                        if has_s:
                            pst, pb = amode(hT, [hb[4]], TB, NSMP, wt, wbuf)
                            T.op(act, lambda e: e.activation(out=xqs_f[:, jb * 512:(jb + 1) * 512], in_=pst[0:NSMP, 0:512], func=AF.Copy, scale=1.0 / 16),
                                 reads=[pb], writes=[b_xqs])
                        wt, wbuf = wget(("in", l, XG0 + jb * 512))

                        def ev_xg(mi, t0, tn, ps_, pb):
                            if t0 >= TB:
                                T.op(act, lambda e: e.activation(out=sxgs[:, 4 * jb + mi, :], in_=ps_, func=AF.Silu), reads=[pb], writes=[b_sxgs])
                            else:
                                T.op(act, lambda e: e.activation(out=sxg[:, mi, t0:t0 + tn], in_=ps_, func=AF.Silu), reads=[pb], writes=[b_sxg])
                        bmode(wt, wbuf, M4, subs, hT, hb, ev_xg)
                        hst = {}

                        def head_a(hh):
                            h = 2 * jb + hh
                            pT, b_pT = pTr.next()
                            for mc in range(2):
                                sp_, sb_ = T.ps()
                                T.mm([lambda e, dc=dc: e.matmul(sp_[:, 0:TB], mkT[:, l, 2 * h + dc, mc * 128:(mc + 1) * 128], xqb[:, 2 * hh + dc, 0:TB],
                                                                start=(dc == 0), stop=(dc == 1)) for dc in range(2)],
                                     reads=[b_mkT, b_xqb], writes=[sb_])
                                T.op(act, lambda e: e.activation(out=pT[:, mc, :], in_=sp_[:, 0:TB], func=AF.Exp), reads=[sb_], writes=[b_pT])
                            hst[hh] = (pT, b_pT)

                        def head_b(hh):
                            h = 2 * jb + hh
                            pT, b_pT = hst[hh]
                            sm_, smb = T.ps()
                            T.mm([lambda e, mc=mc: e.matmul(sm_[:, 0:TB], ones_b[:], pT[:, mc, :], start=(mc == 0), stop=(mc == 1)) for mc in range(2)],
                                 reads=[b_pT, b_c], writes=[smb])
                            T.op(act, lambda e: e.activation(out=rinv[:], in_=sm_[:, 0:TB], func=AF.Ln), reads=[smb], writes=[b_rinv])
                            T.op(act, lambda e: e.activation(out=rinv[:], in_=rinv[:], func=AF.Exp, scale=-1.0), reads=[b_rinv], writes=[b_rinv])
                            for dc in range(2):
                                op_, opb = T.ps()
                                T.mm([lambda e, mc=mc: e.matmul(op_[:, 0:TB], mvb[:, l, mc, (2 * h + dc) * 128:(2 * h + dc + 1) * 128], pT[:, mc, :],
                                                                start=(mc == 0), stop=(mc == 1)) for mc in range(2)],
                                     reads=[b_mvb, b_pT], writes=[opb])
                                tx, b_tx = tmpx.next()
                                T.op(dve, lambda e: e.tensor_tensor(out=tx[:], in0=op_[:, 0:TB], in1=rinv[:], op=ALU.mult),
                                     reads=[opb, b_rinv], writes=[b_tx])
                                T.op(dve, lambda e: e.tensor_tensor(out=brT[:, 2 * h + dc, 0:TB], in0=tx[:], in1=sxg[:, 2 * hh + dc, :], op=ALU.mult),
                                     reads=[b_tx, b_sxg], writes=brb[0:4])

                        head_a(0)
                        head_a(1)
                        head_b(0)
                        head_b(1)
                    chk("xa2_%d_%d" % (blk, l))
                    if has_s:
                        kbr = Rot(T, "kbuf", 2, [128, D], F32, ph)
                        vbuf = Rot(T, "vbuf", 2, [128, 2, D], BF16, ph)
                        vfr = Rot(T, "vf", 2, [128, D], F32, ph)
                        prodr = Rot(T, "prod", 3, [128, 512], F32, ph)
                        sTall, b_sTall = T.sb("sTall", [128, 2, NSMP, 4], F32, ph)
                        stok, b_stok = T.sb("stok", [64, 256], F32, ph)
                        pnf, b_pnf = T.sb("pnf", [64, 256], F32, ph)
                        pn, b_pn = T.sb("pn", [64, 256], BF16, ph)
                        smx, b_smx = T.sb("smx", [64, 8], F32, ph)
                        pTs, b_pTs = T.sb("pTs", [128, 2, NSMP, 4], BF16, ph)
                        pmask, b_pmask = T.sb("pmask", [128, 2, 4, NSMP, NSMP], BF16, ph)
                        cb, b_cb = T.sb("cb", [NSMP, D], BF16, ph)
                        sel, b_sel = T.sb("sel", [NSMP, NSMP, 128], BF16, ph)
                        for b in range(NSMP):
                            T.op(dve, lambda e, b=b: e.tensor_scalar(out=sel[:, b, :], in0=ones_f[0:NSMP, :], scalar1=ident_f[0:NSMP, b:b + 1],
                                                                     scalar2=None, op0=ALU.mult), reads=[b_c], writes=[b_sel])
                        for b in range(NSMP):
                            qb = [T.ps(), T.ps()]
                            for bk in range(2):
                                T.mm([lambda e, bk=bk, b=b: e.matmul(qb[bk][0][:, 0:512], sel[0:NSMP, b, :], xqs_f[0:NSMP, bk * 512:(bk + 1) * 512],
                                                                     start=True, stop=True)], reads=[b_sel, b_xqs], writes=[qb[bk][1]])
                            for mt in range(2):
                                kb, b_kb = kbr.next()
                                T.dma(sp, kb[:], ck[l, b, mt * 128:(mt + 1) * 128, :], b_kb, writes=[b_kb])
                                pa, b_pa = prodr.next()
                                pb_, b_pb = prodr.next()
                                T.op(dve, lambda e: e.tensor_tensor(out=pa[:], in0=qb[0][0][:, 0:512], in1=kb[:, 0:512], op=ALU.mult),
                                     reads=[qb[0][1], b_kb], writes=[b_pa])
                                T.op(dve, lambda e: e.tensor_tensor(out=pb_[:], in0=qb[1][0][:, 0:512], in1=kb[:, 512:1024], op=ALU.mult),
                                     reads=[qb[1][1], b_kb], writes=[b_pb])
                                T.op(dve, lambda e, mt=mt, b=b: e.tensor_reduce(out=sTall[:, mt, b, 0:2], in_=pa[:].rearrange("q (h d) -> q h d", h=2),
                                                                                axis=AX.X, op=ALU.add), reads=[b_pa], writes=[b_sTall])
                                for hh in range(2):
                                    T.op(act, lambda e, mt=mt, b=b, hh=hh: e.activation(out=junk[:, 0:256], in_=pb_[:, hh * 256:(hh + 1) * 256], func=AF.Copy,
                                                                                         accum_out=sTall[:, mt, b, 2 + hh:3 + hh]),
                                         reads=[b_pb], writes=[b_junk, b_sTall])
                        chk("xa3_%d_%d" % (blk, l))
                        for mt in range(2):
                            tp_, tpb = T.ps()
                            T.mm([lambda e, mt=mt: e.transpose(tp_[0:64, 0:128], sTall[:, mt].rearrange("q b h -> q (b h)"), ident_f[:])],
                                 reads=[b_sTall, b_c], writes=[tpb])
                            T.op(act, lambda e, mt=mt: e.copy(stok[:, mt * 128:(mt + 1) * 128], tp_[0:64, 0:128]), reads=[tpb], writes=[b_stok])
                        T.op(dve, lambda e: e.reduce_max(out=smx[:, 0:1], in_=stok[:], axis=AX.X), reads=[b_stok], writes=[b_smx])
                        T.op(dve, lambda e: e.tensor_scalar(out=smx[:, 1:2], in0=smx[:, 0:1], scalar1=-1.0, scalar2=None, op0=ALU.mult),
                             reads=[b_smx], writes=[b_smx])
                        T.op(act, lambda e: e.activation(out=pnf[:], in_=stok[:], func=AF.Exp, bias=smx[:, 1:2], scale=1.0, accum_out=smx[:, 2:3]),
                             reads=[b_stok, b_smx], writes=[b_pnf, b_smx])
                        T.op(dve, lambda e: e.reciprocal(out=smx[:, 3:4], in_=smx[:, 2:3]), reads=[b_smx], writes=[b_smx])
                        T.op(dve, lambda e: e.tensor_scalar(out=pn[:], in0=pnf[:], scalar1=smx[:, 3:4], scalar2=None, op0=ALU.mult),
                             reads=[b_pnf, b_smx], writes=[b_pn])
                        for mt in range(2):
                            tp_, tpb = T.ps()
                            tpb16 = tp_[:].bitcast(BF16)
                            T.mm([lambda e, mt=mt: e.transpose(tpb16[:, 0:64], pn[0:64, mt * 128:(mt + 1) * 128], ident_b[0:64, 0:64])],
                                 reads=[b_pn, b_c], writes=[tpb])
                            T.op(dve, lambda e, mt=mt: e.tensor_copy(pTs[:, mt].rearrange("q b h -> q (b h)"), tpb16[:, 0:64]),
                                 reads=[tpb], writes=[b_pTs])
                        T.op(dve, lambda e: e.memset(pmask[:], 0.0), writes=[b_pmask])
                        for b in range(NSMP):
                            T.op(dve, lambda e, b=b: e.tensor_copy(pmask[:, :, :, b, b], pTs[:, :, b, :]), reads=[b_pTs], writes=[b_pmask])
                        chk("xa4_%d_%d" % (blk, l))
                        T.ps_range = (4, 8)
                        msteps = merge_steps(2, psubs)
                        for b in range(NSMP):
                            vb, b_vb = vbuf.next()
                            for mt in range(2):
                                vf, b_vf = vfr.next()
                                T.dma(sp, vf[:], cv[l, b, mt * 128:(mt + 1) * 128, :], b_vf, writes=[b_vf])
                                T.op(act, lambda e, mt=mt: e.copy(vb[:, mt, :], vf[:]), reads=[b_vf], writes=[b_vb])
                            for h in range(4):
                                T.mm([lambda e, h=h, mt=mt, b=b: e.matmul(T.psl[h][0][0:NSMP, 0:256], pmask[:, mt, h, b, :], vb[:, mt, h * 256:(h + 1) * 256],
                                                                          start=(b == 0 and mt == 0), stop=(b == NSMP - 1 and mt == 1))
                                      for mt in range(2)], reads=[b_pmask, b_vb], writes=[T.psl[h][1]])
                            if msteps:
                                msteps.pop(0)()
                        while msteps:
                            msteps.pop(0)()
                        for h in range(4):
                            T.op(dve, lambda e, h=h: e.tensor_copy(cb[:, h * 256:(h + 1) * 256], T.psl[h][0][0:NSMP, 0:256]),
                                 reads=[T.psl[h][1]], writes=[b_cb])
                        T.ps_range = (0, 8)

                        def ev_c(v, pb):
                            T.op(dve, lambda e: e.tensor_tensor(out=brT[:, :, TB:NT], in0=v, in1=sxgs[:], op=ALU.mult),
                                 reads=[pb, b_sxgs], writes=[brb[4]])
                        transpose8(cb, b_cb, NSMP, None, None, evac=ev_c)
                        merge(2, [(TB, NSMP)])
                    chk("xattn_%d_%d" % (blk, l))
                    if not has_s:
                        merge(2)

                    for j in range(2):
                        wt, wbuf = wget(("out", l, j))
                        for (i, p, c0) in tiles:
                            pst, pb = amode(mgT, [mgb[i]], c0, p, wt, wbuf)
                            xa = xtile(i)[:, j * 512:(j + 1) * 512] if i < 4 else xS[:, j * 512:(j + 1) * 512]
                            T.op(dve, lambda e: e.tensor_tensor(out=xa, in0=xa, in1=pst[0:p, 0:512], op=ALU.add), reads=[pb, xb[i]], writes=[xb[i]])
                    T.mark_fence()
                chk("layer_%d_%d" % (blk, l))

            phf = ExitStack()
            ytr = Rot(T, "yt", 2, [128, D], F32, phf)
            fgbc, b_fgbc = T.sb("fgbc", [128, D], F32, phf)
            T.dma(sp, fgbc[:], final_gain.partition_broadcast(128), b_fgbc, writes=[b_fgbc])
            sts = [rms_stats(xtile(i), xb[i], p, D, ssr.next()) for (i, p, c0) in tiles]
            for k_, (i, p, c0) in enumerate(tiles):
                rstd, b_ss = sts[k_]
                yt, b_yt = ytr.next()
                T.op(dve, lambda e: e.scalar_tensor_tensor(out=yt[0:p, :], in0=xtile(i), scalar=rstd, in1=fgbc[0:p, :], op0=ALU.mult, op1=ALU.mult),
                     reads=[xb[i], b_ss, b_fgbc], writes=[b_yt])
                if i < 4:
                    T.dma(sp, yp[blk * TB + i * 128: blk * TB + (i + 1) * 128, :], yt[:], b_yt, reads=[b_yt])
                    if blk + 1 < NBLK:
                        T.dma(sp, xP[:, i, :], xp[(blk + 1) * TB + i * 128: (blk + 1) * TB + (i + 1) * 128, :], xb[i], writes=[xb[i]])
                else:
                    T.dma(sp, ys[:, :], yt[0:NSMP, :], b_yt, reads=[b_yt])
            T.mark_fence()
            phf.close()
        T.finish()


_CACHE = {}


def _build():
    if "nc" not in _CACHE:
        nc0 = bass.Bass("TRN2", target_bir_lowering=False)
        seq = program(nc0, None)
        nc = bass.Bass("TRN2", target_bir_lowering=False)
        program(nc, seq)
        _CACHE["nc"] = nc
    return _CACHE["nc"]


def kernel(x_prompt, x_sample, mem_prompt, cache_mem_k, cache_mem_v, state_gla, state_pool,
           w_in, w_a2, b_a, gla_gain, pool_w, pool_scale, w_mk, w_mv, w_branch, w_out,
           norm_gain, final_gain):
    f = lambda a: np.ascontiguousarray(np.asarray(a, dtype=np.float32))
    x_prompt, x_sample, mem_prompt = f(x_prompt), f(x_sample), f(mem_prompt)
    cache_mem_k, cache_mem_v, state_gla, state_pool = f(cache_mem_k), f(cache_mem_v), f(state_gla), f(state_pool)
    shared = dict(w_in=f(w_in), w_a2=f(w_a2), b_a=f(b_a), gla_gain=f(gla_gain), pool_w=f(pool_w), pool_scale=f(pool_scale),
                  w_mk=f(w_mk), w_mv=f(w_mv), w_branch=f(w_branch), w_out=f(w_out), norm_gain=f(norm_gain), final_gain=f(final_gain))
    nc = _build()
    in_maps = []
    for c in range(8):
        s = slice(c * NSMP, (c + 1) * NSMP)
        m = dict(shared)
        m["xp"] = x_prompt[c]
        m["xs"] = np.ascontiguousarray(x_sample[s, 0, :])
        m["mem"] = mem_prompt[c]
        m["ck"] = np.ascontiguousarray(cache_mem_k[:, s].reshape(2, NSMP, 256, D))
        m["cv"] = np.ascontiguousarray(cache_mem_v[:, s].reshape(2, NSMP, 256, D))
        m["sg"] = np.ascontiguousarray(state_gla[:, s])
        m["spool"] = np.ascontiguousarray(state_pool[:, s])
        in_maps.append(m)
    res = run_bass_kernel_spmd(nc, in_maps, core_ids=list(range(8)))
    R = res.results
    y_prompt = np.stack([R[c]["yp"] for c in range(8)], axis=0)
    y_sample = np.concatenate([R[c]["ys"] for c in range(8)], axis=0).reshape(128, 1, D)
    new_mk = np.stack([R[c]["omk"] for c in range(8)], axis=1).reshape(2, 8, 256, 4, 256)
    new_mv = np.stack([R[c]["omv"] for c in range(8)], axis=1).reshape(2, 8, 256, 4, 256)
    new_glap = np.stack([R[c]["oglap"] for c in range(8)], axis=1)
    new_poolp = np.stack([R[c]["opoolp"] for c in range(8)], axis=1)
    new_glas = np.concatenate([R[c]["oglas"] for c in range(8)], axis=1)
    new_pools = np.concatenate([R[c]["opools"] for c in range(8)], axis=1)
    out = (y_prompt, y_sample, new_mk, new_mv, new_glap, new_poolp, new_glas, new_pools)
    return tuple(np.ascontiguousarray(o, dtype=np.float32) for o in out)
```

```python
import numpy as np
from contextlib import ExitStack
import concourse.bass as bass
import concourse.mybir as mybir
from concourse.bass_utils import run_bass_kernel_spmd

F32 = mybir.dt.float32
BF16 = mybir.dt.bfloat16
AF = mybir.ActivationFunctionType
ALU = mybir.AluOpType
AX = mybir.AxisListType

D = 1024
NIN = 10256
TB = 512
NBLK = 4
NSMP = 16
NT = TB + NSMP
Q0, K0, V0, GG0, A0, U0, PG0, XQ0, XG0, MG0 = 0, 512, 1024, 2048, 3072, 3088, 4112, 5136, 6160, 7184
EPS = 1e-6
NSLOT = 4
PREF = 3


class Buf:
    __slots__ = ("name", "w", "r", "excl")

    def __init__(self, name, excl=False):
        self.name = name
        self.w = None
        self.r = {}
        self.excl = excl


class Eng:
    def __init__(self, T, e, name):
        self.T, self.e, self.name = T, e, name
        self.sem = T.newsem("e_" + name)
        self.n = 0
        self.seen = {}

    def wait(self, ev):
        sem, val = ev
        k = id(sem)
        if self.seen.get(k, 0) >= val:
            return
        self.seen[k] = val
        if not self.T.dry:
            self.e.wait_ge(sem, val)


class Tracker:
    def __init__(self, nc, es, dry):
        self.nc, self.es, self.dry = nc, es, dry
        self.pe = Eng(self, nc.tensor, "pe")
        self.act = Eng(self, nc.scalar, "act")
        self.dve = Eng(self, nc.vector, "dve")
        self.pool = Eng(self, nc.gpsimd, "pool")
        self.sp = Eng(self, nc.sync, "sp")
        self.dsems = {}
        self.psl = []
        self.psi = 0
        self.ps_range = (0, 8)
        self.stopped = False
        self.fence = {}
        self.store_sems = {}

    def newsem(self, name):
        if self.dry:
            return object()
        return self.es.enter_context(self.nc.semaphore(name))

    def sb(self, name, shape, dt, stack=None):
        self.uid = getattr(self, "uid", 0) + 1
        nm = "%s_%d" % (name, self.uid)
        t = (stack or self.es).enter_context(self.nc.sbuf_tensor(nm, shape, dt))
        b = Buf(nm)
        if stack is not None:
            b.r = dict(self.fence)
        return t, b

    def init_psum(self):
        for i in range(8):
            t = self.es.enter_context(self.nc.psum_tensor("ps%d" % i, [128, 512], F32))
            self.psl.append((t, Buf("ps%d" % i, excl=True)))

    def ps(self):
        lo, hi = self.ps_range
        r = self.psl[lo + self.psi % (hi - lo)]
        self.psi += 1
        return r

    def _deps(self, eng, reads, writes):
        for b in reads:
            if b.w is not None:
                eng.wait(b.w)
            if b.excl:
                me = id(eng.sem)
                for k, ev in b.r.items():
                    if k != me:
                        eng.wait(ev)
        for b in writes:
            if b.w is not None:
                eng.wait(b.w)
            for ev in b.r.values():
                eng.wait(ev)

    def _mark(self, ev, reads, writes):
        k = id(ev[0])
        for b in reads:
            b.r[k] = ev
        for b in writes:
            b.w = ev
            b.r = {}

    def op(self, eng, fn, reads=(), writes=()):
        if self.stopped:
            return
        self._deps(eng, reads, writes)
        eng.n += 1
        if not self.dry:
            fn(eng.e).then_inc(eng.sem, 1)
        self._mark((eng.sem, eng.n), reads, writes)

    def mm(self, fns, reads=(), writes=()):
        if self.stopped:
            return
        eng = self.pe
        self._deps(eng, reads, writes)
        eng.n += 1
        if not self.dry:
            ins = None
            for fn in fns:
                ins = fn(eng.e)
            ins.then_inc(eng.sem, 1)
        self._mark((eng.sem, eng.n), reads, writes)

    def dma(self, eng, out, in_, key, reads=(), writes=()):
        if self.stopped:
            return
        self._deps(eng, reads, writes)
        if id(key) not in self.dsems:
            self.dsems[id(key)] = [self.newsem("d_" + key.name), 0]
        st = self.dsems[id(key)]
        st[1] += 16
        if len(reads) > 0:
            self.store_sems[id(st[0])] = st
        if not self.dry:
            eng.e.dma_start(out=out, in_=in_).then_inc(st[0], 16)
        self._mark((st[0], st[1]), reads, writes)

    def barrier(self, with_pool=False):
        if self.stopped:
            return
        engs = [self.pe, self.act, self.dve, self.sp]
        allv = engs + [self.pool]
        for e in (allv if with_pool else engs):
            for o in allv:
                if o.n > 0:
                    e.wait((o.sem, o.n))
            for st in self.dsems.values():
                e.wait((st[0], st[1]))

    def mark_fence(self):
        if self.stopped:
            return
        f = {}
        for o in [self.pe, self.act, self.dve, self.sp, self.pool]:
            if o.n > 0:
                f[id(o.sem)] = (o.sem, o.n)
        for st in self.store_sems.values():
            f[id(st[0])] = (st[0], st[1])
        self.fence = f

    def finish(self):
        self.barrier()


class Rot:
    def __init__(self, T, name, n, shape, dt, stack=None):
        self.items = [T.sb("%s%d" % (name, i), shape, dt, stack) for i in range(n)]
        self.i = 0

    def next(self):
        r = self.items[self.i % len(self.items)]
        self.i += 1
        return r


class _Stop(Exception):
    pass


STOP = None


def program(nc, wseq):
    dry = wseq is None
    rec = []
    es = ExitStack()
    with es:
      T = None
      try:
        _program(nc, wseq, dry, rec, es)
      except _Stop:
        pass
    return rec


def _program(nc, wseq, dry, rec, es):
    if True:
        es.enter_context(nc.allow_non_contiguous_dma(reason="small strided parameter loads"))
        es.enter_context(nc.allow_low_precision(reason="bf16 matmul operands, fp32 accumulate"))
        T = Tracker(nc, es, dry)
        pe, act, dve, pool, sp = T.pe, T.act, T.dve, T.pool, T.sp

        def chk(name):
            if STOP == name and not T.stopped:
                T.finish()
                T.stopped = True

        def din(name, shape):
            return nc.dram_tensor(name, shape, F32, kind="ExternalInput").ap()

        def dout(name, shape):
            return nc.dram_tensor(name, shape, F32, kind="ExternalOutput").ap()

        xp = din("xp", [2048, D]); xs = din("xs", [NSMP, D]); mem = din("mem", [256, D])
        ck = din("ck", [2, NSMP, 256, D]); cv = din("cv", [2, NSMP, 256, D])
        sg = din("sg", [2, NSMP, 4, 128, 256]); spool = din("spool", [2, NSMP, 15, D])
        w_in = din("w_in", [2, D, NIN]); w_a2 = din("w_a2", [2, 16, 512]); b_a = din("b_a", [2, 512])
        gla_gain = din("gla_gain", [2, D]); pool_w = din("pool_w", [2, 4, 256, 256])
        pool_scale = din("pool_scale", [2, D]); w_mk = din("w_mk", [2, D, D]); w_mv = din("w_mv", [2, D, D])
        w_branch = din("w_branch", [2, 3, D, D]); w_out = din("w_out", [2, D, D])
        norm_gain = din("norm_gain", [2, D]); final_gain = din("final_gain", [D])
        yp = dout("yp", [2048, D]); ys = dout("ys", [NSMP, D]); omk = dout("omk", [2, 256, D]); omv = dout("omv", [2, 256, D])
        oglap = dout("oglap", [2, 4, 128, 256]); opoolp = dout("opoolp", [2, 15, D])
        oglas = dout("oglas", [2, NSMP, 4, 128, 256]); opools = dout("opools", [2, NSMP, 15, D])

        T.init_psum()

        xP, _ = T.sb("xP", [128, 4, D], F32)
        xb = [Buf("xb%d" % i) for i in range(5)]
        xS, _ = T.sb("xS", [NSMP, D], F32)
        hT, _ = T.sb("hT", [128, 8, NT], BF16); hb = [Buf("hb%d" % i) for i in range(5)]
        brT, _ = T.sb("brT", [128, 8, NT], BF16); brb = [Buf("brb%d" % i) for i in range(5)]
        mgT, _ = T.sb("mgT", [128, 8, NT], BF16); mgb = [Buf("mgb%d" % i) for i in range(5)]
        gblk, b_gblk = T.sb("gblk", [128, 4, NT], BF16)
        slots = [T.sb("wslot%d" % i, [128, 8, 512], BF16) for i in range(NSLOT)]
        S, b_S = T.sb("S", [128, 2, D], F32)
        Sbf, b_Sbf = T.sb("Sbf", [128, D], BF16)
        mkT, b_mkT = T.sb("mkT", [128, 2, 8, 256], BF16)
        mvb, b_mvb = T.sb("mvb", [128, 2, 2, D], BF16)
        memT, b_memT = T.sb("memT", [128, 8, 256], BF16)
        ones_f, b_c = T.sb("ones_f", [128, 128], F32)
        triU_f, _ = T.sb("triU_f", [128, 128], F32)
        triLs_f, _ = T.sb("triLs_f", [128, 128], F32)
        ident_f, _ = T.sb("ident_f", [128, 128], F32)
        ident_b, _ = T.sb("ident_b", [128, 128], BF16)
        ones_b, _ = T.sb("ones_b", [128, 128], BF16)
        triU4_b, _ = T.sb("triU4_b", [128, 512], BF16)
        rc, _ = T.sb("rc", [128, 4, 16], F32)
        a_aug, b_aaug = T.sb("a_aug", [17, NT], F32)
        w_a2aug, b_wa2 = T.sb("w_a2aug", [17, 2, 512], F32)
        wa, b_wa = T.sb("wa", [128, 2, 8, 16], BF16)
        pw, b_pw = T.sb("pw", [128, 2, 4, 2, 256], BF16)
        gainT, b_gainT = T.sb("gainT", [128, 2, 8], F32)
        pscT, b_pscT = T.sb("pscT", [128, 2, 8], F32)
        uhist, b_uhist = T.sb("uhist", [128, 2, 8, 15], F32)
        junk, b_junk = T.sb("junk", [128, D], BF16)
        ssr = Rot(T, "ss", 6, [128, 16], F32)
        b_dd = Buf("dram2dram")

        class WS:
            pos = 0
            issued = 0

        def wsrc(key):
            kind = key[0]
            if kind == "in":
                src = w_in[key[1], :, key[2]:key[2] + 512]
            elif kind == "br":
                src = w_branch[key[1], key[2], :, key[3] * 512:(key[3] + 1) * 512]
            elif kind == "out":
                src = w_out[key[1], :, key[2] * 512:(key[2] + 1) * 512]
            elif kind == "mk":
                src = w_mk[key[1], :, key[2] * 512:(key[2] + 1) * 512]
            else:
                src = w_mv[key[1], :, key[2] * 512:(key[2] + 1) * 512]
            return src.rearrange("(k p) n -> p k n", p=128)

        def wget(key):
            if T.stopped:
                return slots[0]
            i = WS.pos
            WS.pos += 1
            if dry:
                rec.append(key)
                return slots[0]
            assert wseq[i] == key, (i, wseq[i], key)
            while WS.issued < len(wseq) and WS.issued <= i + PREF:
                j = WS.issued
                st, sbuf_ = slots[j % NSLOT]
                T.dma(pool, st[:], wsrc(wseq[j]), sbuf_, writes=[sbuf_])
                WS.issued += 1
            return slots[i % NSLOT]

        T.op(dve, lambda e: e.memset(ones_f[:], 1.0), writes=[b_c])
        T.op(dve, lambda e: e.memset(ones_b[:], 1.0), writes=[b_c])
        T.op(pool, lambda e: e.affine_select(out=triU_f[:], in_=ones_f[:], pattern=[[1, 128]], compare_op=ALU.is_ge,
                                             fill=0.0, base=0, channel_multiplier=-1), reads=[b_c], writes=[b_c])
        T.op(pool, lambda e: e.affine_select(out=triLs_f[:], in_=ones_f[:], pattern=[[-1, 128]], compare_op=ALU.is_gt,
                                             fill=0.0, base=0, channel_multiplier=1), reads=[b_c], writes=[b_c])
        T.op(pool, lambda e: e.affine_select(out=ident_f[:], in_=ones_f[:], pattern=[[1, 128]], compare_op=ALU.is_equal,
                                             fill=0.0, base=0, channel_multiplier=-1), reads=[b_c], writes=[b_c])
        T.op(pool, lambda e: e.affine_select(out=ident_b[:], in_=ones_f[:], pattern=[[1, 128]], compare_op=ALU.is_equal,
                                             fill=0.0, base=0, channel_multiplier=-1), reads=[b_c], writes=[b_c])
        for h in range(4):
            T.op(pool, lambda e, h=h: e.affine_select(out=triU4_b[:, h * 128:(h + 1) * 128], in_=ones_f[:], pattern=[[1, 128]],
                                                      compare_op=ALU.is_ge, fill=0.0, base=0, channel_multiplier=-1),
                 reads=[b_c], writes=[b_c])
        for g in range(4):
            w = 2 << g
            T.op(dve, lambda e, g=g, w=w: e.memset(rc[:, g, :], 1.0 / w), writes=[b_c])
            for t in range(w - 1):
                T.op(dve, lambda e, g=g, t=t: e.memset(rc[:, g, t:t + 1], 1.0 / (t + 1)), writes=[b_c])
        T.op(dve, lambda e: e.memset(a_aug[:], 1.0), writes=[b_aaug])
        T.op(dve, lambda e: e.memset(S[:], 0.0), writes=[b_S])
        T.op(dve, lambda e: e.memset(uhist[:], 0.0), writes=[b_uhist])
        for l in range(2):
            T.dma(sp, w_a2aug[0:16, l, :], w_a2[l], b_wa2, writes=[b_wa2])
            T.dma(sp, w_a2aug[16:17, l, :], b_a[l:l + 1, :], b_wa2, writes=[b_wa2])
            T.dma(pool, wa[:, l], w_in[l, :, A0:A0 + 16].rearrange("(k p) n -> p k n", p=128), b_wa, writes=[b_wa])
            for g in range(4):
                T.dma(pool, pw[:, l, g], pool_w[l, g].rearrange("(k p) e -> p k e", p=128), b_pw, writes=[b_pw])
            T.dma(sp, gainT[:, l, :], norm_gain[l].rearrange("(k p) -> p k", p=128), b_gainT, writes=[b_gainT])
            T.dma(sp, pscT[:, l, :], pool_scale[l].rearrange("(k p) -> p k", p=128), b_pscT, writes=[b_pscT])

        def tb(bufs, t0, tn):
            if t0 >= TB:
                return [bufs[4]]
            return bufs[t0 // 128:(t0 + tn + 127) // 128]

        def transpose8(src, b_src, p, dst3, b_dst, evac=None):
            pst, pb = T.ps()
            psb = pst[:].bitcast(BF16)
            T.mm([lambda e, j=j: e.transpose(psb[:, j * 128:j * 128 + p], src[0:p, j * 128:(j + 1) * 128], ident_b[0:p, 0:p])
                  for j in range(8)], reads=[b_src, b_c], writes=[pb])
            v = psb.rearrange("q (j t) -> q j t", t=128)[:, :, 0:p]
            if evac is None:
                T.op(act, lambda e: e.copy(dst3, v), reads=[pb], writes=[b_dst])
            else:
                evac(v, pb)

        def rms_stats(xap, b_x, p, n, ssl):
            ss, b_ss = ssl
            T.op(act, lambda e: e.activation(out=junk[0:p, 0:n], in_=xap, func=AF.Square, accum_out=ss[0:p, 0:1]),
                 reads=[b_x], writes=[b_junk, b_ss])
            T.op(act, lambda e: e.activation(out=ss[0:p, 1:2], in_=ss[0:p, 0:1], func=AF.Sqrt, scale=1.0 / n, bias=EPS),
                 reads=[b_ss], writes=[b_ss])
            T.op(dve, lambda e: e.reciprocal(out=ss[0:p, 2:3], in_=ss[0:p, 1:2]), reads=[b_ss], writes=[b_ss])
            return ss[0:p, 2:3], b_ss

        def bmode(wt, wbuf, ms, subs, rhs3, rbufs, evac, nk=8, lhs_fn=None):
            for (mi, c0, mc) in ms:
                for (t0, tn) in subs:
                    pst, pb = T.ps()
                    T.mm([lambda e, kc=kc: e.matmul(pst[0:mc, 0:tn],
                                                    (lhs_fn(kc, c0, mc) if lhs_fn else wt[:, kc, c0:c0 + mc]),
                                                    rhs3[:, kc, t0:t0 + tn], start=(kc == 0), stop=(kc == nk - 1))
                          for kc in range(nk)], reads=[wbuf] + tb(rbufs, t0, tn), writes=[pb])
                    evac(mi, t0, tn, pst[0:mc, 0:tn], pb)

        def amode(lhs3, lbufs, c0, p, wt, wbuf, ncols=512):
            pst, pb = T.ps()
            T.mm([lambda e, kc=kc: e.matmul(pst[0:p, 0:ncols], lhs3[:, kc, c0:c0 + p], wt[:, kc, 0:ncols],
                                            start=(kc == 0), stop=(kc == 7)) for kc in range(8)],
                 reads=[wbuf] + lbufs, writes=[pb])
            return pst, pb

        M4 = [(m, m * 128, 128) for m in range(4)]

        with ExitStack() as ph:
            memf, b_memf = T.sb("memf", [128, 2, D], F32, ph)
            memb, b_memb = T.sb("memb", [128, 2, D], BF16, ph)
            T.dma(sp, memf[:], mem.rearrange("(t p) d -> p t d", p=128), b_memf, writes=[b_memf])
            T.op(dve, lambda e: e.tensor_copy(memb[:], memf[:]), reads=[b_memf], writes=[b_memb])
            for mt in range(2):
                transpose8(memb[:, mt, :], b_memb, 128, memT[:, :, mt * 128:(mt + 1) * 128], b_memT)
            T.mark_fence()

        chk("setup")
        for blk in range(NBLK):
            has_s = blk == 0
            tiles = [(i, 128, i * 128) for i in range(4)] + ([(4, NSMP, TB)] if has_s else [])
            subs = [(0, TB)] + ([(TB, NSMP)] if has_s else [])
            psubs = [(0, TB)]

            def xtile(i):
                return (xP[:, i, :] if i < 4 else xS[:, :])

            if blk == 0:
                for i in range(4):
                    T.dma(sp, xP[:, i, :], xp[blk * TB + i * 128: blk * TB + (i + 1) * 128, :], xb[i], writes=[xb[i]])
            if has_s:
                T.dma(sp, xS[:, :], xs[:, :], xb[4], writes=[xb[4]])

            for l in range(2):
                ph0 = ExitStack()
                xnr = Rot(T, "xn", len(tiles), [128, D], BF16, ph0)
                gbc, b_gbc = T.sb("gbc", [128, D], F32, ph0)
                T.dma(sp, gbc[:], norm_gain[l].partition_broadcast(128), b_gbc, writes=[b_gbc])
                sts = [rms_stats(xtile(i), xb[i], p, D, ssr.next()) for (i, p, c0) in tiles]
                xns = []
                for k_, (i, p, c0) in enumerate(tiles):
                    rstd, b_ss = sts[k_]
                    xn, b_xn = xnr.next()
                    T.op(dve, lambda e: e.scalar_tensor_tensor(out=xn[0:p, :], in0=xtile(i), scalar=rstd, in1=gbc[0:p, :],
                                                               op0=ALU.mult, op1=ALU.mult),
                         reads=[xb[i], b_ss, b_gbc], writes=[b_xn])
                    xns.append((xn, b_xn))
                for k_, (i, p, c0) in enumerate(tiles):
                    xn, b_xn = xns[k_]
                    transpose8(xn, b_xn, p, hT[:, :, c0:c0 + p], hb[i])
                T.mark_fence()
                ph0.close()
                chk("prenorm_%d_%d" % (blk, l))

                with ExitStack() as ph:
                    ks_f, b_ksf = T.sb("ks_f", [NSMP, 512], BF16, ph)
                    vs_f, b_vsf = T.sb("vs_f", [NSMP, D], BF16, ph)
                    Xs, b_Xs = T.sb("Xs", [NSMP, D], BF16, ph)
                    qTs_f, b_qTs = T.sb("qTs_f", [128, 4, NSMP], F32, ph)
                    brr = Rot(T, "br", 2, [128, D], BF16, ph)
                    ph2 = ExitStack()
                    qT, b_qT = T.sb("qT", [128, 4, NT], BF16, ph2)
                    kT, b_kT = T.sb("kT", [128, 4, NT], BF16, ph2)
                    ktok, b_ktok = T.sb("ktok", [128, 4, 512], BF16, ph2)
                    vtok, b_vtok = T.sb("vtok", [128, 4, D], BF16, ph2)
                    Xtok, b_Xtok = T.sb("Xtok", [128, 4, D], BF16, ph2)
                    tmpF = Rot(T, "tmpF", 4, [128, 512], F32, ph2)
                    splr = Rot(T, "spl", 4, [128, 512], F32, ph2)
                    tmpB = Rot(T, "tmpB", 12, [128, 512], BF16, ph2)
                    dec, b_dec = T.sb("dec", [128, 4, 4], F32, ph2)
                    osbr = Rot(T, "osb", 2, [128, D], BF16, ph2)
                    ggl, b_ggl = T.sb("ggl", [128, D], F32, ph2)
                    T.dma(sp, ggl[:], gla_gain[l].partition_broadcast(128), b_ggl, writes=[b_ggl])

                    def ev_a(mi, t0, tn, ps_, pb):
                        T.op(act, lambda e: e.copy(a_aug[0:16, t0:t0 + tn], ps_), reads=[pb], writes=[b_aaug])
                    bmode(wa[:, l], b_wa, [(0, 0, 16)], subs, hT, hb, ev_a)
                    wt, wbuf = wget(("in", l, Q0))

                    def ev_q(mi, t0, tn, ps_, pb):
                        T.op(act, lambda e: e.activation(out=qT[:, mi, t0:t0 + tn], in_=ps_, func=AF.Copy, scale=128.0 ** -0.5),
                             reads=[pb], writes=[b_qT])
                        if t0 >= TB:
                            T.op(dve, lambda e: e.tensor_scalar(out=qTs_f[:, mi, :], in0=ps_, scalar1=128.0 ** -0.5, scalar2=None,
                                                                op0=ALU.mult), reads=[pb], writes=[b_qTs])
                    bmode(wt, wbuf, M4, subs, hT, hb, ev_q)
                    wt, wbuf = wget(("in", l, K0))

                    def ev_k(mi, t0, tn, ps_, pb):
                        T.op(dve, lambda e: e.tensor_copy(kT[:, mi, t0:t0 + tn], ps_), reads=[pb], writes=[b_kT])
                    bmode(wt, wbuf, M4, psubs, hT, hb, ev_k)
                    for (i, p, c0) in tiles:
                        pst, pb = amode(hT, [hb[i]], c0, p, wt, wbuf)
                        if i < 4:
                            T.op(act, lambda e: e.copy(ktok[:, i, :], pst[:, 0:512]), reads=[pb], writes=[b_ktok])
                        else:
                            T.op(act, lambda e: e.copy(ks_f[:, :], pst[0:p, 0:512]), reads=[pb], writes=[b_ksf])
                    for j in range(2):
                        wt, wbuf = wget(("in", l, V0 + j * 512))
                        for (i, p, c0) in tiles:
                            pst, pb = amode(hT, [hb[i]], c0, p, wt, wbuf)
                            if i < 4:
                                T.op(act, lambda e: e.copy(vtok[:, i, j * 512:(j + 1) * 512], pst[:, 0:512]), reads=[pb], writes=[b_vtok])
                            else:
                                T.op(act, lambda e: e.copy(vs_f[:, j * 512:(j + 1) * 512], pst[0:p, 0:512]), reads=[pb], writes=[b_vsf])
                    for j in range(2):
                        wt, wbuf = wget(("in", l, GG0 + j * 512))
                        for (i, p, c0) in tiles:
                            pst, pb = amode(hT, [hb[i]], c0, p, wt, wbuf)
                            tf, b_tf = tmpF.next()
                            T.op(act, lambda e: e.activation(out=tf[0:p, :], in_=pst[0:p, 0:512], func=AF.Silu), reads=[pb], writes=[b_tf])
                            dst, b_dst = (Xtok[:, i, j * 512:(j + 1) * 512], b_Xtok) if i < 4 else (Xs[:, j * 512:(j + 1) * 512], b_Xs)
                            T.op(dve, lambda e: e.tensor_tensor(out=dst, in0=tf[0:p, :], in1=ggl[0:p, j * 512:(j + 1) * 512], op=ALU.mult),
                                 reads=[b_tf, b_ggl], writes=[b_dst])

                    T.op(act, lambda e: e.copy(Sbf[:], S[:, l, :]), reads=[b_S], writes=[b_Sbf])
                    chk("glaproj_%d_%d" % (blk, l))

                    def post(osrc, b_os, p, Xap, b_X, c0, i, split=False):
                        ss, b_ss = ssr.next()
                        for h in range(4):
                            T.op(act, lambda e, h=h: e.activation(out=junk[0:p, 0:256], in_=osrc(h), func=AF.Square,
                                                                  accum_out=ss[0:p, h:h + 1]), reads=b_os, writes=[b_junk, b_ss])
                        T.op(act, lambda e: e.activation(out=ss[0:p, 4:8], in_=ss[0:p, 0:4], func=AF.Sqrt, scale=1.0 / 256, bias=EPS),
                             reads=[b_ss], writes=[b_ss])
                        T.op(dve, lambda e: e.reciprocal(out=ss[0:p, 8:12], in_=ss[0:p, 4:8]), reads=[b_ss], writes=[b_ss])
                        br, b_br = brr.next()
                        for h in range(4):
                            T.op(dve, lambda e, h=h: e.scalar_tensor_tensor(out=br[0:p, h * 256:(h + 1) * 256], in0=osrc(h),
                                                                            scalar=ss[0:p, 8 + h:9 + h], in1=Xap[:, h * 256:(h + 1) * 256],
                                                                            op0=ALU.mult, op1=ALU.mult),
                                 reads=b_os + [b_ss, b_X], writes=[b_br])
                        if split:
                            return (lambda: transpose8(br, b_br, p, brT[:, :, c0:c0 + p], brb[i]))
                        transpose8(br, b_br, p, brT[:, :, c0:c0 + p], brb[i])

                    zps = []
                    for i in range(4):
                        tok = slice(i * 128, (i + 1) * 128)
                        zp, zb = T.ps()
                        T.mm([lambda e: e.matmul(zp[:, 0:512], a_aug[0:17, tok], w_a2aug[0:17, l, :], start=True, stop=True)],
                             reads=[b_aaug, b_wa2], writes=[zb])
                        zps.append((zp, zb))
                    spls = []
                    for i in range(4):
                        zp, zb = zps[i]
                        spl, b_spl = splr.next()
                        T.op(act, lambda e: e.activation(out=spl[:], in_=zp[:, 0:512], func=AF.Exp, scale=-1.0), reads=[zb], writes=[b_spl])
                        T.op(act, lambda e: e.activation(out=spl[:], in_=spl[:], func=AF.Ln, bias=1.0, scale=1.0), reads=[b_spl], writes=[b_spl])
                        spls.append((spl, b_spl))
                    cums = []
                    for i in range(4):
                        spl, b_spl = spls[i]
                        bp, bb = T.ps()
                        T.mm([lambda e, h=h: e.matmul(bp[:, h * 128:(h + 1) * 128], spl[:, h * 128:(h + 1) * 128], triU_f[:], start=True, stop=True)
                              for h in range(4)], reads=[b_spl, b_c], writes=[bb])
                        rp, rb = T.ps()
                        T.mm([lambda e: e.matmul(rp[:, 0:512], triLs_f[:], spl[:], start=True, stop=True)], reads=[b_spl, b_c], writes=[rb])
                        cums.append((bp, bb, rp, rb))
                    qk = []
                    for i in range(4):
                        tok = slice(i * 128, (i + 1) * 128)
                        bp, bb, rp, rb = cums[i]
                        expb, b_expb = tmpF.next()
                        T.op(act, lambda e: e.activation(out=expb[:], in_=bp[:, 0:512], func=AF.Exp, scale=-1.0 / 16), reads=[bb], writes=[b_expb])
                        expnb, b_expnb = tmpF.next()
                        T.op(act, lambda e: e.activation(out=expnb[:], in_=bp[:, 0:512], func=AF.Exp, scale=1.0 / 16), reads=[bb], writes=[b_expnb])
                        expr, b_expr = tmpF.next()
                        T.op(act, lambda e: e.activation(out=expr[:], in_=rp[:, 0:512], func=AF.Exp, scale=-1.0 / 16), reads=[rb], writes=[b_expr])
                        qdec, b_qdec = tmpB.next()
                        T.op(dve, lambda e: e.tensor_tensor(out=qdec[:].rearrange("q (h c) -> q h c", h=4), in0=qT[:, :, tok],
                                                            in1=expb[:].rearrange("q (h c) -> q h c", h=4), op=ALU.mult),
                             reads=[b_qT, b_expb], writes=[b_qdec])
                        T.op(dve, lambda e: e.tensor_copy(dec[:, i, :], expb[:].rearrange("q (h c) -> q h c", h=4)[:, :, 127]),
                             reads=[b_expb], writes=[b_dec])
                        kinv, b_kinv = tmpB.next()
                        T.op(dve, lambda e: e.tensor_tensor(out=kinv[:].rearrange("q (h c) -> q h c", h=4), in0=kT[:, :, tok],
                                                            in1=expnb[:].rearrange("q (h c) -> q h c", h=4), op=ALU.mult),
                             reads=[b_kT, b_expnb], writes=[b_kinv])
                        kend, b_kend = tmpB.next()
                        T.op(dve, lambda e: e.tensor_tensor(out=kend[:], in0=ktok[:, i, :], in1=expr[:], op=ALU.mult),
                             reads=[b_ktok, b_expr], writes=[b_kend])
                        qk.append((qdec, b_qdec, kinv, b_kinv, kend, b_kend))
                    atts = []
                    for i in range(4):
                        qdec, b_qdec, kinv, b_kinv, kend, b_kend = qk[i]
                        ap_, ab = T.ps()
                        T.mm([lambda e, h=h: e.matmul(ap_[:, h * 128:(h + 1) * 128], kinv[:, h * 128:(h + 1) * 128],
                                                      qdec[:, h * 128:(h + 1) * 128], start=True, stop=True) for h in range(4)],
                             reads=[b_kinv, b_qdec], writes=[ab])
                        atts.append((ap_, ab))
                    for i in range(4):
                        ap_, ab = atts[i]
                        attT, b_attT = qk[i][2], qk[i][3]
                        T.op(dve, lambda e: e.tensor_tensor(out=attT[:], in0=ap_[:, 0:512], in1=triU4_b[:], op=ALU.mult),
                             reads=[ab, b_c], writes=[b_attT])
                        atts[i] = (attT, b_attT)
                    pend = None
                    pend_b = None
                    T.ps_range = (4, 8)
                    for i in range(4):
                        qdec, b_qdec, kinv, b_kinv, kend, b_kend = qk[i]
                        attT, b_attT = atts[i]
                        ob = [T.psl[(i % 2) * 2], T.psl[(i % 2) * 2 + 1]]
                        for bk in range(2):
                            fns = []
                            for hh in range(2):
                                h = bk * 2 + hh
                                o_ap = ob[bk][0][:, hh * 256:(hh + 1) * 256]
                                fns.append(lambda e, h=h, o_ap=o_ap: e.matmul(o_ap, qdec[:, h * 128:(h + 1) * 128], Sbf[:, h * 256:(h + 1) * 256],
                                                                              start=True, stop=False))
                                fns.append(lambda e, h=h, o_ap=o_ap: e.matmul(o_ap, attT[:, h * 128:(h + 1) * 128], vtok[:, i, h * 256:(h + 1) * 256],
                                                                              start=False, stop=True))
                            T.mm(fns, reads=[b_qdec, b_Sbf, b_attT, b_vtok], writes=[ob[bk][1]])
                        db = [T.ps(), T.ps()]
                        for bk in range(2):
                            T.mm([lambda e, h=bk * 2 + hh, hh=hh: e.matmul(db[bk][0][:, hh * 256:(hh + 1) * 256], kend[:, h * 128:(h + 1) * 128],
                                                                           vtok[:, i, h * 256:(h + 1) * 256], start=True, stop=True)
                                  for hh in range(2)], reads=[b_kend, b_vtok], writes=[db[bk][1]])
                        osb_t, b_osb = osbr.next()
                        for bk in range(2):
                            T.op(act, lambda e, bk=bk: e.copy(osb_t[:, bk * 512:(bk + 1) * 512], ob[bk][0][:, 0:512]), reads=[ob[bk][1]], writes=[b_osb])
                        for h in range(4):
                            T.op(dve, lambda e, h=h: e.scalar_tensor_tensor(out=S[:, l, h * 256:(h + 1) * 256], in0=S[:, l, h * 256:(h + 1) * 256],
                                                                            scalar=dec[:, i, h:h + 1],
                                                                            in1=db[h // 2][0][:, (h % 2) * 256:(h % 2 + 1) * 256],
                                                                            op0=ALU.mult, op1=ALU.add),
                                 reads=[b_S, b_dec, db[h // 2][1]], writes=[b_S])
                        T.op(dve, lambda e: e.tensor_copy(Sbf[:], S[:, l, :]), reads=[b_S], writes=[b_Sbf])
                        if pend_b is not None:
                            pend_b()
                            pend_b = None
                        if pend is not None:
                            pend_b = pend()
                        pend = (lambda osb_t=osb_t, b_osb=b_osb, i=i: post(lambda h: osb_t[:, h * 256:(h + 1) * 256], [b_osb], 128,
                                                                           Xtok[:, i, :], b_Xtok, i * 128, i, split=True))
                    if pend_b is not None:
                        pend_b()
                    pend()()
                    T.ps_range = (0, 8)
                    if blk == NBLK - 1:
                        T.dma(sp, oglap[l].rearrange("h k v -> k h v"), S[:, l, :].rearrange("q (h v) -> q h v", h=4), b_S, reads=[b_S])

                    T.mark_fence()
                    ph2.close()
                    chk("glachunk_%d_%d" % (blk, l))
                    if has_s:
                        aTs, b_aTs = T.sb("aTs", [128, 64], F32, ph)
                        qmask, b_qmask = T.sb("qmask", [128, 4, NSMP, NSMP], BF16, ph)
                        s0r = Rot(T, "s0", 3, [128, D], F32, ph)
                        snr = Rot(T, "sn", 3, [128, D], F32, ph)
                        kmr = Rot(T, "km", 2, [NSMP, 512], BF16, ph)
                        snbr = Rot(T, "snb", 2, [128, D], BF16, ph)
                        zs, zsb = T.ps()
                        T.mm([lambda e, h=h: e.matmul(zs[:, h * 16:(h + 1) * 16], w_a2aug[0:17, l, h * 128:(h + 1) * 128], a_aug[0:17, TB:NT],
                                                      start=True, stop=True) for h in range(4)], reads=[b_aaug, b_wa2], writes=[zsb])
                        T.op(act, lambda e: e.activation(out=aTs[:], in_=zs[:, 0:64], func=AF.Exp, scale=-1.0), reads=[zsb], writes=[b_aTs])
                        T.op(act, lambda e: e.activation(out=aTs[:], in_=aTs[:], func=AF.Ln, bias=1.0, scale=1.0), reads=[b_aTs], writes=[b_aTs])
                        T.op(act, lambda e: e.activation(out=aTs[:], in_=aTs[:], func=AF.Exp, scale=-1.0 / 16), reads=[b_aTs], writes=[b_aTs])
                        T.op(dve, lambda e: e.memset(qmask[:], 0.0), writes=[b_qmask])
                        for b in range(NSMP):
                            T.op(dve, lambda e, b=b: e.tensor_copy(qmask[:, :, b, b], qTs_f[:, :, b]), reads=[b_qTs], writes=[b_qmask])
                        sst = {}

                        sld = {}

                        def stL(b):
                            s0, b_s0 = s0r.next()
                            T.dma(sp, s0[:].rearrange("q (h v) -> q h v", h=4), sg[l, b].rearrange("h k v -> k h v"), b_s0, writes=[b_s0])
                            sld[b] = (s0, b_s0)

                        def stA(b):
                            s0, b_s0 = sld.pop(b)
                            km, b_km = kmr.next()
                            T.op(dve, lambda e: e.tensor_scalar(out=km[:], in0=ks_f[:], scalar1=ident_f[0:NSMP, b:b + 1], scalar2=None,
                                                                op0=ALU.mult), reads=[b_ksf, b_c], writes=[b_km])
                            kvb = [T.ps(), T.ps()]
                            for bk in range(2):
                                T.mm([lambda e, h=bk * 2 + hh, hh=hh: e.matmul(kvb[bk][0][:, hh * 256:(hh + 1) * 256], km[0:NSMP, h * 128:(h + 1) * 128],
                                                                               vs_f[0:NSMP, h * 256:(h + 1) * 256], start=True, stop=True)
                                      for hh in range(2)], reads=[b_km, b_vsf], writes=[kvb[bk][1]])
                            sst[b] = [s0, b_s0, kvb]

                        def stB(b):
                            s0, b_s0, kvb = sst[b]
                            sn, b_sn = snr.next()
                            for h in range(4):
                                T.op(dve, lambda e, h=h: e.scalar_tensor_tensor(out=sn[:, h * 256:(h + 1) * 256], in0=s0[:, h * 256:(h + 1) * 256],
                                                                                scalar=aTs[:, h * 16 + b:h * 16 + b + 1],
                                                                                in1=kvb[h // 2][0][:, (h % 2) * 256:(h % 2 + 1) * 256],
                                                                                op0=ALU.mult, op1=ALU.add),
                                     reads=[b_s0, b_aTs, kvb[h // 2][1]], writes=[b_sn])
                            T.dma(sp, oglas[l, b].rearrange("h k v -> k h v"), sn[:].rearrange("q (h v) -> q h v", h=4), b_sn, reads=[b_sn])
                            snb, b_snb = snbr.next()
                            T.op(act, lambda e: e.copy(snb[:], sn[:]), reads=[b_sn], writes=[b_snb])
                            sst[b] = [snb, b_snb]

                        def stC(b):
                            snb, b_snb = sst.pop(b)
                            T.mm([lambda e, h=h: e.matmul(T.psl[h][0][0:NSMP, 0:256], qmask[:, h, b, :], snb[:, h * 256:(h + 1) * 256],
                                                          start=(b == 0), stop=(b == NSMP - 1)) for h in range(4)],
                                 reads=[b_qmask, b_snb], writes=[T.psl[h][1] for h in range(4)])

                        T.ps_range = (4, 8)
                        stL(0)
                        stL(1)
                        stA(0)
                        for b in range(NSMP):
                            if b + 2 < NSMP:
                                stL(b + 2)
                            if b + 1 < NSMP:
                                stA(b + 1)
                            stB(b)
                            if b >= 1:
                                stC(b - 1)
                        stC(NSMP - 1)
                        post(lambda h: T.psl[h][0][0:NSMP, 0:256], [T.psl[h][1] for h in range(4)], NSMP, Xs[:, :], b_Xs, TB, 4)
                        T.ps_range = (0, 8)
                    T.mark_fence()

                def merge(bi):
                    for j in range(2):
                        wt, wbuf = wget(("in", l, MG0 + bi * D + j * 512))

                        def ev_g(mi, t0, tn, ps_, pb):
                            T.op(act, lambda e: e.activation(out=gblk[:, mi, t0:t0 + tn], in_=ps_, func=AF.Sigmoid), reads=[pb], writes=[b_gblk])
                        bmode(wt, wbuf, M4, subs, hT, hb, ev_g)
                        wt, wbuf = wget(("br", l, bi, j))

                        def ev_m(mi, t0, tn, ps_, pb):
                            dst = mgT[:, 4 * j + mi, t0:t0 + tn]
                            mb = tb(mgb, t0, tn)
                            if bi == 0:
                                T.op(dve, lambda e: e.tensor_tensor(out=dst, in0=ps_, in1=gblk[:, mi, t0:t0 + tn], op=ALU.mult),
                                     reads=[pb, b_gblk], writes=mb)
                            else:
                                tm, b_tm = mtmp.next()
                                T.op(dve, lambda e: e.tensor_tensor(out=tm[:, 0:tn], in0=ps_, in1=gblk[:, mi, t0:t0 + tn], op=ALU.mult),
                                     reads=[pb, b_gblk], writes=[b_tm])
                                T.op(dve, lambda e: e.tensor_tensor(out=dst, in0=dst, in1=tm[:, 0:tn], op=ALU.add),
                                     reads=[b_tm] + mb, writes=mb)
                        bmode(wt, wbuf, M4, subs, brT, brb, ev_m)

                chk("glasample_%d_%d" % (blk, l))
                merge(0)
                chk("merge0_%d_%d" % (blk, l))

                with ExitStack() as ph:
                    L = 15 + TB
                    uT, b_uT = T.sb("uT", [128, 4, 15 + NT], F32, ph)
                    wa_, b_wa_ = T.sb("wa_", [128, L], F32, ph)
                    wb_, b_wb_ = T.sb("wb_", [128, L], F32, ph)
                    diffT, b_diffT = T.sb("diffT", [128, 8, NT], BF16, ph)
                    spg, b_spg = T.sb("spg", [128, 8, NT], BF16, ph)
                    us_f, b_usf = T.sb("us_f", [NSMP, D], F32, ph)
                    ulast, b_ulast = T.sb("ulast", [128, D], F32, ph)
                    t16, b_t16 = T.sb("t16", [128, 16], F32, ph)
                    mtmp = Rot(T, "mtmp", 2, [128, 512], BF16, ph)
                    for jb in range(2):
                        wt, wbuf = wget(("in", l, U0 + jb * 512))

                        def ev_u(mi, t0, tn, ps_, pb):
                            T.op(act, lambda e: e.copy(uT[:, mi, 15 + t0:15 + t0 + tn], ps_), reads=[pb], writes=[b_uT])
                        bmode(wt, wbuf, M4, psubs, hT, hb, ev_u)
                        if has_s:
                            pst, pb = amode(hT, [hb[4]], TB, NSMP, wt, wbuf)
                            T.op(act, lambda e: e.copy(us_f[:, jb * 512:(jb + 1) * 512], pst[0:NSMP, 0:512]), reads=[pb], writes=[b_usf])
                        if blk == NBLK - 1:
                            pst, pb = amode(hT, [hb[3]], 384, 128, wt, wbuf)
                            T.op(act, lambda e: e.copy(ulast[:, jb * 512:(jb + 1) * 512], pst[:, 0:512]), reads=[pb], writes=[b_ulast])
                        for mi in range(4):
                            ch = 4 * jb + mi
                            g = ch // 2
                            w = 2 << g
                            u = uT[:, mi, :]
                            T.op(dve, lambda e: e.tensor_copy(uT[:, mi, 0:15], uhist[:, l, ch, :]), reads=[b_uhist], writes=[b_uT])
                            T.op(dve, lambda e: e.tensor_tensor(out=wa_[:, 1:L], in0=u[:, 1:L], in1=u[:, 0:L - 1], op=ALU.add),
                                 reads=[b_uT], writes=[b_wa_])
                            s_ap, b_s = wa_, b_wa_
                            if g >= 1:
                                T.op(dve, lambda e: e.tensor_tensor(out=wb_[:, 3:L], in0=wa_[:, 3:L], in1=wa_[:, 1:L - 2], op=ALU.add),
                                     reads=[b_wa_], writes=[b_wb_])
                                s_ap, b_s = wb_, b_wb_
                            if g >= 2:
                                T.op(dve, lambda e: e.tensor_tensor(out=wa_[:, 7:L], in0=wb_[:, 7:L], in1=wb_[:, 3:L - 4], op=ALU.add),
                                     reads=[b_wb_], writes=[b_wa_])
                                s_ap, b_s = wa_, b_wa_
                            if g >= 3:
                                T.op(dve, lambda e: e.tensor_tensor(out=wb_[:, 15:L], in0=wa_[:, 15:L], in1=wa_[:, 7:L - 8], op=ALU.add),
                                     reads=[b_wa_], writes=[b_wb_])
                                s_ap, b_s = wb_, b_wb_
                            T.op(dve, lambda e: e.scalar_tensor_tensor(out=diffT[:, ch, 0:TB], in0=s_ap[:, 15:L], scalar=1.0 / w,
                                                                       in1=u[:, 15:L], op0=ALU.mult, op1=ALU.subtract),
                                 reads=[b_s, b_uT], writes=[b_diffT])
                            if blk == 0:
                                T.op(dve, lambda e: e.tensor_tensor(out=t16[:], in0=s_ap[:, 15:31], in1=rc[:, g, :], op=ALU.mult),
                                     reads=[b_s, b_c], writes=[b_t16])
                                T.op(dve, lambda e: e.tensor_tensor(out=diffT[:, ch, 0:16], in0=t16[:], in1=u[:, 15:31], op=ALU.subtract),
                                     reads=[b_t16, b_uT], writes=[b_diffT])
                            if blk < NBLK - 1:
                                T.op(dve, lambda e: e.tensor_copy(uhist[:, l, ch, :], uT[:, mi, TB:TB + 15]), reads=[b_uT], writes=[b_uhist])
                    if blk == NBLK - 1:
                        T.dma(sp, opoolp[l], ulast[113:128, :], b_ulast, reads=[b_ulast])
                    if has_s:
                        hbuf, b_hbuf = T.sb("hbuf", [NSMP, 15, 256], F32, ph)
                        hs, b_hs = T.sb("hs", [NSMP, 256], F32, ph)
                        dsb, b_dsb = T.sb("dsb", [NSMP, D], BF16, ph)
                        for g in range(4):
                            w = 2 << g
                            cs = slice(g * 256, (g + 1) * 256)
                            T.dma(sp, hbuf[:, 0:w - 1, :], spool[l, :, 15 - (w - 1):15, cs], b_hbuf, writes=[b_hbuf])
                            T.op(dve, lambda e: e.tensor_reduce(out=hs[:], in_=hbuf[:, 0:w - 1, :].rearrange("p r c -> p c r"), axis=AX.X, op=ALU.add),
                                 reads=[b_hbuf], writes=[b_hs])
                            T.op(dve, lambda e: e.tensor_tensor(out=hs[:], in0=hs[:], in1=us_f[:, cs], op=ALU.add), reads=[b_hs, b_usf], writes=[b_hs])
                            T.op(dve, lambda e: e.scalar_tensor_tensor(out=dsb[:, cs], in0=hs[:], scalar=1.0 / w, in1=us_f[:, cs],
                                                                       op0=ALU.mult, op1=ALU.subtract), reads=[b_hs, b_usf], writes=[b_dsb])
                        transpose8(dsb, b_dsb, NSMP, diffT[:, :, TB:NT], b_diffT)
                        T.dma(sp, opools[l, :, 14, :], us_f[:, :], b_usf, reads=[b_usf])
                        T.dma(sp, opools[l, :, 0:14, :], spool[l, :, 1:15, :], b_dd)
                    for jb in range(2):
                        wt, wbuf = wget(("in", l, PG0 + jb * 512))

                        def ev_pg(mi, t0, tn, ps_, pb):
                            T.op(act, lambda e: e.activation(out=spg[:, 4 * jb + mi, t0:t0 + tn], in_=ps_, func=AF.Silu), reads=[pb], writes=[b_spg])
                        bmode(wt, wbuf, M4, subs, hT, hb, ev_pg)
                    for jb in range(2):
                        for gg in range(2):
                            g = 2 * jb + gg
                            for ee in range(2):
                                ch = 2 * g + ee
                                for (t0, tn) in subs:
                                    pst, pb = T.ps()
                                    T.mm([lambda e, k2=k2: e.matmul(pst[:, 0:tn], pw[:, l, g, k2, ee * 128:(ee + 1) * 128],
                                                                    diffT[:, 2 * g + k2, t0:t0 + tn], start=(k2 == 0), stop=(k2 == 1))
                                          for k2 in range(2)], reads=[b_pw, b_diffT], writes=[pb])
                                    T.op(dve, lambda e: e.scalar_tensor_tensor(out=brT[:, ch, t0:t0 + tn], in0=pst[:, 0:tn],
                                                                               scalar=pscT[:, l, ch:ch + 1], in1=spg[:, ch, t0:t0 + tn],
                                                                               op0=ALU.mult, op1=ALU.mult),
                                         reads=[pb, b_pscT, b_spg], writes=tb(brb, t0, tn))
                    merge(1)
                    T.mark_fence()
                chk("pool_%d_%d" % (blk, l))

                with ExitStack() as ph:
                    xqb, b_xqb = T.sb("xqb", [128, 4, NT], BF16, ph)
                    sxg, b_sxg = T.sb("sxg", [128, 4, TB], BF16, ph)
                    sxgs, b_sxgs = T.sb("sxgs", [128, 8, NSMP], BF16, ph)
                    pTr = Rot(T, "pT", 2, [128, 2, TB], BF16, ph)
                    rinv, b_rinv = T.sb("rinv", [128, TB], F32, ph)
                    tmpx = Rot(T, "tmpx", 2, [128, TB], F32, ph)
                    stg = Rot(T, "stg", 2, [128, 512], F32, ph)
                    xqs_f, b_xqs = T.sb("xqs_f", [NSMP, D], BF16, ph)
                    mtmp = Rot(T, "mtmp", 2, [128, 512], BF16, ph)
                    if blk == 0:
                        for j in range(2):
                            wt, wbuf = wget(("mk", l, j))
                            for mt in range(2):
                                pst, pb = amode(memT, [b_memT], mt * 128, 128, wt, wbuf)
                                st_, b_st = stg.next()
                                T.op(act, lambda e: e.copy(st_[:], pst[:, 0:512]), reads=[pb], writes=[b_st])
                                T.dma(sp, omk[l, mt * 128:(mt + 1) * 128, j * 512:(j + 1) * 512], st_[:], b_st, reads=[b_st])

                            def ev_mk(mi, t0, tn, ps_, pb):
                                T.op(dve, lambda e: e.tensor_copy(mkT[:, l, 4 * j + mi, :], ps_), reads=[pb], writes=[b_mkT])
                            bmode(wt, wbuf, M4, [(0, 256)], memT, [b_memT] * 5, ev_mk)
                        for j in range(2):
                            wt, wbuf = wget(("mv", l, j))
                            for mt in range(2):
                                pst, pb = amode(memT, [b_memT], mt * 128, 128, wt, wbuf)
                                st_, b_st = stg.next()
                                T.op(act, lambda e: e.copy(st_[:], pst[:, 0:512]), reads=[pb], writes=[b_st])
                                T.dma(sp, omv[l, mt * 128:(mt + 1) * 128, j * 512:(j + 1) * 512], st_[:], b_st, reads=[b_st])
                                T.op(dve, lambda e: e.tensor_copy(mvb[:, l, mt, j * 512:(j + 1) * 512], pst[:, 0:512]), reads=[pb], writes=[b_mvb])
                    chk("xa1_%d_%d" % (blk, l))
                    for jb in range(2):
                        wt, wbuf = wget(("in", l, XQ0 + jb * 512))

                        def ev_xq(mi, t0, tn, ps_, pb):
                            T.op(act, lambda e: e.activation(out=xqb[:, mi, t0:t0 + tn], in_=ps_, func=AF.Copy, scale=1.0 / 16),
                                 reads=[pb], writes=[b_xqb])
                        bmode(wt, wbuf, M4, psubs, hT, hb, ev_xq)
                        if has_s:
                            pst, pb = amode(hT, [hb[4]], TB, NSMP, wt, wbuf)
                            T.op(act, lambda e: e.activation(out=xqs_f[:, jb * 512:(jb + 1) * 512], in_=pst[0:NSMP, 0:512], func=AF.Copy, scale=1.0 / 16),
                                 reads=[pb], writes=[b_xqs])
                        wt, wbuf = wget(("in", l, XG0 + jb * 512))

                        def ev_xg(mi, t0, tn, ps_, pb):
                            if t0 >= TB:
                                T.op(act, lambda e: e.activation(out=sxgs[:, 4 * jb + mi, :], in_=ps_, func=AF.Silu), reads=[pb], writes=[b_sxgs])
                            else:
                                T.op(act, lambda e: e.activation(out=sxg[:, mi, t0:t0 + tn], in_=ps_, func=AF.Silu), reads=[pb], writes=[b_sxg])
                        bmode(wt, wbuf, M4, subs, hT, hb, ev_xg)
                        hst = {}

                        def head_a(hh):
                            h = 2 * jb + hh
                            pT, b_pT = pTr.next()
                            for mc in range(2):
                                sp_, sb_ = T.ps()
                                T.mm([lambda e, dc=dc: e.matmul(sp_[:, 0:TB], mkT[:, l, 2 * h + dc, mc * 128:(mc + 1) * 128], xqb[:, 2 * hh + dc, 0:TB],
                                                                start=(dc == 0), stop=(dc == 1)) for dc in range(2)],
                                     reads=[b_mkT, b_xqb], writes=[sb_])
                                T.op(act, lambda e: e.activation(out=pT[:, mc, :], in_=sp_[:, 0:TB], func=AF.Exp), reads=[sb_], writes=[b_pT])
                            hst[hh] = (pT, b_pT)

                        def head_b(hh):
                            h = 2 * jb + hh
                            pT, b_pT = hst[hh]
                            sm_, smb = T.ps()
                            T.mm([lambda e, mc=mc: e.matmul(sm_[:, 0:TB], ones_b[:], pT[:, mc, :], start=(mc == 0), stop=(mc == 1)) for mc in range(2)],
                                 reads=[b_pT, b_c], writes=[smb])
                            T.op(act, lambda e: e.activation(out=rinv[:], in_=sm_[:, 0:TB], func=AF.Ln), reads=[smb], writes=[b_rinv])
                            T.op(act, lambda e: e.activation(out=rinv[:], in_=rinv[:], func=AF.Exp, scale=-1.0), reads=[b_rinv], writes=[b_rinv])
                            for dc in range(2):
                                op_, opb = T.ps()
                                T.mm([lambda e, mc=mc: e.matmul(op_[:, 0:TB], mvb[:, l, mc, (2 * h + dc) * 128:(2 * h + dc + 1) * 128], pT[:, mc, :],
                                                                start=(mc == 0), stop=(mc == 1)) for mc in range(2)],
                                     reads=[b_mvb, b_pT], writes=[opb])
                                tx, b_tx = tmpx.next()
                                T.op(dve, lambda e: e.tensor_tensor(out=tx[:], in0=op_[:, 0:TB], in1=rinv[:], op=ALU.mult),
                                     reads=[opb, b_rinv], writes=[b_tx])
                                T.op(dve, lambda e: e.tensor_tensor(out=brT[:, 2 * h + dc, 0:TB], in0=tx[:], in1=sxg[:, 2 * hh + dc, :], op=ALU.mult),
                                     reads=[b_tx, b_sxg], writes=brb[0:4])

                        head_a(0)
                        head_a(1)
                        head_b(0)
                        head_b(1)
                    chk("xa2_%d_%d" % (blk, l))
                    if has_s:
                        kbr = Rot(T, "kbuf", 4, [128, D], F32, ph)
                        vbuf = Rot(T, "vbuf", 2, [128, 2, D], BF16, ph)
                        vfr = kbr
                        prodr = Rot(T, "prod", 3, [128, 512], F32, ph)
                        sTall, b_sTall = T.sb("sTall", [128, 2, NSMP, 4], F32, ph)
                        stok, b_stok = T.sb("stok", [64, 256], F32, ph)
                        pnf, b_pnf = T.sb("pnf", [64, 256], F32, ph)
                        pn, b_pn = T.sb("pn", [64, 256], BF16, ph)
                        smx, b_smx = T.sb("smx", [64, 8], F32, ph)
                        pTs, b_pTs = T.sb("pTs", [128, 2, NSMP, 4], BF16, ph)
                        pmask, b_pmask = T.sb("pmask", [128, 2, 4, NSMP, NSMP], BF16, ph)
                        cb, b_cb = T.sb("cb", [NSMP, D], BF16, ph)
                        sel, b_sel = T.sb("sel", [NSMP, NSMP, 128], BF16, ph)
                        for b in range(NSMP):
                            T.op(dve, lambda e, b=b: e.tensor_scalar(out=sel[:, b, :], in0=ones_f[0:NSMP, :], scalar1=ident_f[0:NSMP, b:b + 1],
                                                                     scalar2=None, op0=ALU.mult), reads=[b_c], writes=[b_sel])
                        for b in range(NSMP):
                            qb = [T.ps(), T.ps()]
                            for bk in range(2):
                                T.mm([lambda e, bk=bk, b=b: e.matmul(qb[bk][0][:, 0:512], sel[0:NSMP, b, :], xqs_f[0:NSMP, bk * 512:(bk + 1) * 512],
                                                                     start=True, stop=True)], reads=[b_sel, b_xqs], writes=[qb[bk][1]])
                            for mt in range(2):
                                kb, b_kb = kbr.next()
                                T.dma(sp, kb[:], ck[l, b, mt * 128:(mt + 1) * 128, :], b_kb, writes=[b_kb])
                                pa, b_pa = prodr.next()
                                pb_, b_pb = prodr.next()
                                T.op(dve, lambda e: e.tensor_tensor(out=pa[:], in0=qb[0][0][:, 0:512], in1=kb[:, 0:512], op=ALU.mult),
                                     reads=[qb[0][1], b_kb], writes=[b_pa])
                                T.op(dve, lambda e: e.tensor_tensor(out=pb_[:], in0=qb[1][0][:, 0:512], in1=kb[:, 512:1024], op=ALU.mult),
                                     reads=[qb[1][1], b_kb], writes=[b_pb])
                                T.op(dve, lambda e, mt=mt, b=b: e.tensor_reduce(out=sTall[:, mt, b, 0:2], in_=pa[:].rearrange("q (h d) -> q h d", h=2),
                                                                                axis=AX.X, op=ALU.add), reads=[b_pa], writes=[b_sTall])
                                for hh in range(2):
                                    T.op(act, lambda e, mt=mt, b=b, hh=hh: e.activation(out=junk[:, 0:256], in_=pb_[:, hh * 256:(hh + 1) * 256], func=AF.Copy,
                                                                                         accum_out=sTall[:, mt, b, 2 + hh:3 + hh]),
                                         reads=[b_pb], writes=[b_junk, b_sTall])
                        chk("xa3_%d_%d" % (blk, l))
                        for mt in range(2):
                            tp_, tpb = T.ps()
                            T.mm([lambda e, mt=mt: e.transpose(tp_[0:64, 0:128], sTall[:, mt].rearrange("q b h -> q (b h)"), ident_f[:])],
                                 reads=[b_sTall, b_c], writes=[tpb])
                            T.op(act, lambda e, mt=mt: e.copy(stok[:, mt * 128:(mt + 1) * 128], tp_[0:64, 0:128]), reads=[tpb], writes=[b_stok])
                        T.op(dve, lambda e: e.reduce_max(out=smx[:, 0:1], in_=stok[:], axis=AX.X), reads=[b_stok], writes=[b_smx])
                        T.op(dve, lambda e: e.tensor_scalar(out=smx[:, 1:2], in0=smx[:, 0:1], scalar1=-1.0, scalar2=None, op0=ALU.mult),
                             reads=[b_smx], writes=[b_smx])
                        T.op(act, lambda e: e.activation(out=pnf[:], in_=stok[:], func=AF.Exp, bias=smx[:, 1:2], scale=1.0, accum_out=smx[:, 2:3]),
                             reads=[b_stok, b_smx], writes=[b_pnf, b_smx])
                        T.op(dve, lambda e: e.reciprocal(out=smx[:, 3:4], in_=smx[:, 2:3]), reads=[b_smx], writes=[b_smx])
                        T.op(dve, lambda e: e.tensor_scalar(out=pn[:], in0=pnf[:], scalar1=smx[:, 3:4], scalar2=None, op0=ALU.mult),
                             reads=[b_pnf, b_smx], writes=[b_pn])
                        for mt in range(2):
                            tp_, tpb = T.ps()
                            tpb16 = tp_[:].bitcast(BF16)
                            T.mm([lambda e, mt=mt: e.transpose(tpb16[:, 0:64], pn[0:64, mt * 128:(mt + 1) * 128], ident_b[0:64, 0:64])],
                                 reads=[b_pn, b_c], writes=[tpb])
                            T.op(dve, lambda e, mt=mt: e.tensor_copy(pTs[:, mt].rearrange("q b h -> q (b h)"), tpb16[:, 0:64]),
                                 reads=[tpb], writes=[b_pTs])
                        T.op(dve, lambda e: e.memset(pmask[:], 0.0), writes=[b_pmask])
                        for b in range(NSMP):
                            T.op(dve, lambda e, b=b: e.tensor_copy(pmask[:, :, :, b, b], pTs[:, :, b, :]), reads=[b_pTs], writes=[b_pmask])
                        chk("xa4_%d_%d" % (blk, l))
                        T.ps_range = (4, 8)
                        for b in range(NSMP):
                            vb, b_vb = vbuf.next()
                            for mt in range(2):
                                vf, b_vf = vfr.next()
                                T.dma(sp, vf[:], cv[l, b, mt * 128:(mt + 1) * 128, :], b_vf, writes=[b_vf])
                                T.op(act, lambda e, mt=mt: e.copy(vb[:, mt, :], vf[:]), reads=[b_vf], writes=[b_vb])
                            for h in range(4):
                                T.mm([lambda e, h=h, mt=mt, b=b: e.matmul(T.psl[h][0][0:NSMP, 0:256], pmask[:, mt, h, b, :], vb[:, mt, h * 256:(h + 1) * 256],
                                                                          start=(b == 0 and mt == 0), stop=(b == NSMP - 1 and mt == 1))
                                      for mt in range(2)], reads=[b_pmask, b_vb], writes=[T.psl[h][1]])
                        for h in range(4):
                            T.op(dve, lambda e, h=h: e.tensor_copy(cb[:, h * 256:(h + 1) * 256], T.psl[h][0][0:NSMP, 0:256]),
                                 reads=[T.psl[h][1]], writes=[b_cb])
                        T.ps_range = (0, 8)

                        def ev_c(v, pb):
                            T.op(dve, lambda e: e.tensor_tensor(out=brT[:, :, TB:NT], in0=v, in1=sxgs[:], op=ALU.mult),
                                 reads=[pb, b_sxgs], writes=[brb[4]])
                        transpose8(cb, b_cb, NSMP, None, None, evac=ev_c)
                    chk("xattn_%d_%d" % (blk, l))
                    merge(2)

                    for j in range(2):
                        wt, wbuf = wget(("out", l, j))
                        for (i, p, c0) in tiles:
                            pst, pb = amode(mgT, [mgb[i]], c0, p, wt, wbuf)
                            xa = xtile(i)[:, j * 512:(j + 1) * 512] if i < 4 else xS[:, j * 512:(j + 1) * 512]
                            T.op(dve, lambda e: e.tensor_tensor(out=xa, in0=xa, in1=pst[0:p, 0:512], op=ALU.add), reads=[pb, xb[i]], writes=[xb[i]])
                    T.mark_fence()
                chk("layer_%d_%d" % (blk, l))

            phf = ExitStack()
            ytr = Rot(T, "yt", 2, [128, D], F32, phf)
            fgbc, b_fgbc = T.sb("fgbc", [128, D], F32, phf)
            T.dma(sp, fgbc[:], final_gain.partition_broadcast(128), b_fgbc, writes=[b_fgbc])
            sts = [rms_stats(xtile(i), xb[i], p, D, ssr.next()) for (i, p, c0) in tiles]
            for k_, (i, p, c0) in enumerate(tiles):
                rstd, b_ss = sts[k_]
                yt, b_yt = ytr.next()
                T.op(dve, lambda e: e.scalar_tensor_tensor(out=yt[0:p, :], in0=xtile(i), scalar=rstd, in1=fgbc[0:p, :], op0=ALU.mult, op1=ALU.mult),
                     reads=[xb[i], b_ss, b_fgbc], writes=[b_yt])
                if i < 4:
                    T.dma(sp, yp[blk * TB + i * 128: blk * TB + (i + 1) * 128, :], yt[:], b_yt, reads=[b_yt])
                    if blk + 1 < NBLK:
                        T.dma(sp, xP[:, i, :], xp[(blk + 1) * TB + i * 128: (blk + 1) * TB + (i + 1) * 128, :], xb[i], writes=[xb[i]])
                else:
                    T.dma(sp, ys[:, :], yt[0:NSMP, :], b_yt, reads=[b_yt])
            T.mark_fence()
            phf.close()
        T.finish()


_CACHE = {}


def _build():
    if "nc" not in _CACHE:
        nc0 = bass.Bass("TRN2", target_bir_lowering=False)
        seq = program(nc0, None)
        nc = bass.Bass("TRN2", target_bir_lowering=False)
        program(nc, seq)
        _CACHE["nc"] = nc
    return _CACHE["nc"]


def kernel(x_prompt, x_sample, mem_prompt, cache_mem_k, cache_mem_v, state_gla, state_pool,
           w_in, w_a2, b_a, gla_gain, pool_w, pool_scale, w_mk, w_mv, w_branch, w_out,
           norm_gain, final_gain):
    f = lambda a: np.ascontiguousarray(np.asarray(a, dtype=np.float32))
    x_prompt, x_sample, mem_prompt = f(x_prompt), f(x_sample), f(mem_prompt)
    cache_mem_k, cache_mem_v, state_gla, state_pool = f(cache_mem_k), f(cache_mem_v), f(state_gla), f(state_pool)
    shared = dict(w_in=f(w_in), w_a2=f(w_a2), b_a=f(b_a), gla_gain=f(gla_gain), pool_w=f(pool_w), pool_scale=f(pool_scale),
                  w_mk=f(w_mk), w_mv=f(w_mv), w_branch=f(w_branch), w_out=f(w_out), norm_gain=f(norm_gain), final_gain=f(final_gain))
    nc = _build()
    in_maps = []
    for c in range(8):
        s = slice(c * NSMP, (c + 1) * NSMP)
        m = dict(shared)
        m["xp"] = x_prompt[c]
        m["xs"] = np.ascontiguousarray(x_sample[s, 0, :])
        m["mem"] = mem_prompt[c]
        m["ck"] = np.ascontiguousarray(cache_mem_k[:, s].reshape(2, NSMP, 256, D))
        m["cv"] = np.ascontiguousarray(cache_mem_v[:, s].reshape(2, NSMP, 256, D))
        m["sg"] = np.ascontiguousarray(state_gla[:, s])
        m["spool"] = np.ascontiguousarray(state_pool[:, s])
        in_maps.append(m)
    res = run_bass_kernel_spmd(nc, in_maps, core_ids=list(range(8)))
    R = res.results
    y_prompt = np.stack([R[c]["yp"] for c in range(8)], axis=0)
    y_sample = np.concatenate([R[c]["ys"] for c in range(8)], axis=0).reshape(128, 1, D)
    new_mk = np.stack([R[c]["omk"] for c in range(8)], axis=1).reshape(2, 8, 256, 4, 256)
    new_mv = np.stack([R[c]["omv"] for c in range(8)], axis=1).reshape(2, 8, 256, 4, 256)
    new_glap = np.stack([R[c]["oglap"] for c in range(8)], axis=1)
    new_poolp = np.stack([R[c]["opoolp"] for c in range(8)], axis=1)
    new_glas = np.concatenate([R[c]["oglas"] for c in range(8)], axis=1)
    new_pools = np.concatenate([R[c]["opools"] for c in range(8)], axis=1)
    out = (y_prompt, y_sample, new_mk, new_mv, new_glap, new_poolp, new_glas, new_pools)
    return tuple(np.ascontiguousarray(o, dtype=np.float32) for o in out)
```

```python
import numpy as np
from contextlib import ExitStack
import concourse.bass as bass
import concourse.mybir as mybir
from concourse.bass_utils import run_bass_kernel_spmd

F32 = mybir.dt.float32
BF16 = mybir.dt.bfloat16
AF = mybir.ActivationFunctionType
ALU = mybir.AluOpType
AX = mybir.AxisListType

D = 1024
NIN = 10256
TB = 512
NBLK = 4
NSMP = 16
NT = TB + NSMP
Q0, K0, V0, GG0, A0, U0, PG0, XQ0, XG0, MG0 = 0, 512, 1024, 2048, 3072, 3088, 4112, 5136, 6160, 7184
EPS = 1e-6
NSLOT = 4
PREF = 3


class Buf:
    __slots__ = ("name", "w", "r", "excl")

    def __init__(self, name, excl=False):
        self.name = name
        self.w = None
        self.r = {}
        self.excl = excl


class Eng:
    def __init__(self, T, e, name):
        self.T, self.e, self.name = T, e, name
        self.sem = T.newsem("e_" + name)
        self.n = 0
        self.seen = {}

    def wait(self, ev):
        sem, val = ev
        k = id(sem)
        if self.seen.get(k, 0) >= val:
            return
        self.seen[k] = val
        if not self.T.dry:
            self.e.wait_ge(sem, val)


class Tracker:
    def __init__(self, nc, es, dry):
        self.nc, self.es, self.dry = nc, es, dry
        self.pe = Eng(self, nc.tensor, "pe")
        self.act = Eng(self, nc.scalar, "act")
        self.dve = Eng(self, nc.vector, "dve")
        self.pool = Eng(self, nc.gpsimd, "pool")
        self.sp = Eng(self, nc.sync, "sp")
        self.dsems = {}
        self.psl = []
        self.psi = 0
        self.ps_range = (0, 8)
        self.stopped = False
        self.fence = {}
        self.store_sems = {}

    def newsem(self, name):
        if self.dry:
            return object()
        return self.es.enter_context(self.nc.semaphore(name))

    def sb(self, name, shape, dt, stack=None):
        self.uid = getattr(self, "uid", 0) + 1
        nm = "%s_%d" % (name, self.uid)
        t = (stack or self.es).enter_context(self.nc.sbuf_tensor(nm, shape, dt))
        b = Buf(nm)
        if stack is not None:
            b.r = dict(self.fence)
        return t, b

    def init_psum(self):
        for i in range(8):
            t = self.es.enter_context(self.nc.psum_tensor("ps%d" % i, [128, 512], F32))
            self.psl.append((t, Buf("ps%d" % i, excl=True)))

    def ps(self):
        lo, hi = self.ps_range
        r = self.psl[lo + self.psi % (hi - lo)]
        self.psi += 1
        return r

    def _deps(self, eng, reads, writes):
        for b in reads:
            if b.w is not None:
                eng.wait(b.w)
            if b.excl:
                me = id(eng.sem)
                for k, ev in b.r.items():
                    if k != me:
                        eng.wait(ev)
        for b in writes:
            if b.w is not None:
                eng.wait(b.w)
            for ev in b.r.values():
                eng.wait(ev)

    def _mark(self, ev, reads, writes):
        k = id(ev[0])
        for b in reads:
            b.r[k] = ev
        for b in writes:
            b.w = ev
            b.r = {}

    def op(self, eng, fn, reads=(), writes=()):
        if self.stopped:
            return
        self._deps(eng, reads, writes)
        eng.n += 1
        if not self.dry:
            fn(eng.e).then_inc(eng.sem, 1)
        self._mark((eng.sem, eng.n), reads, writes)

    def mm(self, fns, reads=(), writes=()):
        if self.stopped:
            return
        eng = self.pe
        self._deps(eng, reads, writes)
        eng.n += 1
        if not self.dry:
            ins = None
            for fn in fns:
                ins = fn(eng.e)
            ins.then_inc(eng.sem, 1)
        self._mark((eng.sem, eng.n), reads, writes)

    def dma(self, eng, out, in_, key, reads=(), writes=()):
        if self.stopped:
            return
        self._deps(eng, reads, writes)
        if id(key) not in self.dsems:
            self.dsems[id(key)] = [self.newsem("d_" + key.name), 0]
        st = self.dsems[id(key)]
        st[1] += 16
        if len(reads) > 0:
            self.store_sems[id(st[0])] = st
        if not self.dry:
            eng.e.dma_start(out=out, in_=in_).then_inc(st[0], 16)
        self._mark((st[0], st[1]), reads, writes)

    def barrier(self, with_pool=False):
        if self.stopped:
            return
        engs = [self.pe, self.act, self.dve, self.sp]
        allv = engs + [self.pool]
        for e in (allv if with_pool else engs):
            for o in allv:
                if o.n > 0:
                    e.wait((o.sem, o.n))
            for st in self.dsems.values():
                e.wait((st[0], st[1]))

    def mark_fence(self):
        if self.stopped:
            return
        f = {}
        for o in [self.pe, self.act, self.dve, self.sp, self.pool]:
            if o.n > 0:
                f[id(o.sem)] = (o.sem, o.n)
        for st in self.store_sems.values():
            f[id(st[0])] = (st[0], st[1])
        self.fence = f

    def finish(self):
        self.barrier()


class Rot:
    def __init__(self, T, name, n, shape, dt, stack=None):
        self.items = [T.sb("%s%d" % (name, i), shape, dt, stack) for i in range(n)]
        self.i = 0

    def next(self):
        r = self.items[self.i % len(self.items)]
        self.i += 1
        return r


class _Stop(Exception):
    pass


STOP = None


def program(nc, wseq):
    dry = wseq is None
    rec = []
    es = ExitStack()
    with es:
      T = None
      try:
        _program(nc, wseq, dry, rec, es)
      except _Stop:
        pass
    return rec


def _program(nc, wseq, dry, rec, es):
    if True:
        es.enter_context(nc.allow_non_contiguous_dma(reason="small strided parameter loads"))
        es.enter_context(nc.allow_low_precision(reason="bf16 matmul operands, fp32 accumulate"))
        T = Tracker(nc, es, dry)
        pe, act, dve, pool, sp = T.pe, T.act, T.dve, T.pool, T.sp

        def chk(name):
            if STOP == name and not T.stopped:
                T.finish()
                T.stopped = True

        def din(name, shape):
            return nc.dram_tensor(name, shape, F32, kind="ExternalInput").ap()

        def dout(name, shape):
            return nc.dram_tensor(name, shape, F32, kind="ExternalOutput").ap()

        xp = din("xp", [2048, D]); xs = din("xs", [NSMP, D]); mem = din("mem", [256, D])
        ck = din("ck", [2, NSMP, 256, D]); cv = din("cv", [2, NSMP, 256, D])
        sg = din("sg", [2, NSMP, 4, 128, 256]); spool = din("spool", [2, NSMP, 15, D])
        w_in = din("w_in", [2, D, NIN]); w_a2 = din("w_a2", [2, 16, 512]); b_a = din("b_a", [2, 512])
        gla_gain = din("gla_gain", [2, D]); pool_w = din("pool_w", [2, 4, 256, 256])
        pool_scale = din("pool_scale", [2, D]); w_mk = din("w_mk", [2, D, D]); w_mv = din("w_mv", [2, D, D])
        w_branch = din("w_branch", [2, 3, D, D]); w_out = din("w_out", [2, D, D])
        norm_gain = din("norm_gain", [2, D]); final_gain = din("final_gain", [D])
        yp = dout("yp", [2048, D]); ys = dout("ys", [NSMP, D]); omk = dout("omk", [2, 256, D]); omv = dout("omv", [2, 256, D])
        oglap = dout("oglap", [2, 4, 128, 256]); opoolp = dout("opoolp", [2, 15, D])
        oglas = dout("oglas", [2, NSMP, 4, 128, 256]); opools = dout("opools", [2, NSMP, 15, D])

        T.init_psum()

        xP, _ = T.sb("xP", [128, 4, D], F32)
        xb = [Buf("xb%d" % i) for i in range(5)]
        xS, _ = T.sb("xS", [NSMP, D], F32)
        hT, _ = T.sb("hT", [128, 8, NT], BF16); hb = [Buf("hb%d" % i) for i in range(5)]
        brT, _ = T.sb("brT", [128, 8, NT], BF16); brb = [Buf("brb%d" % i) for i in range(5)]
        mgT, _ = T.sb("mgT", [128, 8, NT], BF16); mgb = [Buf("mgb%d" % i) for i in range(5)]
        gblk, b_gblk = T.sb("gblk", [128, 4, NT], BF16)
        slots = [T.sb("wslot%d" % i, [128, 8, 512], BF16) for i in range(NSLOT)]
        S, b_S = T.sb("S", [128, 2, D], F32)
        Sbf, b_Sbf = T.sb("Sbf", [128, D], BF16)
        mkT, b_mkT = T.sb("mkT", [128, 2, 8, 256], BF16)
        mvb, b_mvb = T.sb("mvb", [128, 2, 2, D], BF16)
        memT, b_memT = T.sb("memT", [128, 8, 256], BF16)
        ones_f, b_c = T.sb("ones_f", [128, 128], F32)
        triU_f, _ = T.sb("triU_f", [128, 128], F32)
        triLs_f, _ = T.sb("triLs_f", [128, 128], F32)
        ident_f, _ = T.sb("ident_f", [128, 128], F32)
        ident_b, _ = T.sb("ident_b", [128, 128], BF16)
        ones_b, _ = T.sb("ones_b", [128, 128], BF16)
        triU4_b, _ = T.sb("triU4_b", [128, 512], BF16)
        rc, _ = T.sb("rc", [128, 4, 16], F32)
        a_aug, b_aaug = T.sb("a_aug", [17, NT], F32)
        w_a2aug, b_wa2 = T.sb("w_a2aug", [17, 2, 512], F32)
        wa, b_wa = T.sb("wa", [128, 2, 8, 16], BF16)
        pw, b_pw = T.sb("pw", [128, 2, 4, 2, 256], BF16)
        gainT, b_gainT = T.sb("gainT", [128, 2, 8], F32)
        pscT, b_pscT = T.sb("pscT", [128, 2, 8], F32)
        uhist, b_uhist = T.sb("uhist", [128, 2, 8, 15], F32)
        junk, b_junk = T.sb("junk", [128, D], BF16)
        ssr = Rot(T, "ss", 6, [128, 16], F32)
        b_dd = Buf("dram2dram")

        class WS:
            pos = 0
            issued = 0

        def wsrc(key):
            kind = key[0]
            if kind == "in":
                src = w_in[key[1], :, key[2]:key[2] + 512]
            elif kind == "br":
                src = w_branch[key[1], key[2], :, key[3] * 512:(key[3] + 1) * 512]
            elif kind == "out":
                src = w_out[key[1], :, key[2] * 512:(key[2] + 1) * 512]
            elif kind == "mk":
                src = w_mk[key[1], :, key[2] * 512:(key[2] + 1) * 512]
            else:
                src = w_mv[key[1], :, key[2] * 512:(key[2] + 1) * 512]
            return src.rearrange("(k p) n -> p k n", p=128)

        def wget(key):
            if T.stopped:
                return slots[0]
            i = WS.pos
            WS.pos += 1
            if dry:
                rec.append(key)
                return slots[0]
            assert wseq[i] == key, (i, wseq[i], key)
            while WS.issued < len(wseq) and WS.issued <= i + PREF:
                j = WS.issued
                st, sbuf_ = slots[j % NSLOT]
                T.dma(pool, st[:], wsrc(wseq[j]), sbuf_, writes=[sbuf_])
                WS.issued += 1
            return slots[i % NSLOT]

        T.op(dve, lambda e: e.memset(ones_f[:], 1.0), writes=[b_c])
        T.op(dve, lambda e: e.memset(ones_b[:], 1.0), writes=[b_c])
        T.op(pool, lambda e: e.affine_select(out=triU_f[:], in_=ones_f[:], pattern=[[1, 128]], compare_op=ALU.is_ge,
                                             fill=0.0, base=0, channel_multiplier=-1), reads=[b_c], writes=[b_c])
        T.op(pool, lambda e: e.affine_select(out=triLs_f[:], in_=ones_f[:], pattern=[[-1, 128]], compare_op=ALU.is_gt,
                                             fill=0.0, base=0, channel_multiplier=1), reads=[b_c], writes=[b_c])
        T.op(pool, lambda e: e.affine_select(out=ident_f[:], in_=ones_f[:], pattern=[[1, 128]], compare_op=ALU.is_equal,
                                             fill=0.0, base=0, channel_multiplier=-1), reads=[b_c], writes=[b_c])
        T.op(pool, lambda e: e.affine_select(out=ident_b[:], in_=ones_f[:], pattern=[[1, 128]], compare_op=ALU.is_equal,
                                             fill=0.0, base=0, channel_multiplier=-1), reads=[b_c], writes=[b_c])
        for h in range(4):
            T.op(pool, lambda e, h=h: e.affine_select(out=triU4_b[:, h * 128:(h + 1) * 128], in_=ones_f[:], pattern=[[1, 128]],
                                                      compare_op=ALU.is_ge, fill=0.0, base=0, channel_multiplier=-1),
                 reads=[b_c], writes=[b_c])
        for g in range(4):
            w = 2 << g
            T.op(dve, lambda e, g=g, w=w: e.memset(rc[:, g, :], 1.0 / w), writes=[b_c])
            for t in range(w - 1):
                T.op(dve, lambda e, g=g, t=t: e.memset(rc[:, g, t:t + 1], 1.0 / (t + 1)), writes=[b_c])
        T.op(dve, lambda e: e.memset(a_aug[:], 1.0), writes=[b_aaug])
        T.op(dve, lambda e: e.memset(S[:], 0.0), writes=[b_S])
        T.op(dve, lambda e: e.memset(uhist[:], 0.0), writes=[b_uhist])
        for l in range(2):
            T.dma(sp, w_a2aug[0:16, l, :], w_a2[l], b_wa2, writes=[b_wa2])
            T.dma(sp, w_a2aug[16:17, l, :], b_a[l:l + 1, :], b_wa2, writes=[b_wa2])
            T.dma(pool, wa[:, l], w_in[l, :, A0:A0 + 16].rearrange("(k p) n -> p k n", p=128), b_wa, writes=[b_wa])
            for g in range(4):
                T.dma(pool, pw[:, l, g], pool_w[l, g].rearrange("(k p) e -> p k e", p=128), b_pw, writes=[b_pw])
            T.dma(sp, gainT[:, l, :], norm_gain[l].rearrange("(k p) -> p k", p=128), b_gainT, writes=[b_gainT])
            T.dma(sp, pscT[:, l, :], pool_scale[l].rearrange("(k p) -> p k", p=128), b_pscT, writes=[b_pscT])

        def tb(bufs, t0, tn):
            if t0 >= TB:
                return [bufs[4]]
            return bufs[t0 // 128:(t0 + tn + 127) // 128]

        def transpose8(src, b_src, p, dst3, b_dst, evac=None):
            pst, pb = T.ps()
            psb = pst[:].bitcast(BF16)
            T.mm([lambda e, j=j: e.transpose(psb[:, j * 128:j * 128 + p], src[0:p, j * 128:(j + 1) * 128], ident_b[0:p, 0:p])
                  for j in range(8)], reads=[b_src, b_c], writes=[pb])
            v = psb.rearrange("q (j t) -> q j t", t=128)[:, :, 0:p]
            if evac is None:
                T.op(act, lambda e: e.copy(dst3, v), reads=[pb], writes=[b_dst])
            else:
                evac(v, pb)

        def rms_stats(xap, b_x, p, n, ssl):
            ss, b_ss = ssl
            T.op(act, lambda e: e.activation(out=junk[0:p, 0:n], in_=xap, func=AF.Square, accum_out=ss[0:p, 0:1]),
                 reads=[b_x], writes=[b_junk, b_ss])
            T.op(act, lambda e: e.activation(out=ss[0:p, 1:2], in_=ss[0:p, 0:1], func=AF.Sqrt, scale=1.0 / n, bias=EPS),
                 reads=[b_ss], writes=[b_ss])
            T.op(dve, lambda e: e.reciprocal(out=ss[0:p, 2:3], in_=ss[0:p, 1:2]), reads=[b_ss], writes=[b_ss])
            return ss[0:p, 2:3], b_ss

        def bmode(wt, wbuf, ms, subs, rhs3, rbufs, evac, nk=8, lhs_fn=None):
            for (mi, c0, mc) in ms:
                for (t0, tn) in subs:
                    pst, pb = T.ps()
                    T.mm([lambda e, kc=kc: e.matmul(pst[0:mc, 0:tn],
                                                    (lhs_fn(kc, c0, mc) if lhs_fn else wt[:, kc, c0:c0 + mc]),
                                                    rhs3[:, kc, t0:t0 + tn], start=(kc == 0), stop=(kc == nk - 1))
                          for kc in range(nk)], reads=[wbuf] + tb(rbufs, t0, tn), writes=[pb])
                    evac(mi, t0, tn, pst[0:mc, 0:tn], pb)

        def amode(lhs3, lbufs, c0, p, wt, wbuf, ncols=512):
            pst, pb = T.ps()
            T.mm([lambda e, kc=kc: e.matmul(pst[0:p, 0:ncols], lhs3[:, kc, c0:c0 + p], wt[:, kc, 0:ncols],
                                            start=(kc == 0), stop=(kc == 7)) for kc in range(8)],
                 reads=[wbuf] + lbufs, writes=[pb])
            return pst, pb

        M4 = [(m, m * 128, 128) for m in range(4)]

        with ExitStack() as ph:
            memf, b_memf = T.sb("memf", [128, 2, D], F32, ph)
            memb, b_memb = T.sb("memb", [128, 2, D], BF16, ph)
            T.dma(sp, memf[:], mem.rearrange("(t p) d -> p t d", p=128), b_memf, writes=[b_memf])
            T.op(dve, lambda e: e.tensor_copy(memb[:], memf[:]), reads=[b_memf], writes=[b_memb])
            for mt in range(2):
                transpose8(memb[:, mt, :], b_memb, 128, memT[:, :, mt * 128:(mt + 1) * 128], b_memT)
            T.mark_fence()

        chk("setup")
        for blk in range(NBLK):
            has_s = blk == 0
            tiles = [(i, 128, i * 128) for i in range(4)] + ([(4, NSMP, TB)] if has_s else [])
            subs = [(0, TB)] + ([(TB, NSMP)] if has_s else [])
            psubs = [(0, TB)]

            def xtile(i):
                return (xP[:, i, :] if i < 4 else xS[:, :])

            if blk == 0:
                for i in range(4):
                    T.dma(sp, xP[:, i, :], xp[blk * TB + i * 128: blk * TB + (i + 1) * 128, :], xb[i], writes=[xb[i]])
            if has_s:
                T.dma(sp, xS[:, :], xs[:, :], xb[4], writes=[xb[4]])

            for l in range(2):
                ph0 = ExitStack()
                xnr = Rot(T, "xn", len(tiles), [128, D], BF16, ph0)
                gbc, b_gbc = T.sb("gbc", [128, D], F32, ph0)
                T.dma(sp, gbc[:], norm_gain[l].partition_broadcast(128), b_gbc, writes=[b_gbc])
                sts = [rms_stats(xtile(i), xb[i], p, D, ssr.next()) for (i, p, c0) in tiles]
                xns = []
                for k_, (i, p, c0) in enumerate(tiles):
                    rstd, b_ss = sts[k_]
                    xn, b_xn = xnr.next()
                    T.op(dve, lambda e: e.scalar_tensor_tensor(out=xn[0:p, :], in0=xtile(i), scalar=rstd, in1=gbc[0:p, :],
                                                               op0=ALU.mult, op1=ALU.mult),
                         reads=[xb[i], b_ss, b_gbc], writes=[b_xn])
                    xns.append((xn, b_xn))
                for k_, (i, p, c0) in enumerate(tiles):
                    xn, b_xn = xns[k_]
                    transpose8(xn, b_xn, p, hT[:, :, c0:c0 + p], hb[i])
                T.mark_fence()
                ph0.close()
                chk("prenorm_%d_%d" % (blk, l))

                with ExitStack() as ph:
                    ks_f, b_ksf = T.sb("ks_f", [NSMP, 512], BF16, ph)
                    vs_f, b_vsf = T.sb("vs_f", [NSMP, D], BF16, ph)
                    Xs, b_Xs = T.sb("Xs", [NSMP, D], BF16, ph)
                    qTs_f, b_qTs = T.sb("qTs_f", [128, 4, NSMP], F32, ph)
                    brr = Rot(T, "br", 2, [128, D], BF16, ph)
                    ph2 = ExitStack()
                    qT, b_qT = T.sb("qT", [128, 4, NT], BF16, ph2)
                    kT, b_kT = T.sb("kT", [128, 4, NT], BF16, ph2)
                    ktok, b_ktok = T.sb("ktok", [128, 4, 512], BF16, ph2)
                    vtok, b_vtok = T.sb("vtok", [128, 4, D], BF16, ph2)
                    Xtok, b_Xtok = T.sb("Xtok", [128, 4, D], BF16, ph2)
                    tmpF = Rot(T, "tmpF", 4, [128, 512], F32, ph2)
                    splr = Rot(T, "spl", 4, [128, 512], F32, ph2)
                    tmpB = Rot(T, "tmpB", 12, [128, 512], BF16, ph2)
                    dec, b_dec = T.sb("dec", [128, 4, 4], F32, ph2)
                    osbr = Rot(T, "osb", 2, [128, D], BF16, ph2)
                    ggl, b_ggl = T.sb("ggl", [128, D], F32, ph2)
                    T.dma(sp, ggl[:], gla_gain[l].partition_broadcast(128), b_ggl, writes=[b_ggl])

                    def ev_a(mi, t0, tn, ps_, pb):
                        T.op(act, lambda e: e.copy(a_aug[0:16, t0:t0 + tn], ps_), reads=[pb], writes=[b_aaug])
                    bmode(wa[:, l], b_wa, [(0, 0, 16)], subs, hT, hb, ev_a)
                    wt, wbuf = wget(("in", l, Q0))

                    def ev_q(mi, t0, tn, ps_, pb):
                        T.op(act, lambda e: e.activation(out=qT[:, mi, t0:t0 + tn], in_=ps_, func=AF.Copy, scale=128.0 ** -0.5),
                             reads=[pb], writes=[b_qT])
                        if t0 >= TB:
                            T.op(dve, lambda e: e.tensor_scalar(out=qTs_f[:, mi, :], in0=ps_, scalar1=128.0 ** -0.5, scalar2=None,
                                                                op0=ALU.mult), reads=[pb], writes=[b_qTs])
                    bmode(wt, wbuf, M4, subs, hT, hb, ev_q)
                    wt, wbuf = wget(("in", l, K0))

                    def ev_k(mi, t0, tn, ps_, pb):
                        T.op(dve, lambda e: e.tensor_copy(kT[:, mi, t0:t0 + tn], ps_), reads=[pb], writes=[b_kT])
                    bmode(wt, wbuf, M4, psubs, hT, hb, ev_k)
                    for (i, p, c0) in tiles:
                        pst, pb = amode(hT, [hb[i]], c0, p, wt, wbuf)
                        if i < 4:
                            T.op(act, lambda e: e.copy(ktok[:, i, :], pst[:, 0:512]), reads=[pb], writes=[b_ktok])
                        else:
                            T.op(act, lambda e: e.copy(ks_f[:, :], pst[0:p, 0:512]), reads=[pb], writes=[b_ksf])
                    for j in range(2):
                        wt, wbuf = wget(("in", l, V0 + j * 512))
                        for (i, p, c0) in tiles:
                            pst, pb = amode(hT, [hb[i]], c0, p, wt, wbuf)
                            if i < 4:
                                T.op(act, lambda e: e.copy(vtok[:, i, j * 512:(j + 1) * 512], pst[:, 0:512]), reads=[pb], writes=[b_vtok])
                            else:
                                T.op(act, lambda e: e.copy(vs_f[:, j * 512:(j + 1) * 512], pst[0:p, 0:512]), reads=[pb], writes=[b_vsf])
                    for j in range(2):
                        wt, wbuf = wget(("in", l, GG0 + j * 512))
                        for (i, p, c0) in tiles:
                            pst, pb = amode(hT, [hb[i]], c0, p, wt, wbuf)
                            tf, b_tf = tmpF.next()
                            T.op(act, lambda e: e.activation(out=tf[0:p, :], in_=pst[0:p, 0:512], func=AF.Silu), reads=[pb], writes=[b_tf])
                            dst, b_dst = (Xtok[:, i, j * 512:(j + 1) * 512], b_Xtok) if i < 4 else (Xs[:, j * 512:(j + 1) * 512], b_Xs)
                            T.op(dve, lambda e: e.tensor_tensor(out=dst, in0=tf[0:p, :], in1=ggl[0:p, j * 512:(j + 1) * 512], op=ALU.mult),
                                 reads=[b_tf, b_ggl], writes=[b_dst])

                    T.op(act, lambda e: e.copy(Sbf[:], S[:, l, :]), reads=[b_S], writes=[b_Sbf])
                    chk("glaproj_%d_%d" % (blk, l))

                    def post(osrc, b_os, p, Xap, b_X, c0, i, split=False):
                        ss, b_ss = ssr.next()
                        for h in range(4):
                            T.op(act, lambda e, h=h: e.activation(out=junk[0:p, 0:256], in_=osrc(h), func=AF.Square,
                                                                  accum_out=ss[0:p, h:h + 1]), reads=b_os, writes=[b_junk, b_ss])
                        T.op(act, lambda e: e.activation(out=ss[0:p, 4:8], in_=ss[0:p, 0:4], func=AF.Sqrt, scale=1.0 / 256, bias=EPS),
                             reads=[b_ss], writes=[b_ss])
                        T.op(dve, lambda e: e.reciprocal(out=ss[0:p, 8:12], in_=ss[0:p, 4:8]), reads=[b_ss], writes=[b_ss])
                        br, b_br = brr.next()
                        for h in range(4):
                            T.op(dve, lambda e, h=h: e.scalar_tensor_tensor(out=br[0:p, h * 256:(h + 1) * 256], in0=osrc(h),
                                                                            scalar=ss[0:p, 8 + h:9 + h], in1=Xap[:, h * 256:(h + 1) * 256],
                                                                            op0=ALU.mult, op1=ALU.mult),
                                 reads=b_os + [b_ss, b_X], writes=[b_br])
                        if split:
                            return (lambda: transpose8(br, b_br, p, brT[:, :, c0:c0 + p], brb[i]))
                        transpose8(br, b_br, p, brT[:, :, c0:c0 + p], brb[i])

                    zps = []
                    for i in range(4):
                        tok = slice(i * 128, (i + 1) * 128)
                        zp, zb = T.ps()
                        T.mm([lambda e: e.matmul(zp[:, 0:512], a_aug[0:17, tok], w_a2aug[0:17, l, :], start=True, stop=True)],
                             reads=[b_aaug, b_wa2], writes=[zb])
                        zps.append((zp, zb))
                    spls = []
                    for i in range(4):
                        zp, zb = zps[i]
                        spl, b_spl = splr.next()
                        T.op(act, lambda e: e.activation(out=spl[:], in_=zp[:, 0:512], func=AF.Exp, scale=-1.0), reads=[zb], writes=[b_spl])
                        T.op(act, lambda e: e.activation(out=spl[:], in_=spl[:], func=AF.Ln, bias=1.0, scale=1.0), reads=[b_spl], writes=[b_spl])
                        spls.append((spl, b_spl))
                    cums = []
                    for i in range(4):
                        spl, b_spl = spls[i]
                        bp, bb = T.ps()
                        T.mm([lambda e, h=h: e.matmul(bp[:, h * 128:(h + 1) * 128], spl[:, h * 128:(h + 1) * 128], triU_f[:], start=True, stop=True)
                              for h in range(4)], reads=[b_spl, b_c], writes=[bb])
                        rp, rb = T.ps()
                        T.mm([lambda e: e.matmul(rp[:, 0:512], triLs_f[:], spl[:], start=True, stop=True)], reads=[b_spl, b_c], writes=[rb])
                        cums.append((bp, bb, rp, rb))
                    qk = []
                    for i in range(4):
                        tok = slice(i * 128, (i + 1) * 128)
                        bp, bb, rp, rb = cums[i]
                        expb, b_expb = tmpF.next()
                        T.op(act, lambda e: e.activation(out=expb[:], in_=bp[:, 0:512], func=AF.Exp, scale=-1.0 / 16), reads=[bb], writes=[b_expb])
                        expnb, b_expnb = tmpF.next()
                        T.op(act, lambda e: e.activation(out=expnb[:], in_=bp[:, 0:512], func=AF.Exp, scale=1.0 / 16), reads=[bb], writes=[b_expnb])
                        expr, b_expr = tmpF.next()
                        T.op(act, lambda e: e.activation(out=expr[:], in_=rp[:, 0:512], func=AF.Exp, scale=-1.0 / 16), reads=[rb], writes=[b_expr])
                        qdec, b_qdec = tmpB.next()
                        T.op(dve, lambda e: e.tensor_tensor(out=qdec[:].rearrange("q (h c) -> q h c", h=4), in0=qT[:, :, tok],
                                                            in1=expb[:].rearrange("q (h c) -> q h c", h=4), op=ALU.mult),
                             reads=[b_qT, b_expb], writes=[b_qdec])
                        T.op(dve, lambda e: e.tensor_copy(dec[:, i, :], expb[:].rearrange("q (h c) -> q h c", h=4)[:, :, 127]),
                             reads=[b_expb], writes=[b_dec])
                        kinv, b_kinv = tmpB.next()
                        T.op(dve, lambda e: e.tensor_tensor(out=kinv[:].rearrange("q (h c) -> q h c", h=4), in0=kT[:, :, tok],
                                                            in1=expnb[:].rearrange("q (h c) -> q h c", h=4), op=ALU.mult),
                             reads=[b_kT, b_expnb], writes=[b_kinv])
                        kend, b_kend = tmpB.next()
                        T.op(dve, lambda e: e.tensor_tensor(out=kend[:], in0=ktok[:, i, :], in1=expr[:], op=ALU.mult),
                             reads=[b_ktok, b_expr], writes=[b_kend])
                        qk.append((qdec, b_qdec, kinv, b_kinv, kend, b_kend))
                    atts = []
                    for i in range(4):
                        qdec, b_qdec, kinv, b_kinv, kend, b_kend = qk[i]
                        ap_, ab = T.ps()
                        T.mm([lambda e, h=h: e.matmul(ap_[:, h * 128:(h + 1) * 128], kinv[:, h * 128:(h + 1) * 128],
                                                      qdec[:, h * 128:(h + 1) * 128], start=True, stop=True) for h in range(4)],
                             reads=[b_kinv, b_qdec], writes=[ab])
                        atts.append((ap_, ab))
                    for i in range(4):
                        ap_, ab = atts[i]
                        attT, b_attT = qk[i][2], qk[i][3]
                        T.op(dve, lambda e: e.tensor_tensor(out=attT[:], in0=ap_[:, 0:512], in1=triU4_b[:], op=ALU.mult),
                             reads=[ab, b_c], writes=[b_attT])
                        atts[i] = (attT, b_attT)
                    pend = None
                    pend_b = None
                    T.ps_range = (4, 8)
                    for i in range(4):
                        qdec, b_qdec, kinv, b_kinv, kend, b_kend = qk[i]
                        attT, b_attT = atts[i]
                        ob = [T.psl[(i % 2) * 2], T.psl[(i % 2) * 2 + 1]]
                        for bk in range(2):
                            fns = []
                            for hh in range(2):
                                h = bk * 2 + hh
                                o_ap = ob[bk][0][:, hh * 256:(hh + 1) * 256]
                                fns.append(lambda e, h=h, o_ap=o_ap: e.matmul(o_ap, qdec[:, h * 128:(h + 1) * 128], Sbf[:, h * 256:(h + 1) * 256],
                                                                              start=True, stop=False))
                                fns.append(lambda e, h=h, o_ap=o_ap: e.matmul(o_ap, attT[:, h * 128:(h + 1) * 128], vtok[:, i, h * 256:(h + 1) * 256],
                                                                              start=False, stop=True))
                            T.mm(fns, reads=[b_qdec, b_Sbf, b_attT, b_vtok], writes=[ob[bk][1]])
                        db = [T.ps(), T.ps()]
                        for bk in range(2):
                            T.mm([lambda e, h=bk * 2 + hh, hh=hh: e.matmul(db[bk][0][:, hh * 256:(hh + 1) * 256], kend[:, h * 128:(h + 1) * 128],
                                                                           vtok[:, i, h * 256:(h + 1) * 256], start=True, stop=True)
                                  for hh in range(2)], reads=[b_kend, b_vtok], writes=[db[bk][1]])
                        osb_t, b_osb = osbr.next()
                        for bk in range(2):
                            T.op(act, lambda e, bk=bk: e.copy(osb_t[:, bk * 512:(bk + 1) * 512], ob[bk][0][:, 0:512]), reads=[ob[bk][1]], writes=[b_osb])
                        for h in range(4):
                            T.op(dve, lambda e, h=h: e.scalar_tensor_tensor(out=S[:, l, h * 256:(h + 1) * 256], in0=S[:, l, h * 256:(h + 1) * 256],
                                                                            scalar=dec[:, i, h:h + 1],
                                                                            in1=db[h // 2][0][:, (h % 2) * 256:(h % 2 + 1) * 256],
                                                                            op0=ALU.mult, op1=ALU.add),
                                 reads=[b_S, b_dec, db[h // 2][1]], writes=[b_S])
                        T.op(dve, lambda e: e.tensor_copy(Sbf[:], S[:, l, :]), reads=[b_S], writes=[b_Sbf])
                        if pend_b is not None:
                            pend_b()
                            pend_b = None
                        if pend is not None:
                            pend_b = pend()
                        pend = (lambda osb_t=osb_t, b_osb=b_osb, i=i: post(lambda h: osb_t[:, h * 256:(h + 1) * 256], [b_osb], 128,
                                                                           Xtok[:, i, :], b_Xtok, i * 128, i, split=True))
                    if pend_b is not None:
                        pend_b()
                    pend()()
                    T.ps_range = (0, 8)
                    if blk == NBLK - 1:
                        T.dma(sp, oglap[l].rearrange("h k v -> k h v"), S[:, l, :].rearrange("q (h v) -> q h v", h=4), b_S, reads=[b_S])

                    T.mark_fence()
                    ph2.close()
                    chk("glachunk_%d_%d" % (blk, l))
                    if has_s:
                        aTs, b_aTs = T.sb("aTs", [128, 64], F32, ph)
                        qmask, b_qmask = T.sb("qmask", [128, 4, NSMP, NSMP], BF16, ph)
                        s0r = Rot(T, "s0", 6, [128, D], F32, ph)
                        snr = Rot(T, "sn", 6, [128, D], F32, ph)
                        kmr = Rot(T, "km", 2, [NSMP, 512], BF16, ph)
                        snbr = Rot(T, "snb", 2, [128, D], BF16, ph)
                        zs, zsb = T.ps()
                        T.mm([lambda e, h=h: e.matmul(zs[:, h * 16:(h + 1) * 16], w_a2aug[0:17, l, h * 128:(h + 1) * 128], a_aug[0:17, TB:NT],
                                                      start=True, stop=True) for h in range(4)], reads=[b_aaug, b_wa2], writes=[zsb])
                        T.op(act, lambda e: e.activation(out=aTs[:], in_=zs[:, 0:64], func=AF.Exp, scale=-1.0), reads=[zsb], writes=[b_aTs])
                        T.op(act, lambda e: e.activation(out=aTs[:], in_=aTs[:], func=AF.Ln, bias=1.0, scale=1.0), reads=[b_aTs], writes=[b_aTs])
                        T.op(act, lambda e: e.activation(out=aTs[:], in_=aTs[:], func=AF.Exp, scale=-1.0 / 16), reads=[b_aTs], writes=[b_aTs])
                        T.op(dve, lambda e: e.memset(qmask[:], 0.0), writes=[b_qmask])
                        for b in range(NSMP):
                            T.op(dve, lambda e, b=b: e.tensor_copy(qmask[:, :, b, b], qTs_f[:, :, b]), reads=[b_qTs], writes=[b_qmask])
                        sst = {}

                        sld = {}

                        def stL(b):
                            s0, b_s0 = s0r.next()
                            T.dma(sp, s0[:].rearrange("q (h v) -> q h v", h=4), sg[l, b].rearrange("h k v -> k h v"), b_s0, writes=[b_s0])
                            sld[b] = (s0, b_s0)

                        def stA(b):
                            s0, b_s0 = sld.pop(b)
                            km, b_km = kmr.next()
                            T.op(dve, lambda e: e.tensor_scalar(out=km[:], in0=ks_f[:], scalar1=ident_f[0:NSMP, b:b + 1], scalar2=None,
                                                                op0=ALU.mult), reads=[b_ksf, b_c], writes=[b_km])
                            kvb = [T.ps(), T.ps()]
                            for bk in range(2):
                                T.mm([lambda e, h=bk * 2 + hh, hh=hh: e.matmul(kvb[bk][0][:, hh * 256:(hh + 1) * 256], km[0:NSMP, h * 128:(h + 1) * 128],
                                                                               vs_f[0:NSMP, h * 256:(h + 1) * 256], start=True, stop=True)
                                      for hh in range(2)], reads=[b_km, b_vsf], writes=[kvb[bk][1]])
                            sst[b] = [s0, b_s0, kvb]

                        def stB(b):
                            s0, b_s0, kvb = sst[b]
                            sn, b_sn = snr.next()
                            for h in range(4):
                                T.op(dve, lambda e, h=h: e.scalar_tensor_tensor(out=sn[:, h * 256:(h + 1) * 256], in0=s0[:, h * 256:(h + 1) * 256],
                                                                                scalar=aTs[:, h * 16 + b:h * 16 + b + 1],
                                                                                in1=kvb[h // 2][0][:, (h % 2) * 256:(h % 2 + 1) * 256],
                                                                                op0=ALU.mult, op1=ALU.add),
                                     reads=[b_s0, b_aTs, kvb[h // 2][1]], writes=[b_sn])
                            T.dma(sp, oglas[l, b].rearrange("h k v -> k h v"), sn[:].rearrange("q (h v) -> q h v", h=4), b_sn, reads=[b_sn])
                            snb, b_snb = snbr.next()
                            T.op(act, lambda e: e.copy(snb[:], sn[:]), reads=[b_sn], writes=[b_snb])
                            sst[b] = [snb, b_snb]

                        def stC(b):
                            snb, b_snb = sst.pop(b)
                            T.mm([lambda e, h=h: e.matmul(T.psl[h][0][0:NSMP, 0:256], qmask[:, h, b, :], snb[:, h * 256:(h + 1) * 256],
                                                          start=(b == 0), stop=(b == NSMP - 1)) for h in range(4)],
                                 reads=[b_qmask, b_snb], writes=[T.psl[h][1] for h in range(4)])

                        T.ps_range = (4, 8)
                        for b0 in range(5):
                            stL(b0)
                        stA(0)
                        for b in range(NSMP):
                            if b + 5 < NSMP:
                                stL(b + 5)
                            if b + 1 < NSMP:
                                stA(b + 1)
                            stB(b)
                            if b >= 1:
                                stC(b - 1)
                        stC(NSMP - 1)
                        post(lambda h: T.psl[h][0][0:NSMP, 0:256], [T.psl[h][1] for h in range(4)], NSMP, Xs[:, :], b_Xs, TB, 4)
                        T.ps_range = (0, 8)
                    T.mark_fence()

                def merge(bi):
                    for j in range(2):
                        wt, wbuf = wget(("in", l, MG0 + bi * D + j * 512))

                        def ev_g(mi, t0, tn, ps_, pb):
                            T.op(act, lambda e: e.activation(out=gblk[:, mi, t0:t0 + tn], in_=ps_, func=AF.Sigmoid), reads=[pb], writes=[b_gblk])
                        bmode(wt, wbuf, M4, subs, hT, hb, ev_g)
                        wt, wbuf = wget(("br", l, bi, j))

                        def ev_m(mi, t0, tn, ps_, pb):
                            dst = mgT[:, 4 * j + mi, t0:t0 + tn]
                            mb = tb(mgb, t0, tn)
                            if bi == 0:
                                T.op(dve, lambda e: e.tensor_tensor(out=dst, in0=ps_, in1=gblk[:, mi, t0:t0 + tn], op=ALU.mult),
                                     reads=[pb, b_gblk], writes=mb)
                            else:
                                tm, b_tm = mtmp.next()
                                T.op(dve, lambda e: e.tensor_tensor(out=tm[:, 0:tn], in0=ps_, in1=gblk[:, mi, t0:t0 + tn], op=ALU.mult),
                                     reads=[pb, b_gblk], writes=[b_tm])
                                T.op(dve, lambda e: e.tensor_tensor(out=dst, in0=dst, in1=tm[:, 0:tn], op=ALU.add),
                                     reads=[b_tm] + mb, writes=mb)
                        bmode(wt, wbuf, M4, subs, brT, brb, ev_m)

                chk("glasample_%d_%d" % (blk, l))
                merge(0)
                chk("merge0_%d_%d" % (blk, l))

                with ExitStack() as ph:
                    L = 15 + TB
                    uT, b_uT = T.sb("uT", [128, 4, 15 + NT], F32, ph)
                    wa_, b_wa_ = T.sb("wa_", [128, L], F32, ph)
                    wb_, b_wb_ = T.sb("wb_", [128, L], F32, ph)
                    diffT, b_diffT = T.sb("diffT", [128, 8, NT], BF16, ph)
                    spg, b_spg = T.sb("spg", [128, 8, NT], BF16, ph)
                    us_f, b_usf = T.sb("us_f", [NSMP, D], F32, ph)
                    ulast, b_ulast = T.sb("ulast", [128, D], F32, ph)
                    t16, b_t16 = T.sb("t16", [128, 16], F32, ph)
                    mtmp = Rot(T, "mtmp", 2, [128, 512], BF16, ph)
                    for jb in range(2):
                        wt, wbuf = wget(("in", l, U0 + jb * 512))

                        def ev_u(mi, t0, tn, ps_, pb):
                            T.op(act, lambda e: e.copy(uT[:, mi, 15 + t0:15 + t0 + tn], ps_), reads=[pb], writes=[b_uT])
                        bmode(wt, wbuf, M4, psubs, hT, hb, ev_u)
                        if has_s:
                            pst, pb = amode(hT, [hb[4]], TB, NSMP, wt, wbuf)
                            T.op(act, lambda e: e.copy(us_f[:, jb * 512:(jb + 1) * 512], pst[0:NSMP, 0:512]), reads=[pb], writes=[b_usf])
                        if blk == NBLK - 1:
                            pst, pb = amode(hT, [hb[3]], 384, 128, wt, wbuf)
                            T.op(act, lambda e: e.copy(ulast[:, jb * 512:(jb + 1) * 512], pst[:, 0:512]), reads=[pb], writes=[b_ulast])
                        for mi in range(4):
                            ch = 4 * jb + mi
                            g = ch // 2
                            w = 2 << g
                            u = uT[:, mi, :]
                            T.op(dve, lambda e: e.tensor_copy(uT[:, mi, 0:15], uhist[:, l, ch, :]), reads=[b_uhist], writes=[b_uT])
                            T.op(dve, lambda e: e.tensor_tensor(out=wa_[:, 1:L], in0=u[:, 1:L], in1=u[:, 0:L - 1], op=ALU.add),
                                 reads=[b_uT], writes=[b_wa_])
                            s_ap, b_s = wa_, b_wa_
                            if g >= 1:
                                T.op(dve, lambda e: e.tensor_tensor(out=wb_[:, 3:L], in0=wa_[:, 3:L], in1=wa_[:, 1:L - 2], op=ALU.add),
                                     reads=[b_wa_], writes=[b_wb_])
                                s_ap, b_s = wb_, b_wb_
                            if g >= 2:
                                T.op(dve, lambda e: e.tensor_tensor(out=wa_[:, 7:L], in0=wb_[:, 7:L], in1=wb_[:, 3:L - 4], op=ALU.add),
                                     reads=[b_wb_], writes=[b_wa_])
                                s_ap, b_s = wa_, b_wa_
                            if g >= 3:
                                T.op(dve, lambda e: e.tensor_tensor(out=wb_[:, 15:L], in0=wa_[:, 15:L], in1=wa_[:, 7:L - 8], op=ALU.add),
                                     reads=[b_wa_], writes=[b_wb_])
                                s_ap, b_s = wb_, b_wb_
                            T.op(dve, lambda e: e.scalar_tensor_tensor(out=diffT[:, ch, 0:TB], in0=s_ap[:, 15:L], scalar=1.0 / w,
                                                                       in1=u[:, 15:L], op0=ALU.mult, op1=ALU.subtract),
                                 reads=[b_s, b_uT], writes=[b_diffT])
                            if blk == 0:
                                T.op(dve, lambda e: e.tensor_tensor(out=t16[:], in0=s_ap[:, 15:31], in1=rc[:, g, :], op=ALU.mult),
                                     reads=[b_s, b_c], writes=[b_t16])
                                T.op(dve, lambda e: e.tensor_tensor(out=diffT[:, ch, 0:16], in0=t16[:], in1=u[:, 15:31], op=ALU.subtract),
                                     reads=[b_t16, b_uT], writes=[b_diffT])
                            if blk < NBLK - 1:
                                T.op(dve, lambda e: e.tensor_copy(uhist[:, l, ch, :], uT[:, mi, TB:TB + 15]), reads=[b_uT], writes=[b_uhist])
                    if blk == NBLK - 1:
                        T.dma(sp, opoolp[l], ulast[113:128, :], b_ulast, reads=[b_ulast])
                    if has_s:
                        hbuf, b_hbuf = T.sb("hbuf", [NSMP, 15, 256], F32, ph)
                        hs, b_hs = T.sb("hs", [NSMP, 256], F32, ph)
                        dsb, b_dsb = T.sb("dsb", [NSMP, D], BF16, ph)
                        for g in range(4):
                            w = 2 << g
                            cs = slice(g * 256, (g + 1) * 256)
                            T.dma(sp, hbuf[:, 0:w - 1, :], spool[l, :, 15 - (w - 1):15, cs], b_hbuf, writes=[b_hbuf])
                            T.op(dve, lambda e: e.tensor_reduce(out=hs[:], in_=hbuf[:, 0:w - 1, :].rearrange("p r c -> p c r"), axis=AX.X, op=ALU.add),
                                 reads=[b_hbuf], writes=[b_hs])
                            T.op(dve, lambda e: e.tensor_tensor(out=hs[:], in0=hs[:], in1=us_f[:, cs], op=ALU.add), reads=[b_hs, b_usf], writes=[b_hs])
                            T.op(dve, lambda e: e.scalar_tensor_tensor(out=dsb[:, cs], in0=hs[:], scalar=1.0 / w, in1=us_f[:, cs],
                                                                       op0=ALU.mult, op1=ALU.subtract), reads=[b_hs, b_usf], writes=[b_dsb])
                        transpose8(dsb, b_dsb, NSMP, diffT[:, :, TB:NT], b_diffT)
                        T.dma(sp, opools[l, :, 14, :], us_f[:, :], b_usf, reads=[b_usf])
                        T.dma(sp, opools[l, :, 0:14, :], spool[l, :, 1:15, :], b_dd)
                    for jb in range(2):
                        wt, wbuf = wget(("in", l, PG0 + jb * 512))

                        def ev_pg(mi, t0, tn, ps_, pb):
                            T.op(act, lambda e: e.activation(out=spg[:, 4 * jb + mi, t0:t0 + tn], in_=ps_, func=AF.Silu), reads=[pb], writes=[b_spg])
                        bmode(wt, wbuf, M4, subs, hT, hb, ev_pg)
                    for jb in range(2):
                        for gg in range(2):
                            g = 2 * jb + gg
                            for ee in range(2):
                                ch = 2 * g + ee
                                for (t0, tn) in subs:
                                    pst, pb = T.ps()
                                    T.mm([lambda e, k2=k2: e.matmul(pst[:, 0:tn], pw[:, l, g, k2, ee * 128:(ee + 1) * 128],
                                                                    diffT[:, 2 * g + k2, t0:t0 + tn], start=(k2 == 0), stop=(k2 == 1))
                                          for k2 in range(2)], reads=[b_pw, b_diffT], writes=[pb])
                                    T.op(dve, lambda e: e.scalar_tensor_tensor(out=brT[:, ch, t0:t0 + tn], in0=pst[:, 0:tn],
                                                                               scalar=pscT[:, l, ch:ch + 1], in1=spg[:, ch, t0:t0 + tn],
                                                                               op0=ALU.mult, op1=ALU.mult),
                                         reads=[pb, b_pscT, b_spg], writes=tb(brb, t0, tn))
                    merge(1)
                    T.mark_fence()
                chk("pool_%d_%d" % (blk, l))

                with ExitStack() as ph:
                    xqb, b_xqb = T.sb("xqb", [128, 4, NT], BF16, ph)
                    sxg, b_sxg = T.sb("sxg", [128, 4, TB], BF16, ph)
                    sxgs, b_sxgs = T.sb("sxgs", [128, 8, NSMP], BF16, ph)
                    pTr = Rot(T, "pT", 2, [128, 2, TB], BF16, ph)
                    rinv, b_rinv = T.sb("rinv", [128, TB], F32, ph)
                    tmpx = Rot(T, "tmpx", 2, [128, TB], F32, ph)
                    stg = tmpx
                    xqs_f, b_xqs = T.sb("xqs_f", [NSMP, D], BF16, ph)
                    mtmp = Rot(T, "mtmp", 2, [128, 512], BF16, ph)
                    if blk == 0:
                        for j in range(2):
                            wt, wbuf = wget(("mk", l, j))
                            for mt in range(2):
                                pst, pb = amode(memT, [b_memT], mt * 128, 128, wt, wbuf)
                                st_, b_st = stg.next()
                                T.op(act, lambda e: e.copy(st_[:], pst[:, 0:512]), reads=[pb], writes=[b_st])
                                T.dma(sp, omk[l, mt * 128:(mt + 1) * 128, j * 512:(j + 1) * 512], st_[:], b_st, reads=[b_st])

                            def ev_mk(mi, t0, tn, ps_, pb):
                                T.op(dve, lambda e: e.tensor_copy(mkT[:, l, 4 * j + mi, :], ps_), reads=[pb], writes=[b_mkT])
                            bmode(wt, wbuf, M4, [(0, 256)], memT, [b_memT] * 5, ev_mk)
                        for j in range(2):
                            wt, wbuf = wget(("mv", l, j))
                            for mt in range(2):
                                pst, pb = amode(memT, [b_memT], mt * 128, 128, wt, wbuf)
                                st_, b_st = stg.next()
                                T.op(act, lambda e: e.copy(st_[:], pst[:, 0:512]), reads=[pb], writes=[b_st])
                                T.dma(sp, omv[l, mt * 128:(mt + 1) * 128, j * 512:(j + 1) * 512], st_[:], b_st, reads=[b_st])
                                T.op(dve, lambda e: e.tensor_copy(mvb[:, l, mt, j * 512:(j + 1) * 512], pst[:, 0:512]), reads=[pb], writes=[b_mvb])
                    chk("xa1_%d_%d" % (blk, l))
                    for jb in range(2):
                        wt, wbuf = wget(("in", l, XQ0 + jb * 512))

                        def ev_xq(mi, t0, tn, ps_, pb):
                            T.op(act, lambda e: e.activation(out=xqb[:, mi, t0:t0 + tn], in_=ps_, func=AF.Copy, scale=1.0 / 16),
                                 reads=[pb], writes=[b_xqb])
                        bmode(wt, wbuf, M4, psubs, hT, hb, ev_xq)
                        if has_s:
                            pst, pb = amode(hT, [hb[4]], TB, NSMP, wt, wbuf)
                            T.op(act, lambda e: e.activation(out=xqs_f[:, jb * 512:(jb + 1) * 512], in_=pst[0:NSMP, 0:512], func=AF.Copy, scale=1.0 / 16),
                                 reads=[pb], writes=[b_xqs])
                        wt, wbuf = wget(("in", l, XG0 + jb * 512))

                        def ev_xg(mi, t0, tn, ps_, pb):
                            if t0 >= TB:
                                T.op(act, lambda e: e.activation(out=sxgs[:, 4 * jb + mi, :], in_=ps_, func=AF.Silu), reads=[pb], writes=[b_sxgs])
                            else:
                                T.op(act, lambda e: e.activation(out=sxg[:, mi, t0:t0 + tn], in_=ps_, func=AF.Silu), reads=[pb], writes=[b_sxg])
                        bmode(wt, wbuf, M4, subs, hT, hb, ev_xg)
                        hst = {}

                        def head_a(hh):
                            h = 2 * jb + hh
                            pT, b_pT = pTr.next()
                            for mc in range(2):
                                sp_, sb_ = T.ps()
                                T.mm([lambda e, dc=dc: e.matmul(sp_[:, 0:TB], mkT[:, l, 2 * h + dc, mc * 128:(mc + 1) * 128], xqb[:, 2 * hh + dc, 0:TB],
                                                                start=(dc == 0), stop=(dc == 1)) for dc in range(2)],
                                     reads=[b_mkT, b_xqb], writes=[sb_])
                                T.op(act, lambda e: e.activation(out=pT[:, mc, :], in_=sp_[:, 0:TB], func=AF.Exp), reads=[sb_], writes=[b_pT])
                            hst[hh] = (pT, b_pT)

                        def head_b(hh):
                            h = 2 * jb + hh
                            pT, b_pT = hst[hh]
                            sm_, smb = T.ps()
                            T.mm([lambda e, mc=mc: e.matmul(sm_[:, 0:TB], ones_b[:], pT[:, mc, :], start=(mc == 0), stop=(mc == 1)) for mc in range(2)],
                                 reads=[b_pT, b_c], writes=[smb])
                            T.op(act, lambda e: e.activation(out=rinv[:], in_=sm_[:, 0:TB], func=AF.Ln), reads=[smb], writes=[b_rinv])
                            T.op(act, lambda e: e.activation(out=rinv[:], in_=rinv[:], func=AF.Exp, scale=-1.0), reads=[b_rinv], writes=[b_rinv])
                            for dc in range(2):
                                op_, opb = T.ps()
                                T.mm([lambda e, mc=mc: e.matmul(op_[:, 0:TB], mvb[:, l, mc, (2 * h + dc) * 128:(2 * h + dc + 1) * 128], pT[:, mc, :],
                                                                start=(mc == 0), stop=(mc == 1)) for mc in range(2)],
                                     reads=[b_mvb, b_pT], writes=[opb])
                                tx, b_tx = tmpx.next()
                                T.op(dve, lambda e: e.tensor_tensor(out=tx[:], in0=op_[:, 0:TB], in1=rinv[:], op=ALU.mult),
                                     reads=[opb, b_rinv], writes=[b_tx])
                                T.op(dve, lambda e: e.tensor_tensor(out=brT[:, 2 * h + dc, 0:TB], in0=tx[:], in1=sxg[:, 2 * hh + dc, :], op=ALU.mult),
                                     reads=[b_tx, b_sxg], writes=brb[0:4])

                        head_a(0)
                        head_a(1)
                        head_b(0)
                        head_b(1)
                    chk("xa2_%d_%d" % (blk, l))
                    if has_s:
                        kbr = Rot(T, "kbuf", 6, [128, D], F32, ph)
                        vbuf = Rot(T, "vbuf", 2, [128, 2, D], BF16, ph)
                        vfr = kbr
                        prodr = Rot(T, "prod", 3, [128, 512], F32, ph)
                        sTall, b_sTall = T.sb("sTall", [128, 2, NSMP, 4], F32, ph)
                        stok, b_stok = T.sb("stok", [64, 256], F32, ph)
                        pnf, b_pnf = T.sb("pnf", [64, 256], F32, ph)
                        pn, b_pn = T.sb("pn", [64, 256], BF16, ph)
                        smx, b_smx = T.sb("smx", [64, 8], F32, ph)
                        pTs, b_pTs = T.sb("pTs", [128, 2, NSMP, 4], BF16, ph)
                        pmask, b_pmask = T.sb("pmask", [128, 2, 4, NSMP, NSMP], BF16, ph)
                        cb, b_cb = T.sb("cb", [NSMP, D], BF16, ph)
                        sel, b_sel = T.sb("sel", [NSMP, NSMP, 128], BF16, ph)
                        for b in range(NSMP):
                            T.op(dve, lambda e, b=b: e.tensor_scalar(out=sel[:, b, :], in0=ones_f[0:NSMP, :], scalar1=ident_f[0:NSMP, b:b + 1],
                                                                     scalar2=None, op0=ALU.mult), reads=[b_c], writes=[b_sel])
                        for b in range(NSMP):
                            qb = [T.ps(), T.ps()]
                            for bk in range(2):
                                T.mm([lambda e, bk=bk, b=b: e.matmul(qb[bk][0][:, 0:512], sel[0:NSMP, b, :], xqs_f[0:NSMP, bk * 512:(bk + 1) * 512],
                                                                     start=True, stop=True)], reads=[b_sel, b_xqs], writes=[qb[bk][1]])
                            for mt in range(2):
                                kb, b_kb = kbr.next()
                                T.dma(sp, kb[:], ck[l, b, mt * 128:(mt + 1) * 128, :], b_kb, writes=[b_kb])
                                pa, b_pa = prodr.next()
                                pb_, b_pb = prodr.next()
                                T.op(dve, lambda e: e.tensor_tensor(out=pa[:], in0=qb[0][0][:, 0:512], in1=kb[:, 0:512], op=ALU.mult),
                                     reads=[qb[0][1], b_kb], writes=[b_pa])
                                T.op(dve, lambda e: e.tensor_tensor(out=pb_[:], in0=qb[1][0][:, 0:512], in1=kb[:, 512:1024], op=ALU.mult),
                                     reads=[qb[1][1], b_kb], writes=[b_pb])
                                T.op(dve, lambda e, mt=mt, b=b: e.tensor_reduce(out=sTall[:, mt, b, 0:2], in_=pa[:].rearrange("q (h d) -> q h d", h=2),
                                                                                axis=AX.X, op=ALU.add), reads=[b_pa], writes=[b_sTall])
                                for hh in range(2):
                                    T.op(act, lambda e, mt=mt, b=b, hh=hh: e.activation(out=junk[:, 0:256], in_=pb_[:, hh * 256:(hh + 1) * 256], func=AF.Copy,
                                                                                         accum_out=sTall[:, mt, b, 2 + hh:3 + hh]),
                                         reads=[b_pb], writes=[b_junk, b_sTall])
                        chk("xa3_%d_%d" % (blk, l))
                        for mt in range(2):
                            tp_, tpb = T.ps()
                            T.mm([lambda e, mt=mt: e.transpose(tp_[0:64, 0:128], sTall[:, mt].rearrange("q b h -> q (b h)"), ident_f[:])],
                                 reads=[b_sTall, b_c], writes=[tpb])
                            T.op(act, lambda e, mt=mt: e.copy(stok[:, mt * 128:(mt + 1) * 128], tp_[0:64, 0:128]), reads=[tpb], writes=[b_stok])
                        T.op(dve, lambda e: e.reduce_max(out=smx[:, 0:1], in_=stok[:], axis=AX.X), reads=[b_stok], writes=[b_smx])
                        T.op(dve, lambda e: e.tensor_scalar(out=smx[:, 1:2], in0=smx[:, 0:1], scalar1=-1.0, scalar2=None, op0=ALU.mult),
                             reads=[b_smx], writes=[b_smx])
                        T.op(act, lambda e: e.activation(out=pnf[:], in_=stok[:], func=AF.Exp, bias=smx[:, 1:2], scale=1.0, accum_out=smx[:, 2:3]),
                             reads=[b_stok, b_smx], writes=[b_pnf, b_smx])
                        T.op(dve, lambda e: e.reciprocal(out=smx[:, 3:4], in_=smx[:, 2:3]), reads=[b_smx], writes=[b_smx])
                        T.op(dve, lambda e: e.tensor_scalar(out=pn[:], in0=pnf[:], scalar1=smx[:, 3:4], scalar2=None, op0=ALU.mult),
                             reads=[b_pnf, b_smx], writes=[b_pn])
                        for mt in range(2):
                            tp_, tpb = T.ps()
                            tpb16 = tp_[:].bitcast(BF16)
                            T.mm([lambda e, mt=mt: e.transpose(tpb16[:, 0:64], pn[0:64, mt * 128:(mt + 1) * 128], ident_b[0:64, 0:64])],
                                 reads=[b_pn, b_c], writes=[tpb])
                            T.op(dve, lambda e, mt=mt: e.tensor_copy(pTs[:, mt].rearrange("q b h -> q (b h)"), tpb16[:, 0:64]),
                                 reads=[tpb], writes=[b_pTs])
                        T.op(dve, lambda e: e.memset(pmask[:], 0.0), writes=[b_pmask])
                        for b in range(NSMP):
                            T.op(dve, lambda e, b=b: e.tensor_copy(pmask[:, :, :, b, b], pTs[:, :, b, :]), reads=[b_pTs], writes=[b_pmask])
                        chk("xa4_%d_%d" % (blk, l))
                        T.ps_range = (4, 8)
                        for b in range(NSMP):
                            vb, b_vb = vbuf.next()
                            for mt in range(2):
                                vf, b_vf = vfr.next()
                                T.dma(sp, vf[:], cv[l, b, mt * 128:(mt + 1) * 128, :], b_vf, writes=[b_vf])
                                T.op(act, lambda e, mt=mt: e.copy(vb[:, mt, :], vf[:]), reads=[b_vf], writes=[b_vb])
                            for h in range(4):
                                T.mm([lambda e, h=h, mt=mt, b=b: e.matmul(T.psl[h][0][0:NSMP, 0:256], pmask[:, mt, h, b, :], vb[:, mt, h * 256:(h + 1) * 256],
                                                                          start=(b == 0 and mt == 0), stop=(b == NSMP - 1 and mt == 1))
                                      for mt in range(2)], reads=[b_pmask, b_vb], writes=[T.psl[h][1]])
                        for h in range(4):
                            T.op(dve, lambda e, h=h: e.tensor_copy(cb[:, h * 256:(h + 1) * 256], T.psl[h][0][0:NSMP, 0:256]),
                                 reads=[T.psl[h][1]], writes=[b_cb])
                        T.ps_range = (0, 8)

                        def ev_c(v, pb):
                            T.op(dve, lambda e: e.tensor_tensor(out=brT[:, :, TB:NT], in0=v, in1=sxgs[:], op=ALU.mult),
                                 reads=[pb, b_sxgs], writes=[brb[4]])
                        transpose8(cb, b_cb, NSMP, None, None, evac=ev_c)
                    chk("xattn_%d_%d" % (blk, l))
                    merge(2)

                    for j in range(2):
                        wt, wbuf = wget(("out", l, j))
                        for (i, p, c0) in tiles:
                            pst, pb = amode(mgT, [mgb[i]], c0, p, wt, wbuf)
                            xa = xtile(i)[:, j * 512:(j + 1) * 512] if i < 4 else xS[:, j * 512:(j + 1) * 512]
                            T.op(dve, lambda e: e.tensor_tensor(out=xa, in0=xa, in1=pst[0:p, 0:512], op=ALU.add), reads=[pb, xb[i]], writes=[xb[i]])
                    T.mark_fence()
                chk("layer_%d_%d" % (blk, l))

            phf = ExitStack()
            ytr = Rot(T, "yt", 2, [128, D], F32, phf)
            fgbc, b_fgbc = T.sb("fgbc", [128, D], F32, phf)
            T.dma(sp, fgbc[:], final_gain.partition_broadcast(128), b_fgbc, writes=[b_fgbc])
            sts = [rms_stats(xtile(i), xb[i], p, D, ssr.next()) for (i, p, c0) in tiles]
            for k_, (i, p, c0) in enumerate(tiles):
                rstd, b_ss = sts[k_]
                yt, b_yt = ytr.next()
                T.op(dve, lambda e: e.scalar_tensor_tensor(out=yt[0:p, :], in0=xtile(i), scalar=rstd, in1=fgbc[0:p, :], op0=ALU.mult, op1=ALU.mult),
                     reads=[xb[i], b_ss, b_fgbc], writes=[b_yt])
                if i < 4:
                    T.dma(sp, yp[blk * TB + i * 128: blk * TB + (i + 1) * 128, :], yt[:], b_yt, reads=[b_yt])
                    if blk + 1 < NBLK:
                        T.dma(sp, xP[:, i, :], xp[(blk + 1) * TB + i * 128: (blk + 1) * TB + (i + 1) * 128, :], xb[i], writes=[xb[i]])
                else:
                    T.dma(sp, ys[:, :], yt[0:NSMP, :], b_yt, reads=[b_yt])
            T.mark_fence()
            phf.close()
        T.finish()


_CACHE = {}


def _build():
    if "nc" not in _CACHE:
        nc0 = bass.Bass("TRN2", target_bir_lowering=False)
        seq = program(nc0, None)
        nc = bass.Bass("TRN2", target_bir_lowering=False)
        program(nc, seq)
        _CACHE["nc"] = nc
    return _CACHE["nc"]


def kernel(x_prompt, x_sample, mem_prompt, cache_mem_k, cache_mem_v, state_gla, state_pool,
           w_in, w_a2, b_a, gla_gain, pool_w, pool_scale, w_mk, w_mv, w_branch, w_out,
           norm_gain, final_gain):
    f = lambda a: np.ascontiguousarray(np.asarray(a, dtype=np.float32))
    x_prompt, x_sample, mem_prompt = f(x_prompt), f(x_sample), f(mem_prompt)
    cache_mem_k, cache_mem_v, state_gla, state_pool = f(cache_mem_k), f(cache_mem_v), f(state_gla), f(state_pool)
    shared = dict(w_in=f(w_in), w_a2=f(w_a2), b_a=f(b_a), gla_gain=f(gla_gain), pool_w=f(pool_w), pool_scale=f(pool_scale),
                  w_mk=f(w_mk), w_mv=f(w_mv), w_branch=f(w_branch), w_out=f(w_out), norm_gain=f(norm_gain), final_gain=f(final_gain))
    nc = _build()
    in_maps = []
    for c in range(8):
        s = slice(c * NSMP, (c + 1) * NSMP)
        m = dict(shared)
        m["xp"] = x_prompt[c]
        m["xs"] = np.ascontiguousarray(x_sample[s, 0, :])
        m["mem"] = mem_prompt[c]
        m["ck"] = np.ascontiguousarray(cache_mem_k[:, s].reshape(2, NSMP, 256, D))
        m["cv"] = np.ascontiguousarray(cache_mem_v[:, s].reshape(2, NSMP, 256, D))
        m["sg"] = np.ascontiguousarray(state_gla[:, s])
        m["spool"] = np.ascontiguousarray(state_pool[:, s])
        in_maps.append(m)
    res = run_bass_kernel_spmd(nc, in_maps, core_ids=list(range(8)))
    R = res.results
    y_prompt = np.stack([R[c]["yp"] for c in range(8)], axis=0)
    y_sample = np.concatenate([R[c]["ys"] for c in range(8)], axis=0).reshape(128, 1, D)
    new_mk = np.stack([R[c]["omk"] for c in range(8)], axis=1).reshape(2, 8, 256, 4, 256)
    new_mv = np.stack([R[c]["omv"] for c in range(8)], axis=1).reshape(2, 8, 256, 4, 256)
    new_glap = np.stack([R[c]["oglap"] for c in range(8)], axis=1)
    new_poolp = np.stack([R[c]["opoolp"] for c in range(8)], axis=1)
    new_glas = np.concatenate([R[c]["oglas"] for c in range(8)], axis=1)
    new_pools = np.concatenate([R[c]["opools"] for c in range(8)], axis=1)
    out = (y_prompt, y_sample, new_mk, new_mv, new_glap, new_poolp, new_glas, new_pools)
    return tuple(np.ascontiguousarray(o, dtype=np.float32) for o in out)
```

```python
import numpy as np
from contextlib import ExitStack
import concourse.bass as bass
import concourse.mybir as mybir
from concourse.bass_utils import run_bass_kernel_spmd

F32 = mybir.dt.float32
BF16 = mybir.dt.bfloat16
AF = mybir.ActivationFunctionType
ALU = mybir.AluOpType
AX = mybir.AxisListType

D = 1024
NIN = 10256
TB = 512
NBLK = 4
NSMP = 16
NT = TB + NSMP
Q0, K0, V0, GG0, A0, U0, PG0, XQ0, XG0, MG0 = 0, 512, 1024, 2048, 3072, 3088, 4112, 5136, 6160, 7184
EPS = 1e-6
NSLOT = 4
PREF = 3


class Buf:
    __slots__ = ("name", "w", "r", "excl")

    def __init__(self, name, excl=False):
        self.name = name
        self.w = None
        self.r = {}
        self.excl = excl


class Eng:
    def __init__(self, T, e, name):
        self.T, self.e, self.name = T, e, name
        self.sem = T.newsem("e_" + name)
        self.n = 0
        self.seen = {}

    def wait(self, ev):
        sem, val = ev
        k = id(sem)
        if self.seen.get(k, 0) >= val:
            return
        self.seen[k] = val
        if not self.T.dry:
            self.e.wait_ge(sem, val)


class Tracker:
    def __init__(self, nc, es, dry):
        self.nc, self.es, self.dry = nc, es, dry
        self.pe = Eng(self, nc.tensor, "pe")
        self.act = Eng(self, nc.scalar, "act")
        self.dve = Eng(self, nc.vector, "dve")
        self.pool = Eng(self, nc.gpsimd, "pool")
        self.sp = Eng(self, nc.sync, "sp")
        self.dsems = {}
        self.psl = []
        self.psi = 0
        self.ps_range = (0, 8)
        self.stopped = False
        self.fence = {}
        self.store_sems = {}

    def newsem(self, name):
        if self.dry:
            return object()
        return self.es.enter_context(self.nc.semaphore(name))

    def sb(self, name, shape, dt, stack=None):
        self.uid = getattr(self, "uid", 0) + 1
        nm = "%s_%d" % (name, self.uid)
        t = (stack or self.es).enter_context(self.nc.sbuf_tensor(nm, shape, dt))
        b = Buf(nm)
        if stack is not None:
            b.r = dict(self.fence)
        return t, b

    def init_psum(self):
        for i in range(8):
            t = self.es.enter_context(self.nc.psum_tensor("ps%d" % i, [128, 512], F32))
            self.psl.append((t, Buf("ps%d" % i, excl=True)))

    def ps(self):
        lo, hi = self.ps_range
        r = self.psl[lo + self.psi % (hi - lo)]
        self.psi += 1
        return r

    def _deps(self, eng, reads, writes):
        for b in reads:
            if b.w is not None:
                eng.wait(b.w)
            if b.excl:
                me = id(eng.sem)
                for k, ev in b.r.items():
                    if k != me:
                        eng.wait(ev)
        for b in writes:
            if b.w is not None:
                eng.wait(b.w)
            for ev in b.r.values():
                eng.wait(ev)

    def _mark(self, ev, reads, writes):
        k = id(ev[0])
        for b in reads:
            b.r[k] = ev
        for b in writes:
            b.w = ev
            b.r = {}

    def op(self, eng, fn, reads=(), writes=()):
        if self.stopped:
            return
        self._deps(eng, reads, writes)
        eng.n += 1
        if not self.dry:
            fn(eng.e).then_inc(eng.sem, 1)
        self._mark((eng.sem, eng.n), reads, writes)

    def mm(self, fns, reads=(), writes=()):
        if self.stopped:
            return
        eng = self.pe
        self._deps(eng, reads, writes)
        eng.n += 1
        if not self.dry:
            ins = None
            for fn in fns:
                ins = fn(eng.e)
            ins.then_inc(eng.sem, 1)
        self._mark((eng.sem, eng.n), reads, writes)

    def dma(self, eng, out, in_, key, reads=(), writes=()):
        if self.stopped:
            return
        self._deps(eng, reads, writes)
        if id(key) not in self.dsems:
            self.dsems[id(key)] = [self.newsem("d_" + key.name), 0]
        st = self.dsems[id(key)]
        st[1] += 16
        if len(reads) > 0:
            self.store_sems[id(st[0])] = st
        if not self.dry:
            eng.e.dma_start(out=out, in_=in_).then_inc(st[0], 16)
        self._mark((st[0], st[1]), reads, writes)

    def barrier(self, with_pool=False):
        if self.stopped:
            return
        engs = [self.pe, self.act, self.dve, self.sp]
        allv = engs + [self.pool]
        for e in (allv if with_pool else engs):
            for o in allv:
                if o.n > 0:
                    e.wait((o.sem, o.n))
            for st in self.dsems.values():
                e.wait((st[0], st[1]))

    def mark_fence(self):
        if self.stopped:
            return
        f = {}
        for o in [self.pe, self.act, self.dve, self.sp, self.pool]:
            if o.n > 0:
                f[id(o.sem)] = (o.sem, o.n)
        for st in self.store_sems.values():
            f[id(st[0])] = (st[0], st[1])
        self.fence = f

    def finish(self):
        self.barrier()


class Rot:
    def __init__(self, T, name, n, shape, dt, stack=None):
        self.items = [T.sb("%s%d" % (name, i), shape, dt, stack) for i in range(n)]
        self.i = 0

    def next(self):
        r = self.items[self.i % len(self.items)]
        self.i += 1
        return r


class _Stop(Exception):
    pass


STOP = None


def program(nc, wseq):
    dry = wseq is None
    rec = []
    es = ExitStack()
    with es:
      T = None
      try:
        _program(nc, wseq, dry, rec, es)
      except _Stop:
        pass
    return rec


def _program(nc, wseq, dry, rec, es):
    if True:
        es.enter_context(nc.allow_non_contiguous_dma(reason="small strided parameter loads"))
        es.enter_context(nc.allow_low_precision(reason="bf16 matmul operands, fp32 accumulate"))
        T = Tracker(nc, es, dry)
        pe, act, dve, pool, sp = T.pe, T.act, T.dve, T.pool, T.sp

        def chk(name):
            if STOP == name and not T.stopped:
                T.finish()
                T.stopped = True

        def din(name, shape):
            return nc.dram_tensor(name, shape, F32, kind="ExternalInput").ap()

        def dout(name, shape):
            return nc.dram_tensor(name, shape, F32, kind="ExternalOutput").ap()

        xp = din("xp", [2048, D]); xs = din("xs", [NSMP, D]); mem = din("mem", [256, D])
        ck = din("ck", [2, NSMP, 256, D]); cv = din("cv", [2, NSMP, 256, D])
        sg = din("sg", [2, NSMP, 4, 128, 256]); spool = din("spool", [2, NSMP, 15, D])
        w_in = din("w_in", [2, D, NIN]); w_a2 = din("w_a2", [2, 16, 512]); b_a = din("b_a", [2, 512])
        gla_gain = din("gla_gain", [2, D]); pool_w = din("pool_w", [2, 4, 256, 256])
        pool_scale = din("pool_scale", [2, D]); w_mk = din("w_mk", [2, D, D]); w_mv = din("w_mv", [2, D, D])
        w_branch = din("w_branch", [2, 3, D, D]); w_out = din("w_out", [2, D, D])
        norm_gain = din("norm_gain", [2, D]); final_gain = din("final_gain", [D])
        yp = dout("yp", [2048, D]); ys = dout("ys", [NSMP, D]); omk = dout("omk", [2, 256, D]); omv = dout("omv", [2, 256, D])
        oglap = dout("oglap", [2, 4, 128, 256]); opoolp = dout("opoolp", [2, 15, D])
        oglas = dout("oglas", [2, NSMP, 4, 128, 256]); opools = dout("opools", [2, NSMP, 15, D])

        T.init_psum()

        xP, _ = T.sb("xP", [128, 4, D], F32)
        xb = [Buf("xb%d" % i) for i in range(5)]
        xS, _ = T.sb("xS", [NSMP, D], F32)
        hT, _ = T.sb("hT", [128, 8, NT], BF16); hb = [Buf("hb%d" % i) for i in range(5)]
        brT, _ = T.sb("brT", [128, 8, NT], BF16); brb = [Buf("brb%d" % i) for i in range(5)]
        mgT, _ = T.sb("mgT", [128, 8, NT], BF16); mgb = [Buf("mgb%d" % i) for i in range(5)]
        gblk, b_gblk = T.sb("gblk", [128, 4, NT], BF16)
        slots = [T.sb("wslot%d" % i, [128, 8, 512], BF16) for i in range(NSLOT)]
        S, b_S = T.sb("S", [128, 2, D], F32)
        Sbf, b_Sbf = T.sb("Sbf", [128, D], BF16)
        mkT, b_mkT = T.sb("mkT", [128, 2, 8, 256], BF16)
        mvb, b_mvb = T.sb("mvb", [128, 2, 2, D], BF16)
        memT, b_memT = T.sb("memT", [128, 8, 256], BF16)
        ones_f, b_c = T.sb("ones_f", [128, 128], F32)
        triU_f, _ = T.sb("triU_f", [128, 128], F32)
        triLs_f, _ = T.sb("triLs_f", [128, 128], F32)
        ident_f, _ = T.sb("ident_f", [128, 128], F32)
        ident_b, _ = T.sb("ident_b", [128, 128], BF16)
        ones_b, _ = T.sb("ones_b", [128, 128], BF16)
        triU4_b, _ = T.sb("triU4_b", [128, 512], BF16)
        rc, _ = T.sb("rc", [128, 4, 16], F32)
        a_aug, b_aaug = T.sb("a_aug", [17, NT], F32)
        w_a2aug, b_wa2 = T.sb("w_a2aug", [17, 2, 512], F32)
        wa, b_wa = T.sb("wa", [128, 2, 8, 16], BF16)
        pw, b_pw = T.sb("pw", [128, 2, 4, 2, 256], BF16)
        gainT, b_gainT = T.sb("gainT", [128, 2, 8], F32)
        pscT, b_pscT = T.sb("pscT", [128, 2, 8], F32)
        uhist, b_uhist = T.sb("uhist", [128, 2, 8, 15], F32)
        junk, b_junk = T.sb("junk", [128, D], BF16)
        ssr = Rot(T, "ss", 6, [128, 16], F32)
        b_dd = Buf("dram2dram")

        class WS:
            pos = 0
            issued = 0

        def wsrc(key):
            kind = key[0]
            if kind == "in":
                src = w_in[key[1], :, key[2]:key[2] + 512]
            elif kind == "br":
                src = w_branch[key[1], key[2], :, key[3] * 512:(key[3] + 1) * 512]
            elif kind == "out":
                src = w_out[key[1], :, key[2] * 512:(key[2] + 1) * 512]
            elif kind == "mk":
                src = w_mk[key[1], :, key[2] * 512:(key[2] + 1) * 512]
            else:
                src = w_mv[key[1], :, key[2] * 512:(key[2] + 1) * 512]
            return src.rearrange("(k p) n -> p k n", p=128)

        def wget(key):
            if T.stopped:
                return slots[0]
            i = WS.pos
            WS.pos += 1
            if dry:
                rec.append(key)
                return slots[0]
            assert wseq[i] == key, (i, wseq[i], key)
            while WS.issued < len(wseq) and WS.issued <= i + PREF:
                j = WS.issued
                st, sbuf_ = slots[j % NSLOT]
                T.dma(pool, st[:], wsrc(wseq[j]), sbuf_, writes=[sbuf_])
                WS.issued += 1
            return slots[i % NSLOT]

        T.op(dve, lambda e: e.memset(ones_f[:], 1.0), writes=[b_c])
        T.op(dve, lambda e: e.memset(ones_b[:], 1.0), writes=[b_c])
        T.op(pool, lambda e: e.affine_select(out=triU_f[:], in_=ones_f[:], pattern=[[1, 128]], compare_op=ALU.is_ge,
                                             fill=0.0, base=0, channel_multiplier=-1), reads=[b_c], writes=[b_c])
        T.op(pool, lambda e: e.affine_select(out=triLs_f[:], in_=ones_f[:], pattern=[[-1, 128]], compare_op=ALU.is_gt,
                                             fill=0.0, base=0, channel_multiplier=1), reads=[b_c], writes=[b_c])
        T.op(pool, lambda e: e.affine_select(out=ident_f[:], in_=ones_f[:], pattern=[[1, 128]], compare_op=ALU.is_equal,
                                             fill=0.0, base=0, channel_multiplier=-1), reads=[b_c], writes=[b_c])
        T.op(pool, lambda e: e.affine_select(out=ident_b[:], in_=ones_f[:], pattern=[[1, 128]], compare_op=ALU.is_equal,
                                             fill=0.0, base=0, channel_multiplier=-1), reads=[b_c], writes=[b_c])
        for h in range(4):
            T.op(pool, lambda e, h=h: e.affine_select(out=triU4_b[:, h * 128:(h + 1) * 128], in_=ones_f[:], pattern=[[1, 128]],
                                                      compare_op=ALU.is_ge, fill=0.0, base=0, channel_multiplier=-1),
                 reads=[b_c], writes=[b_c])
        for g in range(4):
            w = 2 << g
            T.op(dve, lambda e, g=g, w=w: e.memset(rc[:, g, :], 1.0 / w), writes=[b_c])
            for t in range(w - 1):
                T.op(dve, lambda e, g=g, t=t: e.memset(rc[:, g, t:t + 1], 1.0 / (t + 1)), writes=[b_c])
        T.op(dve, lambda e: e.memset(a_aug[:], 1.0), writes=[b_aaug])
        T.op(dve, lambda e: e.memset(S[:], 0.0), writes=[b_S])
        T.op(dve, lambda e: e.memset(uhist[:], 0.0), writes=[b_uhist])
        for l in range(2):
            T.dma(sp, w_a2aug[0:16, l, :], w_a2[l], b_wa2, writes=[b_wa2])
            T.dma(sp, w_a2aug[16:17, l, :], b_a[l:l + 1, :], b_wa2, writes=[b_wa2])
            T.dma(pool, wa[:, l], w_in[l, :, A0:A0 + 16].rearrange("(k p) n -> p k n", p=128), b_wa, writes=[b_wa])
            for g in range(4):
                T.dma(pool, pw[:, l, g], pool_w[l, g].rearrange("(k p) e -> p k e", p=128), b_pw, writes=[b_pw])
            T.dma(sp, gainT[:, l, :], norm_gain[l].rearrange("(k p) -> p k", p=128), b_gainT, writes=[b_gainT])
            T.dma(sp, pscT[:, l, :], pool_scale[l].rearrange("(k p) -> p k", p=128), b_pscT, writes=[b_pscT])

        def tb(bufs, t0, tn):
            if t0 >= TB:
                return [bufs[4]]
            return bufs[t0 // 128:(t0 + tn + 127) // 128]

        def transpose8(src, b_src, p, dst3, b_dst, evac=None):
            pst, pb = T.ps()
            psb = pst[:].bitcast(BF16)
            T.mm([lambda e, j=j: e.transpose(psb[:, j * 128:j * 128 + p], src[0:p, j * 128:(j + 1) * 128], ident_b[0:p, 0:p])
                  for j in range(8)], reads=[b_src, b_c], writes=[pb])
            v = psb.rearrange("q (j t) -> q j t", t=128)[:, :, 0:p]
            if evac is None:
                T.op(act, lambda e: e.copy(dst3, v), reads=[pb], writes=[b_dst])
            else:
                evac(v, pb)

        def rms_stats(xap, b_x, p, n, ssl):
            ss, b_ss = ssl
            T.op(act, lambda e: e.activation(out=junk[0:p, 0:n], in_=xap, func=AF.Square, accum_out=ss[0:p, 0:1]),
                 reads=[b_x], writes=[b_junk, b_ss])
            T.op(act, lambda e: e.activation(out=ss[0:p, 1:2], in_=ss[0:p, 0:1], func=AF.Sqrt, scale=1.0 / n, bias=EPS),
                 reads=[b_ss], writes=[b_ss])
            T.op(dve, lambda e: e.reciprocal(out=ss[0:p, 2:3], in_=ss[0:p, 1:2]), reads=[b_ss], writes=[b_ss])
            return ss[0:p, 2:3], b_ss

        def bmode(wt, wbuf, ms, subs, rhs3, rbufs, evac, nk=8, lhs_fn=None):
            for (mi, c0, mc) in ms:
                for (t0, tn) in subs:
                    pst, pb = T.ps()
                    T.mm([lambda e, kc=kc: e.matmul(pst[0:mc, 0:tn],
                                                    (lhs_fn(kc, c0, mc) if lhs_fn else wt[:, kc, c0:c0 + mc]),
                                                    rhs3[:, kc, t0:t0 + tn], start=(kc == 0), stop=(kc == nk - 1))
                          for kc in range(nk)], reads=[wbuf] + tb(rbufs, t0, tn), writes=[pb])
                    evac(mi, t0, tn, pst[0:mc, 0:tn], pb)

        def amode(lhs3, lbufs, c0, p, wt, wbuf, ncols=512):
            pst, pb = T.ps()
            T.mm([lambda e, kc=kc: e.matmul(pst[0:p, 0:ncols], lhs3[:, kc, c0:c0 + p], wt[:, kc, 0:ncols],
                                            start=(kc == 0), stop=(kc == 7)) for kc in range(8)],
                 reads=[wbuf] + lbufs, writes=[pb])
            return pst, pb

        M4 = [(m, m * 128, 128) for m in range(4)]

        with ExitStack() as ph:
            memf, b_memf = T.sb("memf", [128, 2, D], F32, ph)
            memb, b_memb = T.sb("memb", [128, 2, D], BF16, ph)
            T.dma(sp, memf[:], mem.rearrange("(t p) d -> p t d", p=128), b_memf, writes=[b_memf])
            T.op(dve, lambda e: e.tensor_copy(memb[:], memf[:]), reads=[b_memf], writes=[b_memb])
            for mt in range(2):
                transpose8(memb[:, mt, :], b_memb, 128, memT[:, :, mt * 128:(mt + 1) * 128], b_memT)
            T.mark_fence()

        chk("setup")
        for blk in range(NBLK):
            has_s = blk == 0
            tiles = [(i, 128, i * 128) for i in range(4)] + ([(4, NSMP, TB)] if has_s else [])
            subs = [(0, TB)] + ([(TB, NSMP)] if has_s else [])
            psubs = [(0, TB)]

            def xtile(i):
                return (xP[:, i, :] if i < 4 else xS[:, :])

            if blk == 0:
                for i in range(4):
                    T.dma(sp, xP[:, i, :], xp[blk * TB + i * 128: blk * TB + (i + 1) * 128, :], xb[i], writes=[xb[i]])
            if has_s:
                T.dma(sp, xS[:, :], xs[:, :], xb[4], writes=[xb[4]])

            for l in range(2):
                ph0 = ExitStack()
                xnr = Rot(T, "xn", len(tiles), [128, D], BF16, ph0)
                gbc, b_gbc = T.sb("gbc", [128, D], F32, ph0)
                T.dma(sp, gbc[:], norm_gain[l].partition_broadcast(128), b_gbc, writes=[b_gbc])
                sts = [rms_stats(xtile(i), xb[i], p, D, ssr.next()) for (i, p, c0) in tiles]
                xns = []
                for k_, (i, p, c0) in enumerate(tiles):
                    rstd, b_ss = sts[k_]
                    xn, b_xn = xnr.next()
                    T.op(dve, lambda e: e.scalar_tensor_tensor(out=xn[0:p, :], in0=xtile(i), scalar=rstd, in1=gbc[0:p, :],
                                                               op0=ALU.mult, op1=ALU.mult),
                         reads=[xb[i], b_ss, b_gbc], writes=[b_xn])
                    xns.append((xn, b_xn))
                for k_, (i, p, c0) in enumerate(tiles):
                    xn, b_xn = xns[k_]
                    transpose8(xn, b_xn, p, hT[:, :, c0:c0 + p], hb[i])
                T.mark_fence()
                ph0.close()
                chk("prenorm_%d_%d" % (blk, l))

                with ExitStack() as ph:
                    ks_f, b_ksf = T.sb("ks_f", [NSMP, 512], BF16, ph)
                    vs_f, b_vsf = T.sb("vs_f", [NSMP, D], BF16, ph)
                    Xs, b_Xs = T.sb("Xs", [NSMP, D], BF16, ph)
                    qTs_f, b_qTs = T.sb("qTs_f", [128, 4, NSMP], F32, ph)
                    brr = Rot(T, "br", 2, [128, D], BF16, ph)
                    ph2 = ExitStack()
                    qT, b_qT = T.sb("qT", [128, 4, NT], BF16, ph2)
                    kT, b_kT = T.sb("kT", [128, 4, NT], BF16, ph2)
                    ktok, b_ktok = T.sb("ktok", [128, 4, 512], BF16, ph2)
                    vtok, b_vtok = T.sb("vtok", [128, 4, D], BF16, ph2)
                    Xtok, b_Xtok = T.sb("Xtok", [128, 4, D], BF16, ph2)
                    tmpF = Rot(T, "tmpF", 4, [128, 512], F32, ph2)
                    splr = Rot(T, "spl", 4, [128, 512], F32, ph2)
                    tmpB = Rot(T, "tmpB", 12, [128, 512], BF16, ph2)
                    dec, b_dec = T.sb("dec", [128, 4, 4], F32, ph2)
                    osbr = Rot(T, "osb", 2, [128, D], BF16, ph2)
                    ggl, b_ggl = T.sb("ggl", [128, D], F32, ph2)
                    T.dma(sp, ggl[:], gla_gain[l].partition_broadcast(128), b_ggl, writes=[b_ggl])

                    def ev_a(mi, t0, tn, ps_, pb):
                        T.op(act, lambda e: e.copy(a_aug[0:16, t0:t0 + tn], ps_), reads=[pb], writes=[b_aaug])
                    bmode(wa[:, l], b_wa, [(0, 0, 16)], subs, hT, hb, ev_a)
                    wt, wbuf = wget(("in", l, Q0))

                    def ev_q(mi, t0, tn, ps_, pb):
                        T.op(act, lambda e: e.activation(out=qT[:, mi, t0:t0 + tn], in_=ps_, func=AF.Copy, scale=128.0 ** -0.5),
                             reads=[pb], writes=[b_qT])
                        if t0 >= TB:
                            T.op(dve, lambda e: e.tensor_scalar(out=qTs_f[:, mi, :], in0=ps_, scalar1=128.0 ** -0.5, scalar2=None,
                                                                op0=ALU.mult), reads=[pb], writes=[b_qTs])
                    bmode(wt, wbuf, M4, subs, hT, hb, ev_q)
                    wt, wbuf = wget(("in", l, K0))

                    def ev_k(mi, t0, tn, ps_, pb):
                        T.op(dve, lambda e: e.tensor_copy(kT[:, mi, t0:t0 + tn], ps_), reads=[pb], writes=[b_kT])
                    bmode(wt, wbuf, M4, psubs, hT, hb, ev_k)
                    for (i, p, c0) in tiles:
                        if i < 4:
                            pst, pb = T.ps()
                            psb = pst[:].bitcast(BF16)
                            T.mm([lambda e, h=h: e.transpose(psb[:, h * 128:(h + 1) * 128], kT[:, h, c0:c0 + 128], ident_b[:])
                                  for h in range(4)], reads=[b_kT, b_c], writes=[pb])
                            T.op(act, lambda e: e.copy(ktok[:, i, :], psb[:, 0:512]), reads=[pb], writes=[b_ktok])
                        else:
                            pst, pb = amode(hT, [hb[i]], c0, p, wt, wbuf)
                            T.op(act, lambda e: e.copy(ks_f[:, :], pst[0:p, 0:512]), reads=[pb], writes=[b_ksf])
                    for j in range(2):
                        wt, wbuf = wget(("in", l, V0 + j * 512))
                        for (i, p, c0) in tiles:
                            pst, pb = amode(hT, [hb[i]], c0, p, wt, wbuf)
                            if i < 4:
                                T.op(act, lambda e: e.copy(vtok[:, i, j * 512:(j + 1) * 512], pst[:, 0:512]), reads=[pb], writes=[b_vtok])
                            else:
                                T.op(act, lambda e: e.copy(vs_f[:, j * 512:(j + 1) * 512], pst[0:p, 0:512]), reads=[pb], writes=[b_vsf])
                    for j in range(2):
                        wt, wbuf = wget(("in", l, GG0 + j * 512))
                        for (i, p, c0) in tiles:
                            pst, pb = amode(hT, [hb[i]], c0, p, wt, wbuf)
                            tf, b_tf = tmpF.next()
                            T.op(act, lambda e: e.activation(out=tf[0:p, :], in_=pst[0:p, 0:512], func=AF.Silu), reads=[pb], writes=[b_tf])
                            dst, b_dst = (Xtok[:, i, j * 512:(j + 1) * 512], b_Xtok) if i < 4 else (Xs[:, j * 512:(j + 1) * 512], b_Xs)
                            T.op(dve, lambda e: e.tensor_tensor(out=dst, in0=tf[0:p, :], in1=ggl[0:p, j * 512:(j + 1) * 512], op=ALU.mult),
                                 reads=[b_tf, b_ggl], writes=[b_dst])

                    T.op(act, lambda e: e.copy(Sbf[:], S[:, l, :]), reads=[b_S], writes=[b_Sbf])
                    chk("glaproj_%d_%d" % (blk, l))

                    def post(osrc, b_os, p, Xap, b_X, c0, i, split=False):
                        ss, b_ss = ssr.next()
                        for h in range(4):
                            T.op(act, lambda e, h=h: e.activation(out=junk[0:p, 0:256], in_=osrc(h), func=AF.Square,
                                                                  accum_out=ss[0:p, h:h + 1]), reads=b_os, writes=[b_junk, b_ss])
                        T.op(act, lambda e: e.activation(out=ss[0:p, 4:8], in_=ss[0:p, 0:4], func=AF.Sqrt, scale=1.0 / 256, bias=EPS),
                             reads=[b_ss], writes=[b_ss])
                        T.op(dve, lambda e: e.reciprocal(out=ss[0:p, 8:12], in_=ss[0:p, 4:8]), reads=[b_ss], writes=[b_ss])
                        br, b_br = brr.next()
                        for h in range(4):
                            T.op(dve, lambda e, h=h: e.scalar_tensor_tensor(out=br[0:p, h * 256:(h + 1) * 256], in0=osrc(h),
                                                                            scalar=ss[0:p, 8 + h:9 + h], in1=Xap[:, h * 256:(h + 1) * 256],
                                                                            op0=ALU.mult, op1=ALU.mult),
                                 reads=b_os + [b_ss, b_X], writes=[b_br])
                        if split:
                            return (lambda: transpose8(br, b_br, p, brT[:, :, c0:c0 + p], brb[i]))
                        transpose8(br, b_br, p, brT[:, :, c0:c0 + p], brb[i])

                    zps = []
                    for i in range(4):
                        tok = slice(i * 128, (i + 1) * 128)
                        zp, zb = T.ps()
                        T.mm([lambda e: e.matmul(zp[:, 0:512], a_aug[0:17, tok], w_a2aug[0:17, l, :], start=True, stop=True)],
                             reads=[b_aaug, b_wa2], writes=[zb])
                        zps.append((zp, zb))
                    spls = []
                    for i in range(4):
                        zp, zb = zps[i]
                        spl, b_spl = splr.next()
                        T.op(act, lambda e: e.activation(out=spl[:], in_=zp[:, 0:512], func=AF.Exp, scale=-1.0), reads=[zb], writes=[b_spl])
                        T.op(act, lambda e: e.activation(out=spl[:], in_=spl[:], func=AF.Ln, bias=1.0, scale=1.0), reads=[b_spl], writes=[b_spl])
                        spls.append((spl, b_spl))
                    cums = []
                    for i in range(4):
                        spl, b_spl = spls[i]
                        bp, bb = T.ps()
                        T.mm([lambda e, h=h: e.matmul(bp[:, h * 128:(h + 1) * 128], spl[:, h * 128:(h + 1) * 128], triU_f[:], start=True, stop=True)
                              for h in range(4)], reads=[b_spl, b_c], writes=[bb])
                        rp, rb = T.ps()
                        T.mm([lambda e: e.matmul(rp[:, 0:512], triLs_f[:], spl[:], start=True, stop=True)], reads=[b_spl, b_c], writes=[rb])
                        cums.append((bp, bb, rp, rb))
                    qk = []
                    for i in range(4):
                        tok = slice(i * 128, (i + 1) * 128)
                        bp, bb, rp, rb = cums[i]
                        expb, b_expb = tmpF.next()
                        T.op(act, lambda e: e.activation(out=expb[:], in_=bp[:, 0:512], func=AF.Exp, scale=-1.0 / 16), reads=[bb], writes=[b_expb])
                        expnb, b_expnb = tmpF.next()
                        T.op(act, lambda e: e.activation(out=expnb[:], in_=bp[:, 0:512], func=AF.Exp, scale=1.0 / 16), reads=[bb], writes=[b_expnb])
                        expr, b_expr = tmpF.next()
                        T.op(act, lambda e: e.activation(out=expr[:], in_=rp[:, 0:512], func=AF.Exp, scale=-1.0 / 16), reads=[rb], writes=[b_expr])
                        qdec, b_qdec = tmpB.next()
                        T.op(dve, lambda e: e.tensor_tensor(out=qdec[:].rearrange("q (h c) -> q h c", h=4), in0=qT[:, :, tok],
                                                            in1=expb[:].rearrange("q (h c) -> q h c", h=4), op=ALU.mult),
                             reads=[b_qT, b_expb], writes=[b_qdec])
                        T.op(dve, lambda e: e.tensor_copy(dec[:, i, :], expb[:].rearrange("q (h c) -> q h c", h=4)[:, :, 127]),
                             reads=[b_expb], writes=[b_dec])
                        kinv, b_kinv = tmpB.next()
                        T.op(dve, lambda e: e.tensor_tensor(out=kinv[:].rearrange("q (h c) -> q h c", h=4), in0=kT[:, :, tok],
                                                            in1=expnb[:].rearrange("q (h c) -> q h c", h=4), op=ALU.mult),
                             reads=[b_kT, b_expnb], writes=[b_kinv])
                        kend, b_kend = tmpB.next()
                        T.op(dve, lambda e: e.tensor_tensor(out=kend[:], in0=ktok[:, i, :], in1=expr[:], op=ALU.mult),
                             reads=[b_ktok, b_expr], writes=[b_kend])
                        qk.append((qdec, b_qdec, kinv, b_kinv, kend, b_kend))
                    atts = []
                    for i in range(4):
                        qdec, b_qdec, kinv, b_kinv, kend, b_kend = qk[i]
                        ap_, ab = T.ps()
                        T.mm([lambda e, h=h: e.matmul(ap_[:, h * 128:(h + 1) * 128], kinv[:, h * 128:(h + 1) * 128],
                                                      qdec[:, h * 128:(h + 1) * 128], start=True, stop=True) for h in range(4)],
                             reads=[b_kinv, b_qdec], writes=[ab])
                        atts.append((ap_, ab))
                    for i in range(4):
                        ap_, ab = atts[i]
                        attT, b_attT = qk[i][2], qk[i][3]
                        T.op(dve, lambda e: e.tensor_tensor(out=attT[:], in0=ap_[:, 0:512], in1=triU4_b[:], op=ALU.mult),
                             reads=[ab, b_c], writes=[b_attT])
                        atts[i] = (attT, b_attT)
                    pend = None
                    pend_b = None
                    T.ps_range = (4, 8)
                    for i in range(4):
                        qdec, b_qdec, kinv, b_kinv, kend, b_kend = qk[i]
                        attT, b_attT = atts[i]
                        ob = [T.psl[(i % 2) * 2], T.psl[(i % 2) * 2 + 1]]
                        for bk in range(2):
                            fns = []
                            for hh in range(2):
                                h = bk * 2 + hh
                                o_ap = ob[bk][0][:, hh * 256:(hh + 1) * 256]
                                fns.append(lambda e, h=h, o_ap=o_ap: e.matmul(o_ap, qdec[:, h * 128:(h + 1) * 128], Sbf[:, h * 256:(h + 1) * 256],
                                                                              start=True, stop=False))
                                fns.append(lambda e, h=h, o_ap=o_ap: e.matmul(o_ap, attT[:, h * 128:(h + 1) * 128], vtok[:, i, h * 256:(h + 1) * 256],
                                                                              start=False, stop=True))
                            T.mm(fns, reads=[b_qdec, b_Sbf, b_attT, b_vtok], writes=[ob[bk][1]])
                        db = [T.ps(), T.ps()]
                        for bk in range(2):
                            T.mm([lambda e, h=bk * 2 + hh, hh=hh: e.matmul(db[bk][0][:, hh * 256:(hh + 1) * 256], kend[:, h * 128:(h + 1) * 128],
                                                                           vtok[:, i, h * 256:(h + 1) * 256], start=True, stop=True)
                                  for hh in range(2)], reads=[b_kend, b_vtok], writes=[db[bk][1]])
                        osb_t, b_osb = osbr.next()
                        for bk in range(2):
                            T.op(act, lambda e, bk=bk: e.copy(osb_t[:, bk * 512:(bk + 1) * 512], ob[bk][0][:, 0:512]), reads=[ob[bk][1]], writes=[b_osb])
                        for h in range(4):
                            T.op(dve, lambda e, h=h: e.scalar_tensor_tensor(out=S[:, l, h * 256:(h + 1) * 256], in0=S[:, l, h * 256:(h + 1) * 256],
                                                                            scalar=dec[:, i, h:h + 1],
                                                                            in1=db[h // 2][0][:, (h % 2) * 256:(h % 2 + 1) * 256],
                                                                            op0=ALU.mult, op1=ALU.add),
                                 reads=[b_S, b_dec, db[h // 2][1]], writes=[b_S])
                        T.op(dve, lambda e: e.tensor_copy(Sbf[:], S[:, l, :]), reads=[b_S], writes=[b_Sbf])
                        if pend_b is not None:
                            pend_b()
                            pend_b = None
                        if pend is not None:
                            pend_b = pend()
                        pend = (lambda osb_t=osb_t, b_osb=b_osb, i=i: post(lambda h: osb_t[:, h * 256:(h + 1) * 256], [b_osb], 128,
                                                                           Xtok[:, i, :], b_Xtok, i * 128, i, split=True))
                    if pend_b is not None:
                        pend_b()
                    pend()()
                    T.ps_range = (0, 8)
                    if blk == NBLK - 1:
                        T.dma(sp, oglap[l].rearrange("h k v -> k h v"), S[:, l, :].rearrange("q (h v) -> q h v", h=4), b_S, reads=[b_S])

                    T.mark_fence()
                    ph2.close()
                    chk("glachunk_%d_%d" % (blk, l))
                    if has_s:
                        aTs, b_aTs = T.sb("aTs", [128, 64], F32, ph)
                        qmask, b_qmask = T.sb("qmask", [128, 4, NSMP, NSMP], BF16, ph)
                        s0r = Rot(T, "s0", 4, [128, D], F32, ph)
                        snr = Rot(T, "sn", 4, [128, D], F32, ph)
                        kmr = Rot(T, "km", 2, [NSMP, 512], BF16, ph)
                        snbr = Rot(T, "snb", 2, [128, D], BF16, ph)
                        zs, zsb = T.ps()
                        T.mm([lambda e, h=h: e.matmul(zs[:, h * 16:(h + 1) * 16], w_a2aug[0:17, l, h * 128:(h + 1) * 128], a_aug[0:17, TB:NT],
                                                      start=True, stop=True) for h in range(4)], reads=[b_aaug, b_wa2], writes=[zsb])
                        T.op(act, lambda e: e.activation(out=aTs[:], in_=zs[:, 0:64], func=AF.Exp, scale=-1.0), reads=[zsb], writes=[b_aTs])
                        T.op(act, lambda e: e.activation(out=aTs[:], in_=aTs[:], func=AF.Ln, bias=1.0, scale=1.0), reads=[b_aTs], writes=[b_aTs])
                        T.op(act, lambda e: e.activation(out=aTs[:], in_=aTs[:], func=AF.Exp, scale=-1.0 / 16), reads=[b_aTs], writes=[b_aTs])
                        T.op(dve, lambda e: e.memset(qmask[:], 0.0), writes=[b_qmask])
                        for b in range(NSMP):
                            T.op(dve, lambda e, b=b: e.tensor_copy(qmask[:, :, b, b], qTs_f[:, :, b]), reads=[b_qTs], writes=[b_qmask])
                        sst = {}

                        sld = {}

                        def stL(b):
                            s0, b_s0 = s0r.next()
                            T.dma(sp, s0[:].rearrange("q (h v) -> q h v", h=4), sg[l, b].rearrange("h k v -> k h v"), b_s0, writes=[b_s0])
                            sld[b] = (s0, b_s0)

                        def stA(b):
                            s0, b_s0 = sld.pop(b)
                            km, b_km = kmr.next()
                            T.op(dve, lambda e: e.tensor_scalar(out=km[:], in0=ks_f[:], scalar1=ident_f[0:NSMP, b:b + 1], scalar2=None,
                                                                op0=ALU.mult), reads=[b_ksf, b_c], writes=[b_km])
                            kvb = [T.ps(), T.ps()]
                            for bk in range(2):
                                T.mm([lambda e, h=bk * 2 + hh, hh=hh: e.matmul(kvb[bk][0][:, hh * 256:(hh + 1) * 256], km[0:NSMP, h * 128:(h + 1) * 128],
                                                                               vs_f[0:NSMP, h * 256:(h + 1) * 256], start=True, stop=True)
                                      for hh in range(2)], reads=[b_km, b_vsf], writes=[kvb[bk][1]])
                            sst[b] = [s0, b_s0, kvb]

                        def stB(b):
                            s0, b_s0, kvb = sst[b]
                            sn, b_sn = snr.next()
                            for h in range(4):
                                T.op(dve, lambda e, h=h: e.scalar_tensor_tensor(out=sn[:, h * 256:(h + 1) * 256], in0=s0[:, h * 256:(h + 1) * 256],
                                                                                scalar=aTs[:, h * 16 + b:h * 16 + b + 1],
                                                                                in1=kvb[h // 2][0][:, (h % 2) * 256:(h % 2 + 1) * 256],
                                                                                op0=ALU.mult, op1=ALU.add),
                                     reads=[b_s0, b_aTs, kvb[h // 2][1]], writes=[b_sn])
                            T.dma(sp, oglas[l, b].rearrange("h k v -> k h v"), sn[:].rearrange("q (h v) -> q h v", h=4), b_sn, reads=[b_sn])
                            snb, b_snb = snbr.next()
                            T.op(act, lambda e: e.copy(snb[:], sn[:]), reads=[b_sn], writes=[b_snb])
                            sst[b] = [snb, b_snb]

                        def stC(b):
                            snb, b_snb = sst.pop(b)
                            T.mm([lambda e, h=h: e.matmul(T.psl[h][0][0:NSMP, 0:256], qmask[:, h, b, :], snb[:, h * 256:(h + 1) * 256],
                                                          start=(b == 0), stop=(b == NSMP - 1)) for h in range(4)],
                                 reads=[b_qmask, b_snb], writes=[T.psl[h][1] for h in range(4)])

                        T.ps_range = (4, 8)
                        stL(0)
                        stL(1)
                        stL(2)
                        stA(0)
                        for b in range(NSMP):
                            if b + 3 < NSMP:
                                stL(b + 3)
                            if b + 1 < NSMP:
                                stA(b + 1)
                            stB(b)
                            if b >= 1:
                                stC(b - 1)
                        stC(NSMP - 1)
                        post(lambda h: T.psl[h][0][0:NSMP, 0:256], [T.psl[h][1] for h in range(4)], NSMP, Xs[:, :], b_Xs, TB, 4)
                        T.ps_range = (0, 8)
                    T.mark_fence()

                def merge(bi):
                    for j in range(2):
                        wt, wbuf = wget(("in", l, MG0 + bi * D + j * 512))

                        def ev_g(mi, t0, tn, ps_, pb):
                            T.op(act, lambda e: e.activation(out=gblk[:, mi, t0:t0 + tn], in_=ps_, func=AF.Sigmoid), reads=[pb], writes=[b_gblk])
                        bmode(wt, wbuf, M4, subs, hT, hb, ev_g)
                        wt, wbuf = wget(("br", l, bi, j))

                        def ev_m(mi, t0, tn, ps_, pb):
                            dst = mgT[:, 4 * j + mi, t0:t0 + tn]
                            mb = tb(mgb, t0, tn)
                            if bi == 0:
                                T.op(dve, lambda e: e.tensor_tensor(out=dst, in0=ps_, in1=gblk[:, mi, t0:t0 + tn], op=ALU.mult),
                                     reads=[pb, b_gblk], writes=mb)
                            else:
                                tm, b_tm = mtmp.next()
                                T.op(dve, lambda e: e.tensor_tensor(out=tm[:, 0:tn], in0=ps_, in1=gblk[:, mi, t0:t0 + tn], op=ALU.mult),
                                     reads=[pb, b_gblk], writes=[b_tm])
                                T.op(dve, lambda e: e.tensor_tensor(out=dst, in0=dst, in1=tm[:, 0:tn], op=ALU.add),
                                     reads=[b_tm] + mb, writes=mb)
                        bmode(wt, wbuf, M4, subs, brT, brb, ev_m)

                chk("glasample_%d_%d" % (blk, l))
                merge(0)
                chk("merge0_%d_%d" % (blk, l))

                with ExitStack() as ph:
                    L = 15 + TB
                    uT, b_uT = T.sb("uT", [128, 4, 15 + NT], F32, ph)
                    wa_, b_wa_ = T.sb("wa_", [128, L], F32, ph)
                    wb_, b_wb_ = T.sb("wb_", [128, L], F32, ph)
                    diffT, b_diffT = T.sb("diffT", [128, 8, NT], BF16, ph)
                    spg, b_spg = T.sb("spg", [128, 8, NT], BF16, ph)
                    us_f, b_usf = T.sb("us_f", [NSMP, D], F32, ph)
                    ulast, b_ulast = T.sb("ulast", [128, D], F32, ph)
                    t16, b_t16 = T.sb("t16", [128, 16], F32, ph)
                    mtmp = Rot(T, "mtmp", 2, [128, 512], BF16, ph)
                    for jb in range(2):
                        wt, wbuf = wget(("in", l, U0 + jb * 512))

                        def ev_u(mi, t0, tn, ps_, pb):
                            T.op(act, lambda e: e.copy(uT[:, mi, 15 + t0:15 + t0 + tn], ps_), reads=[pb], writes=[b_uT])
                        bmode(wt, wbuf, M4, psubs, hT, hb, ev_u)
                        if has_s:
                            pst, pb = amode(hT, [hb[4]], TB, NSMP, wt, wbuf)
                            T.op(act, lambda e: e.copy(us_f[:, jb * 512:(jb + 1) * 512], pst[0:NSMP, 0:512]), reads=[pb], writes=[b_usf])
                        if blk == NBLK - 1:
                            pst, pb = amode(hT, [hb[3]], 384, 128, wt, wbuf)
                            T.op(act, lambda e: e.copy(ulast[:, jb * 512:(jb + 1) * 512], pst[:, 0:512]), reads=[pb], writes=[b_ulast])
                        for mi in range(4):
                            ch = 4 * jb + mi
                            g = ch // 2
                            w = 2 << g
                            u = uT[:, mi, :]
                            T.op(dve, lambda e: e.tensor_copy(uT[:, mi, 0:15], uhist[:, l, ch, :]), reads=[b_uhist], writes=[b_uT])
                            T.op(dve, lambda e: e.tensor_tensor(out=wa_[:, 1:L], in0=u[:, 1:L], in1=u[:, 0:L - 1], op=ALU.add),
                                 reads=[b_uT], writes=[b_wa_])
                            s_ap, b_s = wa_, b_wa_
                            if g >= 1:
                                T.op(dve, lambda e: e.tensor_tensor(out=wb_[:, 3:L], in0=wa_[:, 3:L], in1=wa_[:, 1:L - 2], op=ALU.add),
                                     reads=[b_wa_], writes=[b_wb_])
                                s_ap, b_s = wb_, b_wb_
                            if g >= 2:
                                T.op(dve, lambda e: e.tensor_tensor(out=wa_[:, 7:L], in0=wb_[:, 7:L], in1=wb_[:, 3:L - 4], op=ALU.add),
                                     reads=[b_wb_], writes=[b_wa_])
                                s_ap, b_s = wa_, b_wa_
                            if g >= 3:
                                T.op(dve, lambda e: e.tensor_tensor(out=wb_[:, 15:L], in0=wa_[:, 15:L], in1=wa_[:, 7:L - 8], op=ALU.add),
                                     reads=[b_wa_], writes=[b_wb_])
                                s_ap, b_s = wb_, b_wb_
                            T.op(dve, lambda e: e.scalar_tensor_tensor(out=diffT[:, ch, 0:TB], in0=s_ap[:, 15:L], scalar=1.0 / w,
                                                                       in1=u[:, 15:L], op0=ALU.mult, op1=ALU.subtract),
                                 reads=[b_s, b_uT], writes=[b_diffT])
                            if blk == 0:
                                T.op(dve, lambda e: e.tensor_tensor(out=t16[:], in0=s_ap[:, 15:31], in1=rc[:, g, :], op=ALU.mult),
                                     reads=[b_s, b_c], writes=[b_t16])
                                T.op(dve, lambda e: e.tensor_tensor(out=diffT[:, ch, 0:16], in0=t16[:], in1=u[:, 15:31], op=ALU.subtract),
                                     reads=[b_t16, b_uT], writes=[b_diffT])
                            if blk < NBLK - 1:
                                T.op(dve, lambda e: e.tensor_copy(uhist[:, l, ch, :], uT[:, mi, TB:TB + 15]), reads=[b_uT], writes=[b_uhist])
                    if blk == NBLK - 1:
                        T.dma(sp, opoolp[l], ulast[113:128, :], b_ulast, reads=[b_ulast])
                    if has_s:
                        hbuf, b_hbuf = T.sb("hbuf", [NSMP, 15, 256], F32, ph)
                        hs, b_hs = T.sb("hs", [NSMP, 256], F32, ph)
                        dsb, b_dsb = T.sb("dsb", [NSMP, D], BF16, ph)
                        for g in range(4):
                            w = 2 << g
                            cs = slice(g * 256, (g + 1) * 256)
                            T.dma(sp, hbuf[:, 0:w - 1, :], spool[l, :, 15 - (w - 1):15, cs], b_hbuf, writes=[b_hbuf])
                            T.op(dve, lambda e: e.tensor_reduce(out=hs[:], in_=hbuf[:, 0:w - 1, :].rearrange("p r c -> p c r"), axis=AX.X, op=ALU.add),
                                 reads=[b_hbuf], writes=[b_hs])
                            T.op(dve, lambda e: e.tensor_tensor(out=hs[:], in0=hs[:], in1=us_f[:, cs], op=ALU.add), reads=[b_hs, b_usf], writes=[b_hs])
                            T.op(dve, lambda e: e.scalar_tensor_tensor(out=dsb[:, cs], in0=hs[:], scalar=1.0 / w, in1=us_f[:, cs],
                                                                       op0=ALU.mult, op1=ALU.subtract), reads=[b_hs, b_usf], writes=[b_dsb])
                        transpose8(dsb, b_dsb, NSMP, diffT[:, :, TB:NT], b_diffT)
                        T.dma(sp, opools[l, :, 14, :], us_f[:, :], b_usf, reads=[b_usf])
                        T.dma(sp, opools[l, :, 0:14, :], spool[l, :, 1:15, :], b_dd)
                    for jb in range(2):
                        wt, wbuf = wget(("in", l, PG0 + jb * 512))

                        def ev_pg(mi, t0, tn, ps_, pb):
                            T.op(act, lambda e: e.activation(out=spg[:, 4 * jb + mi, t0:t0 + tn], in_=ps_, func=AF.Silu), reads=[pb], writes=[b_spg])
                        bmode(wt, wbuf, M4, subs, hT, hb, ev_pg)
                    for jb in range(2):
                        for gg in range(2):
                            g = 2 * jb + gg
                            for ee in range(2):
                                ch = 2 * g + ee
                                for (t0, tn) in subs:
                                    pst, pb = T.ps()
                                    T.mm([lambda e, k2=k2: e.matmul(pst[:, 0:tn], pw[:, l, g, k2, ee * 128:(ee + 1) * 128],
                                                                    diffT[:, 2 * g + k2, t0:t0 + tn], start=(k2 == 0), stop=(k2 == 1))
                                          for k2 in range(2)], reads=[b_pw, b_diffT], writes=[pb])
                                    T.op(dve, lambda e: e.scalar_tensor_tensor(out=brT[:, ch, t0:t0 + tn], in0=pst[:, 0:tn],
                                                                               scalar=pscT[:, l, ch:ch + 1], in1=spg[:, ch, t0:t0 + tn],
                                                                               op0=ALU.mult, op1=ALU.mult),
                                         reads=[pb, b_pscT, b_spg], writes=tb(brb, t0, tn))
                    merge(1)
                    T.mark_fence()
                chk("pool_%d_%d" % (blk, l))

                with ExitStack() as ph:
                    xqb, b_xqb = T.sb("xqb", [128, 4, NT], BF16, ph)
                    sxg, b_sxg = T.sb("sxg", [128, 4, TB], BF16, ph)
                    sxgs, b_sxgs = T.sb("sxgs", [128, 8, NSMP], BF16, ph)
                    pTr = Rot(T, "pT", 2, [128, 2, TB], BF16, ph)
                    rinv, b_rinv = T.sb("rinv", [128, TB], F32, ph)
                    tmpx = Rot(T, "tmpx", 2, [128, TB], F32, ph)
                    stg = Rot(T, "stg", 2, [128, 512], F32, ph)
                    xqs_f, b_xqs = T.sb("xqs_f", [NSMP, D], BF16, ph)
                    mtmp = Rot(T, "mtmp", 2, [128, 512], BF16, ph)
                    if blk == 0:
                        for j in range(2):
                            wt, wbuf = wget(("mk", l, j))
                            for mt in range(2):
                                pst, pb = amode(memT, [b_memT], mt * 128, 128, wt, wbuf)
                                st_, b_st = stg.next()
                                T.op(act, lambda e: e.copy(st_[:], pst[:, 0:512]), reads=[pb], writes=[b_st])
                                T.dma(sp, omk[l, mt * 128:(mt + 1) * 128, j * 512:(j + 1) * 512], st_[:], b_st, reads=[b_st])

                            def ev_mk(mi, t0, tn, ps_, pb):
                                T.op(dve, lambda e: e.tensor_copy(mkT[:, l, 4 * j + mi, :], ps_), reads=[pb], writes=[b_mkT])
                            bmode(wt, wbuf, M4, [(0, 256)], memT, [b_memT] * 5, ev_mk)
                        for j in range(2):
                            wt, wbuf = wget(("mv", l, j))
                            for mt in range(2):
                                pst, pb = amode(memT, [b_memT], mt * 128, 128, wt, wbuf)
                                st_, b_st = stg.next()
                                T.op(act, lambda e: e.copy(st_[:], pst[:, 0:512]), reads=[pb], writes=[b_st])
                                T.dma(sp, omv[l, mt * 128:(mt + 1) * 128, j * 512:(j + 1) * 512], st_[:], b_st, reads=[b_st])
                                T.op(dve, lambda e: e.tensor_copy(mvb[:, l, mt, j * 512:(j + 1) * 512], pst[:, 0:512]), reads=[pb], writes=[b_mvb])
                    chk("xa1_%d_%d" % (blk, l))
                    for jb in range(2):
                        wt, wbuf = wget(("in", l, XQ0 + jb * 512))

                        def ev_xq(mi, t0, tn, ps_, pb):
                            T.op(act, lambda e: e.activation(out=xqb[:, mi, t0:t0 + tn], in_=ps_, func=AF.Copy, scale=1.0 / 16),
                                 reads=[pb], writes=[b_xqb])
                        bmode(wt, wbuf, M4, psubs, hT, hb, ev_xq)
                        if has_s:
                            pst, pb = amode(hT, [hb[4]], TB, NSMP, wt, wbuf)
                            T.op(act, lambda e: e.activation(out=xqs_f[:, jb * 512:(jb + 1) * 512], in_=pst[0:NSMP, 0:512], func=AF.Copy, scale=1.0 / 16),
                                 reads=[pb], writes=[b_xqs])
                        wt, wbuf = wget(("in", l, XG0 + jb * 512))

                        def ev_xg(mi, t0, tn, ps_, pb):
                            if t0 >= TB:
                                T.op(act, lambda e: e.activation(out=sxgs[:, 4 * jb + mi, :], in_=ps_, func=AF.Silu), reads=[pb], writes=[b_sxgs])
                            else:
                                T.op(act, lambda e: e.activation(out=sxg[:, mi, t0:t0 + tn], in_=ps_, func=AF.Silu), reads=[pb], writes=[b_sxg])
                        bmode(wt, wbuf, M4, subs, hT, hb, ev_xg)
                        hst = {}

                        def head_a(hh):
                            h = 2 * jb + hh
                            pT, b_pT = pTr.next()
                            for mc in range(2):
                                sp_, sb_ = T.ps()
                                T.mm([lambda e, dc=dc: e.matmul(sp_[:, 0:TB], mkT[:, l, 2 * h + dc, mc * 128:(mc + 1) * 128], xqb[:, 2 * hh + dc, 0:TB],
                                                                start=(dc == 0), stop=(dc == 1)) for dc in range(2)],
                                     reads=[b_mkT, b_xqb], writes=[sb_])
                                T.op(act, lambda e: e.activation(out=pT[:, mc, :], in_=sp_[:, 0:TB], func=AF.Exp), reads=[sb_], writes=[b_pT])
                            hst[hh] = (pT, b_pT)

                        def head_b(hh):
                            h = 2 * jb + hh
                            pT, b_pT = hst[hh]
                            sm_, smb = T.ps()
                            T.mm([lambda e, mc=mc: e.matmul(sm_[:, 0:TB], ones_b[:], pT[:, mc, :], start=(mc == 0), stop=(mc == 1)) for mc in range(2)],
                                 reads=[b_pT, b_c], writes=[smb])
                            T.op(act, lambda e: e.activation(out=rinv[:], in_=sm_[:, 0:TB], func=AF.Ln), reads=[smb], writes=[b_rinv])
                            T.op(act, lambda e: e.activation(out=rinv[:], in_=rinv[:], func=AF.Exp, scale=-1.0), reads=[b_rinv], writes=[b_rinv])
                            for dc in range(2):
                                op_, opb = T.ps()
                                T.mm([lambda e, mc=mc: e.matmul(op_[:, 0:TB], mvb[:, l, mc, (2 * h + dc) * 128:(2 * h + dc + 1) * 128], pT[:, mc, :],
                                                                start=(mc == 0), stop=(mc == 1)) for mc in range(2)],
                                     reads=[b_mvb, b_pT], writes=[opb])
                                tx, b_tx = tmpx.next()
                                T.op(dve, lambda e: e.tensor_tensor(out=tx[:], in0=op_[:, 0:TB], in1=rinv[:], op=ALU.mult),
                                     reads=[opb, b_rinv], writes=[b_tx])
                                T.op(dve, lambda e: e.tensor_tensor(out=brT[:, 2 * h + dc, 0:TB], in0=tx[:], in1=sxg[:, 2 * hh + dc, :], op=ALU.mult),
                                     reads=[b_tx, b_sxg], writes=brb[0:4])

                        head_a(0)
                        head_a(1)
                        head_b(0)
                        head_b(1)
                    chk("xa2_%d_%d" % (blk, l))
                    if has_s:
                        kbr = Rot(T, "kbuf", 5, [128, D], F32, ph)
                        vbuf = Rot(T, "vbuf", 2, [128, 2, D], BF16, ph)
                        vfr = kbr
                        prodr = Rot(T, "prod", 3, [128, 512], F32, ph)
                        sTall, b_sTall = T.sb("sTall", [128, 2, NSMP, 4], F32, ph)
                        stok, b_stok = T.sb("stok", [64, 256], F32, ph)
                        pnf, b_pnf = T.sb("pnf", [64, 256], F32, ph)
                        pn, b_pn = T.sb("pn", [64, 256], BF16, ph)
                        smx, b_smx = T.sb("smx", [64, 8], F32, ph)
                        pTs, b_pTs = T.sb("pTs", [128, 2, NSMP, 4], BF16, ph)
                        pmask, b_pmask = T.sb("pmask", [128, 2, 4, NSMP, NSMP], BF16, ph)
                        cb, b_cb = T.sb("cb", [NSMP, D], BF16, ph)
                        sel, b_sel = T.sb("sel", [NSMP, NSMP, 128], BF16, ph)
                        for b in range(NSMP):
                            T.op(dve, lambda e, b=b: e.tensor_scalar(out=sel[:, b, :], in0=ones_f[0:NSMP, :], scalar1=ident_f[0:NSMP, b:b + 1],
                                                                     scalar2=None, op0=ALU.mult), reads=[b_c], writes=[b_sel])
                        for b in range(NSMP):
                            qb = [T.ps(), T.ps()]
                            for bk in range(2):
                                T.mm([lambda e, bk=bk, b=b: e.matmul(qb[bk][0][:, 0:512], sel[0:NSMP, b, :], xqs_f[0:NSMP, bk * 512:(bk + 1) * 512],
                                                                     start=True, stop=True)], reads=[b_sel, b_xqs], writes=[qb[bk][1]])
                            for mt in range(2):
                                kb, b_kb = kbr.next()
                                T.dma(sp, kb[:], ck[l, b, mt * 128:(mt + 1) * 128, :], b_kb, writes=[b_kb])
                                pa, b_pa = prodr.next()
                                pb_, b_pb = prodr.next()
                                T.op(dve, lambda e: e.tensor_tensor(out=pa[:], in0=qb[0][0][:, 0:512], in1=kb[:, 0:512], op=ALU.mult),
                                     reads=[qb[0][1], b_kb], writes=[b_pa])
                                T.op(dve, lambda e: e.tensor_tensor(out=pb_[:], in0=qb[1][0][:, 0:512], in1=kb[:, 512:1024], op=ALU.mult),
                                     reads=[qb[1][1], b_kb], writes=[b_pb])
                                T.op(dve, lambda e, mt=mt, b=b: e.tensor_reduce(out=sTall[:, mt, b, 0:2], in_=pa[:].rearrange("q (h d) -> q h d", h=2),
                                                                                axis=AX.X, op=ALU.add), reads=[b_pa], writes=[b_sTall])
                                for hh in range(2):
                                    T.op(act, lambda e, mt=mt, b=b, hh=hh: e.activation(out=junk[:, 0:256], in_=pb_[:, hh * 256:(hh + 1) * 256], func=AF.Copy,
                                                                                         accum_out=sTall[:, mt, b, 2 + hh:3 + hh]),
                                         reads=[b_pb], writes=[b_junk, b_sTall])
                        chk("xa3_%d_%d" % (blk, l))
                        for mt in range(2):
                            tp_, tpb = T.ps()
                            T.mm([lambda e, mt=mt: e.transpose(tp_[0:64, 0:128], sTall[:, mt].rearrange("q b h -> q (b h)"), ident_f[:])],
                                 reads=[b_sTall, b_c], writes=[tpb])
                            T.op(act, lambda e, mt=mt: e.copy(stok[:, mt * 128:(mt + 1) * 128], tp_[0:64, 0:128]), reads=[tpb], writes=[b_stok])
                        T.op(dve, lambda e: e.reduce_max(out=smx[:, 0:1], in_=stok[:], axis=AX.X), reads=[b_stok], writes=[b_smx])
                        T.op(dve, lambda e: e.tensor_scalar(out=smx[:, 1:2], in0=smx[:, 0:1], scalar1=-1.0, scalar2=None, op0=ALU.mult),
                             reads=[b_smx], writes=[b_smx])
                        T.op(act, lambda e: e.activation(out=pnf[:], in_=stok[:], func=AF.Exp, bias=smx[:, 1:2], scale=1.0, accum_out=smx[:, 2:3]),
                             reads=[b_stok, b_smx], writes=[b_pnf, b_smx])
                        T.op(dve, lambda e: e.reciprocal(out=smx[:, 3:4], in_=smx[:, 2:3]), reads=[b_smx], writes=[b_smx])
                        T.op(dve, lambda e: e.tensor_scalar(out=pn[:], in0=pnf[:], scalar1=smx[:, 3:4], scalar2=None, op0=ALU.mult),
                             reads=[b_pnf, b_smx], writes=[b_pn])
                        for mt in range(2):
                            tp_, tpb = T.ps()
                            tpb16 = tp_[:].bitcast(BF16)
                            T.mm([lambda e, mt=mt: e.transpose(tpb16[:, 0:64], pn[0:64, mt * 128:(mt + 1) * 128], ident_b[0:64, 0:64])],
                                 reads=[b_pn, b_c], writes=[tpb])
                            T.op(dve, lambda e, mt=mt: e.tensor_copy(pTs[:, mt].rearrange("q b h -> q (b h)"), tpb16[:, 0:64]),
                                 reads=[tpb], writes=[b_pTs])
                        T.op(dve, lambda e: e.memset(pmask[:], 0.0), writes=[b_pmask])
                        for b in range(NSMP):
                            T.op(dve, lambda e, b=b: e.tensor_copy(pmask[:, :, :, b, b], pTs[:, :, b, :]), reads=[b_pTs], writes=[b_pmask])
                        chk("xa4_%d_%d" % (blk, l))
                        T.ps_range = (4, 8)
                        for b in range(NSMP):
                            vb, b_vb = vbuf.next()
                            for mt in range(2):
                                vf, b_vf = vfr.next()
                                T.dma(sp, vf[:], cv[l, b, mt * 128:(mt + 1) * 128, :], b_vf, writes=[b_vf])
                                T.op(act, lambda e, mt=mt: e.copy(vb[:, mt, :], vf[:]), reads=[b_vf], writes=[b_vb])
                            for h in range(4):
                                T.mm([lambda e, h=h, mt=mt, b=b: e.matmul(T.psl[h][0][0:NSMP, 0:256], pmask[:, mt, h, b, :], vb[:, mt, h * 256:(h + 1) * 256],
                                                                          start=(b == 0 and mt == 0), stop=(b == NSMP - 1 and mt == 1))
                                      for mt in range(2)], reads=[b_pmask, b_vb], writes=[T.psl[h][1]])
                        for h in range(4):
                            T.op(dve, lambda e, h=h: e.tensor_copy(cb[:, h * 256:(h + 1) * 256], T.psl[h][0][0:NSMP, 0:256]),
                                 reads=[T.psl[h][1]], writes=[b_cb])
                        T.ps_range = (0, 8)

                        def ev_c(v, pb):
                            T.op(dve, lambda e: e.tensor_tensor(out=brT[:, :, TB:NT], in0=v, in1=sxgs[:], op=ALU.mult),
                                 reads=[pb, b_sxgs], writes=[brb[4]])
                        transpose8(cb, b_cb, NSMP, None, None, evac=ev_c)
                    chk("xattn_%d_%d" % (blk, l))
                    merge(2)

                    for j in range(2):
                        wt, wbuf = wget(("out", l, j))
                        for (i, p, c0) in tiles:
                            pst, pb = amode(mgT, [mgb[i]], c0, p, wt, wbuf)
                            xa = xtile(i)[:, j * 512:(j + 1) * 512] if i < 4 else xS[:, j * 512:(j + 1) * 512]
                            T.op(dve, lambda e: e.tensor_tensor(out=xa, in0=xa, in1=pst[0:p, 0:512], op=ALU.add), reads=[pb, xb[i]], writes=[xb[i]])
                    T.mark_fence()
                chk("layer_%d_%d" % (blk, l))

            phf = ExitStack()
            ytr = Rot(T, "yt", 2, [128, D], F32, phf)
            fgbc, b_fgbc = T.sb("fgbc", [128, D], F32, phf)
            T.dma(sp, fgbc[:], final_gain.partition_broadcast(128), b_fgbc, writes=[b_fgbc])
            sts = [rms_stats(xtile(i), xb[i], p, D, ssr.next()) for (i, p, c0) in tiles]
            for k_, (i, p, c0) in enumerate(tiles):
                rstd, b_ss = sts[k_]
                yt, b_yt = ytr.next()
                T.op(dve, lambda e: e.scalar_tensor_tensor(out=yt[0:p, :], in0=xtile(i), scalar=rstd, in1=fgbc[0:p, :], op0=ALU.mult, op1=ALU.mult),
                     reads=[xb[i], b_ss, b_fgbc], writes=[b_yt])
                if i < 4:
                    T.dma(sp, yp[blk * TB + i * 128: blk * TB + (i + 1) * 128, :], yt[:], b_yt, reads=[b_yt])
                    if blk + 1 < NBLK:
                        T.dma(sp, xP[:, i, :], xp[(blk + 1) * TB + i * 128: (blk + 1) * TB + (i + 1) * 128, :], xb[i], writes=[xb[i]])
                else:
                    T.dma(sp, ys[:, :], yt[0:NSMP, :], b_yt, reads=[b_yt])
            T.mark_fence()
            phf.close()
        T.finish()


_CACHE = {}


def _build():
    if "nc" not in _CACHE:
        nc0 = bass.Bass("TRN2", target_bir_lowering=False)
        seq = program(nc0, None)
        nc = bass.Bass("TRN2", target_bir_lowering=False)
        program(nc, seq)
        _CACHE["nc"] = nc
    return _CACHE["nc"]


def kernel(x_prompt, x_sample, mem_prompt, cache_mem_k, cache_mem_v, state_gla, state_pool,
           w_in, w_a2, b_a, gla_gain, pool_w, pool_scale, w_mk, w_mv, w_branch, w_out,
           norm_gain, final_gain):
    f = lambda a: np.ascontiguousarray(np.asarray(a, dtype=np.float32))
    x_prompt, x_sample, mem_prompt = f(x_prompt), f(x_sample), f(mem_prompt)
    cache_mem_k, cache_mem_v, state_gla, state_pool = f(cache_mem_k), f(cache_mem_v), f(state_gla), f(state_pool)
    shared = dict(w_in=f(w_in), w_a2=f(w_a2), b_a=f(b_a), gla_gain=f(gla_gain), pool_w=f(pool_w), pool_scale=f(pool_scale),
                  w_mk=f(w_mk), w_mv=f(w_mv), w_branch=f(w_branch), w_out=f(w_out), norm_gain=f(norm_gain), final_gain=f(final_gain))
    nc = _build()
    in_maps = []
    for c in range(8):
        s = slice(c * NSMP, (c + 1) * NSMP)
        m = dict(shared)
        m["xp"] = x_prompt[c]
        m["xs"] = np.ascontiguousarray(x_sample[s, 0, :])
        m["mem"] = mem_prompt[c]
        m["ck"] = np.ascontiguousarray(cache_mem_k[:, s].reshape(2, NSMP, 256, D))
        m["cv"] = np.ascontiguousarray(cache_mem_v[:, s].reshape(2, NSMP, 256, D))
        m["sg"] = np.ascontiguousarray(state_gla[:, s])
        m["spool"] = np.ascontiguousarray(state_pool[:, s])
        in_maps.append(m)
    res = run_bass_kernel_spmd(nc, in_maps, core_ids=list(range(8)))
    R = res.results
    y_prompt = np.stack([R[c]["yp"] for c in range(8)], axis=0)
    y_sample = np.concatenate([R[c]["ys"] for c in range(8)], axis=0).reshape(128, 1, D)
    new_mk = np.stack([R[c]["omk"] for c in range(8)], axis=1).reshape(2, 8, 256, 4, 256)
    new_mv = np.stack([R[c]["omv"] for c in range(8)], axis=1).reshape(2, 8, 256, 4, 256)
    new_glap = np.stack([R[c]["oglap"] for c in range(8)], axis=1)
    new_poolp = np.stack([R[c]["opoolp"] for c in range(8)], axis=1)
    new_glas = np.concatenate([R[c]["oglas"] for c in range(8)], axis=1)
    new_pools = np.concatenate([R[c]["opools"] for c in range(8)], axis=1)
    out = (y_prompt, y_sample, new_mk, new_mv, new_glap, new_poolp, new_glas, new_pools)
    return tuple(np.ascontiguousarray(o, dtype=np.float32) for o in out)
```
